# Optimizing a Trainium2 kernel written in Bass

```python
import jax, jax.numpy as jnp
from jax import lax
import numpy as np

D_MODEL = 2048
BATCH = 2
SEQ = 8192
DEPTH = 2
DEC_BATCH = 8
DEC_SEQ = 4096
PAST_LEN = 128

RET_WIDTH = D_MODEL // 2
SG_WIDTH = D_MODEL - RET_WIDTH
RET_HEADS = 4
RET_HEAD_DIM = RET_WIDTH // RET_HEADS
RET_CHUNK = 128
SG_GROUPS = 8
SG_GROUP_DIM = SG_WIDTH // SG_GROUPS
SG_CHUNK = 128
IN_WIDTH = 4 * RET_WIDTH + 2 * SG_WIDTH
D_FF = -(-8 * D_MODEL // (3 * 256)) * 256
PLE_DIM = 256
ROPE_BASE = 10000.0
EPS = 1e-6

kernel_name = "hymba_retention_gmlp_encoder"


def rms_norm(x, gain=None):
    xf = x.astype(jnp.float32)
    y = xf * lax.rsqrt(jnp.mean(xf * xf, axis=-1, keepdims=True) + EPS)
    if gain is not None:
        y = y * gain.astype(jnp.float32)
    return y.astype(x.dtype)


def layer_norm(x, gain, bias):
    xf = x.astype(jnp.float32)
    mu = jnp.mean(xf, axis=-1, keepdims=True)
    xc = xf - mu
    y = xc * lax.rsqrt(jnp.mean(xc * xc, axis=-1, keepdims=True) + EPS)
    return (y * gain.astype(jnp.float32) + bias.astype(jnp.float32)).astype(x.dtype)


def rotary(x):
    s, d = x.shape[1], x.shape[-1]
    half = d // 2
    inv_freq = ROPE_BASE ** (-jnp.arange(half, dtype=jnp.float32) / half)
    ang = jnp.arange(s, dtype=jnp.float32)[:, None] * inv_freq[None, :]
    cos = jnp.cos(ang)[None, :, None, :].astype(x.dtype)
    sin = jnp.sin(ang)[None, :, None, :].astype(x.dtype)
    x1, x2 = x[..., :half], x[..., half:]
    return jnp.concatenate([x1 * cos - x2 * sin, x1 * sin + x2 * cos], axis=-1)


def retention_scan(q, k, v, log_gamma, include_diag):
    b, s, h, dk = q.shape
    dv = v.shape[-1]
    n = s // RET_CHUNK
    dt = q.dtype
    idx = jnp.arange(RET_CHUNK, dtype=jnp.float32)
    diff = idx[:, None] - idx[None, :]
    allowed = (diff >= 0) if include_diag else (diff > 0)
    intra = jnp.where(allowed[None], jnp.exp(log_gamma[:, None, None] * jnp.where(allowed, diff, 0.0)[None]), 0.0).astype(dt)
    q_dec = jnp.exp(log_gamma[:, None] * (idx + 1.0)[None, :]).astype(dt)
    k_dec = jnp.exp(log_gamma[:, None] * (RET_CHUNK - 1.0 - idx)[None, :]).astype(dt)
    c_dec = jnp.exp(log_gamma * RET_CHUNK).astype(dt)

    def to_chunks(t):
        return t.reshape(b, n, RET_CHUNK, h, t.shape[-1]).transpose(1, 0, 3, 2, 4)

    def step(state, inp):
        qc, kc, vc = inp
        scores = jnp.einsum('bhid,bhjd->bhij', qc, kc) * intra[None]
        y = (jnp.einsum('bhij,bhje->bhie', scores, vc)
             + jnp.einsum('bhid,bhde->bhie', qc * q_dec[None, :, :, None], state))
        state = (state * c_dec[None, :, None, None]
                 + jnp.einsum('bhjd,bhje->bhde', kc * k_dec[None, :, :, None], vc))
        return state, y

    state0 = jnp.zeros((b, h, dk, dv), dt)
    _, y = lax.scan(step, state0, (to_chunks(q), to_chunks(k), to_chunks(v)))
    return y.transpose(1, 0, 3, 2, 4).reshape(b, s, h, dv)


def retention_mixer(q, k, v, g, log_gamma):
    b, s, _ = q.shape
    heads = lambda t: t.reshape(b, s, RET_HEADS, RET_HEAD_DIM)
    q = rotary(heads(q))
    k = rotary(heads(k)) * (RET_HEAD_DIM ** -0.5)
    v = heads(v)
    y_f = retention_scan(q, k, v, log_gamma[0], True)
    y_b = retention_scan(q[:, ::-1], k[:, ::-1], v[:, ::-1], log_gamma[1], False)[:, ::-1]
    y = rms_norm(y_f + y_b)
    return (jax.nn.silu(heads(g)) * y).reshape(b, s, RET_WIDTH)


def spatial_gating_mixer(u, v, ln_g, ln_b, w_s, b_s):
    b, s, _ = u.shape
    n = s // SG_CHUNK
    u = jax.nn.gelu(u)
    v = jax.nn.gelu(v).reshape(b, s, SG_GROUPS, SG_GROUP_DIM)
    v = layer_norm(v, ln_g.reshape(SG_GROUPS, SG_GROUP_DIM), ln_b.reshape(SG_GROUPS, SG_GROUP_DIM))
    v = v.reshape(b, n, SG_CHUNK, SG_GROUPS, SG_GROUP_DIM)
    z = jnp.einsum('gij,bnjgc->bnigc', w_s, v) + b_s.T[:, :, None]
    return u * z.reshape(b, s, SG_WIDTH)


def trunk(x, p, norm_mix_g, w_in, ret_decay, sg_ln_g, sg_ln_b, sg_w, sg_b, w_out,
          norm_ffn_g, w_ffn_gate, w_ffn_up, w_ffn_down, norm_ple_g, w_ple_gate, w_ple_proj,
          norm_final_g):
    h = x
    for l in range(DEPTH):
        hn = rms_norm(h, norm_mix_g[l])
        proj = hn @ w_in[l]
        q, k, v, g, u, sv = jnp.split(
            proj, [RET_WIDTH, 2 * RET_WIDTH, 3 * RET_WIDTH, 4 * RET_WIDTH, 4 * RET_WIDTH + SG_WIDTH], axis=-1)
        log_gamma = jnp.log1p(-jnp.exp2(-5.0 - ret_decay[l].astype(jnp.float32)))
        y_ret = retention_mixer(q, k, v, g, log_gamma)
        y_sg = spatial_gating_mixer(u, sv, sg_ln_g[l], sg_ln_b[l], sg_w[l], sg_b[l])
        h = h + jnp.concatenate([y_ret, y_sg], axis=-1) @ w_out[l]
        hn = rms_norm(h, norm_ffn_g[l])
        h = h + (jax.nn.silu(hn @ w_ffn_gate[l]) * (hn @ w_ffn_up[l])) @ w_ffn_down[l]
        gate = jax.nn.sigmoid(rms_norm(h, norm_ple_g[l]) @ w_ple_gate[l])
        h = h + (p[l] @ w_ple_proj[l]) * gate
    return rms_norm(h, norm_final_g)


def setup_inputs(seed: int = 0) -> dict:
    key = jax.random.key(seed)
    ks = jax.random.split(key, 24)
    f32 = jnp.float32
    nrm = lambda k, shape, scale: jax.random.normal(k, shape, f32) * scale
    gain = lambda k, shape: 1.0 + 0.02 * jax.random.normal(k, shape, f32)
    base = jnp.linspace(0.0, 7.0, RET_HEADS, dtype=f32)
    return {
        "x_prompt": nrm(ks[0], (BATCH, SEQ, D_MODEL), 1.0),
        "x_sample": nrm(ks[1], (DEC_BATCH, DEC_SEQ, D_MODEL), 1.0),
        "p_prompt": nrm(ks[2], (DEPTH, BATCH, SEQ, PLE_DIM), 1.0),
        "p_sample": nrm(ks[3], (DEPTH, DEC_BATCH, DEC_SEQ, PLE_DIM), 1.0),
        "norm_mix_g": gain(ks[4], (DEPTH, D_MODEL)),
        "w_in": nrm(ks[5], (DEPTH, D_MODEL, IN_WIDTH), D_MODEL ** -0.5),
        "ret_decay": base[None, None, :] + 0.25 * jax.random.normal(ks[6], (DEPTH, 2, RET_HEADS), f32),
        "sg_ln_g": gain(ks[7], (DEPTH, SG_WIDTH)),
        "sg_ln_b": nrm(ks[8], (DEPTH, SG_WIDTH), 0.02),
        "sg_w": nrm(ks[9], (DEPTH, SG_GROUPS, SG_CHUNK, SG_CHUNK), SG_CHUNK ** -0.5),
        "sg_b": gain(ks[10], (DEPTH, SG_GROUPS, SG_CHUNK)),
        "w_out": nrm(ks[11], (DEPTH, D_MODEL, D_MODEL), D_MODEL ** -0.5),
        "norm_ffn_g": gain(ks[12], (DEPTH, D_MODEL)),
        "w_ffn_gate": nrm(ks[13], (DEPTH, D_MODEL, D_FF), D_MODEL ** -0.5),
        "w_ffn_up": nrm(ks[14], (DEPTH, D_MODEL, D_FF), D_MODEL ** -0.5),
        "w_ffn_down": nrm(ks[15], (DEPTH, D_FF, D_MODEL), D_FF ** -0.5),
        "norm_ple_g": gain(ks[16], (DEPTH, D_MODEL)),
        "w_ple_gate": nrm(ks[17], (DEPTH, D_MODEL, D_MODEL), D_MODEL ** -0.5),
        "w_ple_proj": nrm(ks[18], (DEPTH, PLE_DIM, D_MODEL), PLE_DIM ** -0.5),
        "norm_final_g": gain(ks[19], (D_MODEL,)),
    }


def reference(x_prompt, x_sample, p_prompt, p_sample, norm_mix_g, w_in, ret_decay, sg_ln_g, sg_ln_b,
              sg_w, sg_b, w_out, norm_ffn_g, w_ffn_gate, w_ffn_up, w_ffn_down, norm_ple_g, w_ple_gate,
              w_ple_proj, norm_final_g):
    y_prompt = trunk(x_prompt, p_prompt, norm_mix_g, w_in, ret_decay, sg_ln_g, sg_ln_b, sg_w, sg_b, w_out,
                     norm_ffn_g, w_ffn_gate, w_ffn_up, w_ffn_down, norm_ple_g, w_ple_gate, w_ple_proj,
                     norm_final_g)
    y_sample = trunk(x_sample, p_sample, norm_mix_g, w_in, ret_decay, sg_ln_g, sg_ln_b, sg_w, sg_b, w_out,
                     norm_ffn_g, w_ffn_gate, w_ffn_up, w_ffn_down, norm_ple_g, w_ple_gate, w_ple_proj,
                     norm_final_g)
    return (y_prompt, y_sample)
```

```python
import math
from contextlib import ExitStack

import numpy as np
import concourse.bass as bass
import concourse.mybir as mybir
from concourse.bass_utils import run_bass_kernel_spmd

F32 = mybir.dt.float32
BF16 = mybir.dt.bfloat16
U8 = mybir.dt.uint8
I32 = mybir.dt.int32
AF = mybir.ActivationFunctionType
ALU = mybir.AluOpType
DTSIZE = {F32: 4, BF16: 2, U8: 1, I32: 4}

D = 2048
KC = D // 128
RW = 1024
H = 4
DH = 256
SGW = 1024
G = 8
DFF = 5632
FC = DFF // 128
PLE = 256
INW = 6144
EPS = 1e-6
T = 512
NCH = T // 128
ARENA = 210000

WSPEC = [("w_in", D, INW, 512), ("w_out", D, D, 256), ("w_ffn_gate", D, DFF, 256), ("w_ffn_up", D, DFF, 256),
         ("w_ffn_down", DFF, D, 128), ("w_ple_gate", D, D, 256), ("w_ple_proj", PLE, D, 256)]
WOFF = {}
_o = 0
for _n, _k, _nn, _g in WSPEC:
    WOFF[_n] = (_o, _k // 128, _g, _nn // _g)
    _o += _k * _nn
WTOT = _o


class Buf:
    __slots__ = ("name", "w", "r")

    def __init__(self, name=""):
        self.name = name
        self.w = None
        self.r = {}


class Sem:
    def __init__(self, h, key):
        self.h = h
        self.key = key
        self.cnt = 0


class Eng:
    def __init__(self, name, sem):
        self.name = name
        self.sem = sem
        self.prog = []
        self.waited = {}
        self.pending = False


class Prog:
    def __init__(self, nc, stack, same_engine_sync=False):
        self.nc = nc
        self.stack = stack
        self.sems = {}
        self.nsem = 0
        self.same_engine_sync = same_engine_sync
        self.eng = {}
        for n in ("pe", "act", "dve", "pool", "sp"):
            self.eng[n] = Eng(n, self.new_sem("e_" + n))
        self.dsems = []
        self.free_dsems = []
        self.phase_dsems = []

    def new_sem(self, name):
        h = self.stack.enter_context(self.nc.semaphore(name))
        s = Sem(h, self.nsem)
        self.sems[s.key] = s
        self.nsem += 1
        return s

    def new_dsem(self, name, persistent=False):
        if not persistent and self.free_dsems:
            s = self.free_dsems.pop()
        else:
            s = self.new_sem(name)
            self.dsems.append(s)
        if not persistent:
            self.phase_dsems.append(s)
        return s

    def release_phase_dsems(self):
        self.free_dsems.extend(self.phase_dsems)
        self.phase_dsems = []

    def _wait(self, E, tok, skip_key=None):
        key, val = tok
        if key == skip_key:
            return
        if key == E.sem.key and (not self.same_engine_sync or val > E.sem.cnt or E.name in ("pe", "sp")):
            return
        if E.waited.get(key, 0) >= val:
            return
        E.waited[key] = val
        E.prog.append(("wait", self.sems[key].h, val))

    def _deps(self, E, reads, writes, skip_key=None):
        for b in reads:
            if b.w is not None:
                self._wait(E, b.w, skip_key)
        for b in writes:
            if b.w is not None:
                self._wait(E, b.w, skip_key)
            for k, v in b.r.items():
                self._wait(E, (k, v), skip_key)

    def _update(self, tok, reads, writes):
        for b in writes:
            b.w = tok
            b.r = {}
        k, v = tok
        for b in reads:
            if b.r.get(k, 0) < v:
                b.r[k] = v

    def op(self, en, emit, reads=(), writes=(), inc=True):
        E = self.eng[en]
        self._deps(E, reads, writes)
        if inc:
            E.sem.cnt += 1
            E.pending = False
            E.prog.append(("ins", emit, E.sem.h))
            tok = (E.sem.key, E.sem.cnt)
        else:
            E.pending = True
            E.prog.append(("ins", emit, None))
            tok = (E.sem.key, E.sem.cnt + 1)
        self._update(tok, reads, writes)

    def dma(self, qn, out, in_, ds, reads=(), writes=()):
        E = self.eng[qn]
        self._deps(E, reads, writes, skip_key=ds.key)
        ds.cnt += 16
        E.prog.append(("dma", out, in_, ds.h))
        self._update((ds.key, ds.cnt), reads, writes)

    def cc(self, kind, groups, in_ap, out_ap, ds, reads=(), writes=()):
        E = self.eng["pool"]
        self._deps(E, reads, writes, skip_key=ds.key)
        ds.cnt += 16
        E.prog.append(("cc", kind, groups, in_ap, out_ap, ds.h))
        self._update((ds.key, ds.cnt), reads, writes)

    def barrier(self):
        for E in self.eng.values():
            assert not E.pending, E.name
        for E in self.eng.values():
            for E2 in self.eng.values():
                if E2 is not E and E2.sem.cnt > 0:
                    self._wait(E, (E2.sem.key, E2.sem.cnt))
            for ds in self.dsems:
                if ds.cnt > 0:
                    self._wait(E, (ds.key, ds.cnt))

    def emit(self, block):
        decos = {"pe": block.tensor, "act": block.scalar, "dve": block.vector,
                 "pool": block.gpsimd, "sp": block.sync}
        for n, deco in decos.items():
            E = self.eng[n]

            def body(h, E=E):
                for it in E.prog:
                    if it[0] == "wait":
                        h.wait_ge(it[1], it[2])
                    elif it[0] == "ins":
                        ins = it[1](h)
                        if it[2] is not None:
                            ins.then_inc(it[2], 1)
                    elif it[0] == "cc":
                        h.collective_compute(it[1], ALU.bypass, replica_groups=it[2], ins=[it[3]], outs=[it[4]]).then_inc(it[5], 16)
                    else:
                        h.dma_start(out=it[1], in_=it[2]).then_inc(it[3], 16)
            deco(body)


class Arena:
    def __init__(self, ap_u8, size):
        self.ap = ap_u8
        self.size = size
        self.off = 0

    def alloc(self, free_shape, dtype, parts=128):
        n = int(np.prod(free_shape))
        nb = n * DTSIZE[dtype]
        self.off = (self.off + 63) // 64 * 64
        assert self.off + nb <= self.size, ("arena overflow", self.off, nb, self.size)
        v = self.ap[0:parts, self.off:self.off + nb]
        if dtype != U8:
            v = v.bitcast(dtype)
        self.off += nb
        if len(free_shape) == 2:
            v = v.rearrange("p (a b) -> p a b", a=free_shape[0], b=free_shape[1])
        elif len(free_shape) == 3:
            v = v.rearrange("p (a b c) -> p a b c", a=free_shape[0], b=free_shape[1], c=free_shape[2])
        return v


class Ring:
    def __init__(self, P, A, n, shape, dtype, name, dsem=True):
        self.slots = []
        for i in range(n):
            self.slots.append((A.alloc(shape, dtype), Buf(f"{name}{i}"), P.new_dsem(f"d_{name}{i}") if dsem else None))
        self.i = 0

    def next(self):
        s = self.slots[self.i % len(self.slots)]
        self.i += 1
        return s


def build(NT, DEPTH=2, stop_after=None, debug=False, same_engine_sync=True, p4_stage=None):
    SEG = NT // 2
    NTILE = NT // T
    TPS = SEG // T
    nc = bass.Bass("TRN2", target_bir_lowering=False)
    dkind = "ExternalOutput" if debug else "Internal"

    def din(name, shape, dt=F32):
        return nc.dram_tensor(name, shape, dt, kind="ExternalInput").ap()

    def dscr(name, shape, dt):
        return nc.dram_tensor(name, shape, dt, kind=dkind).ap()

    xT = din("xT", [D, NT])
    pT = din("pT", [DEPTH * PLE, NT])
    wcat = din("wcat", [DEPTH * WTOT])
    gains = din("gains", [128, (DEPTH * 3 + 1) * KC])
    dec = din("dec", [128, DEPTH * 8])
    sgln = din("sgln", [128, DEPTH * 2 * SGW])
    wsT = din("wsT", [128, DEPTH * G * 128])
    bsr = din("bsr", [1, DEPTH * SGW])
    posb = din("posb", [128, NT])
    invf = din("invf", [128, 1])
    carry = din("carry", [128, 1])
    yT = nc.dram_tensor("yT", [D, NT], F32, kind="ExternalOutput").ap()

    wbf = nc.dram_tensor("wbf", [DEPTH * WTOT], BF16, kind="Internal").ap()
    hT = dscr("hT", [D, NT], F32)
    qT_d = dscr("qT_d", [RW, NT], BF16)
    kT_d = dscr("kT_d", [RW, NT], BF16)
    ktm_d = dscr("ktm_d", [NT, RW], BF16)
    vtm_d = dscr("vtm_d", [NT, RW], BF16)
    gact_d = dscr("gact_d", [NT, RW], BF16)
    svln_d = dscr("svln_d", [NT, SGW], BF16)
    uT_d = dscr("uT_d", [SGW, NT], BF16)
    ypart_d = dscr("ypart_d", [NT, RW], F32)
    ymixT_d = dscr("ymixT_d", [D, NT], BF16)

    with ExitStack() as st:
        P = Prog(nc, st, same_engine_sync=same_engine_sync)
        arena_t = st.enter_context(nc.sbuf_tensor("arena", [128, ARENA], U8))
        A = Arena(arena_t[:], ARENA)
        banks = [st.enter_context(nc.psum_tensor(f"bank{i}", [128, 512], F32)) for i in range(8)]
        bankb = [Buf(f"bank{i}") for i in range(8)]

        def tilebufs(name):
            return [Buf(f"{name}{t}") for t in range(NTILE)]
        B_hT = tilebufs("hT")
        B_q = tilebufs("q"); B_k = tilebufs("k"); B_ktm = tilebufs("ktm"); B_v = tilebufs("v")
        B_g = tilebufs("g"); B_sv = tilebufs("sv"); B_u = tilebufs("u"); B_yp = tilebufs("yp")
        B_ymr = tilebufs("ymr"); B_yms = tilebufs("yms")
        B_w = {(l, n): Buf(f"w{l}{n}") for l in range(DEPTH) for n, *_ in WSPEC}

        CH_EL = 128 * 8192
        cast_q = []
        for l in range(DEPTH):
            for n, k_, nn_, g_ in WSPEC:
                off = l * WTOT + WOFF[n][0]
                tot = k_ * nn_
                ds = P.new_dsem(f"wc{l}{n}", persistent=True)
                o = 0
                while o < tot:
                    sz = min(CH_EL, tot - o)
                    src = wcat[off + o: off + o + sz].rearrange("(p f) -> p f", p=128)
                    dst = wbf[off + o: off + o + sz].rearrange("(p f) -> p f", p=128)
                    cast_q.append((l, n, dst, src, ds))
                    o += sz

        def issue_casts(n=None, layer=None, name=None):
            while cast_q:
                l_, n_, dst, src, ds = cast_q[0]
                if layer is not None and (l_ > layer or (name is not None and (l_, n_) != (layer, name))):
                    break
                if layer is None and n is not None and n <= 0:
                    break
                cast_q.pop(0)
                P.dma("pool", dst, src, ds, writes=[B_w[(l_, n_)]])
                if n is not None:
                    n -= 1

        issue_casts(layer=0, name="w_in")

        def wview(l, name, g):
            off, kc, gw, ng = WOFF[name]
            base = l * WTOT + off + g * 128 * kc * gw
            return wbf[base: base + 128 * kc * gw].rearrange("(p f) -> p f", p=128)

        gains_sb = A.alloc([(DEPTH * 3 + 1) * KC], F32); Bc = Buf("const")
        ld0 = P.new_dsem("ld0", persistent=True)
        ld1 = P.new_dsem("ld1", persistent=True)
        ld2 = P.new_dsem("ld2", persistent=True)
        P.dma("sp", gains_sb, gains, ld0, writes=[Bc])
        dec_sb = A.alloc([DEPTH * 8], F32)
        P.dma("sp", dec_sb, dec, ld0, writes=[Bc])
        invf_sb = A.alloc([1], F32)
        P.dma("sp", invf_sb, invf, ld0, writes=[Bc])
        carry_sb = A.alloc([1], F32)
        P.dma("sp", carry_sb, carry, ld0, writes=[Bc])
        diff = A.alloc([128], F32)
        fidx1 = A.alloc([128], F32)
        pidx = A.alloc([1], F32)
        c128mp = A.alloc([1], F32)
        c127mp = A.alloc([1], F32)
        ident = A.alloc([128], BF16)
        ones = A.alloc([128], BF16)
        tmpc = A.alloc([128], F32)
        P.op("pool", lambda e: e.iota(diff, [[1, 128]], base=0, channel_multiplier=-1, allow_small_or_imprecise_dtypes=True), writes=[Bc])
        P.op("pool", lambda e: e.iota(fidx1, [[1, 128]], base=1, channel_multiplier=0, allow_small_or_imprecise_dtypes=True), writes=[Bc])
        P.op("pool", lambda e: e.iota(pidx, [[0, 1]], base=0, channel_multiplier=1, allow_small_or_imprecise_dtypes=True), writes=[Bc])
        P.op("pool", lambda e: e.memset(ones, 1.0), writes=[Bc])
        P.op("dve", lambda e: e.tensor_scalar(out=c128mp, in0=pidx, scalar1=-1.0, scalar2=128.0, op0=ALU.mult, op1=ALU.add), reads=[Bc], writes=[Bc])
        P.op("dve", lambda e: e.tensor_scalar(out=c127mp, in0=pidx, scalar1=-1.0, scalar2=127.0, op0=ALU.mult, op1=ALU.add), reads=[Bc], writes=[Bc])
        P.op("dve", lambda e: e.tensor_scalar(out=tmpc, in0=diff, scalar1=0.0, scalar2=None, op0=ALU.is_equal), reads=[Bc], writes=[Bc])
        P.op("dve", lambda e: e.tensor_copy(out=ident, in_=tmpc), reads=[Bc], writes=[Bc])
        lg = A.alloc([8], F32)
        e1 = A.alloc([8], F32)
        maskT = A.alloc([H, 128], F32)
        qdecf = A.alloc([8, 128], BF16)
        qdecb = A.alloc([H], F32)
        kdecf = A.alloc([H], F32)
        kdecb = A.alloc([H], F32)
        cdec = A.alloc([8], F32)
        lng = A.alloc([SGW], F32)
        lnb = A.alloc([SGW], F32)
        wsT_b = A.alloc([G, 128], BF16)
        bs_b = A.alloc([SGW], BF16, parts=1)
        Bl = Buf("layerconst")
        A_base = A.off

        def layer_setup(l):
            P.dma("sp", lng, sgln[:, (2 * l) * SGW:(2 * l + 1) * SGW], ld2, writes=[Bl])
            P.dma("sp", lnb, sgln[:, (2 * l + 1) * SGW:(2 * l + 2) * SGW], ld2, writes=[Bl])
            P.dma("pool", wsT_b, wsT[:, l * G * 128:(l + 1) * G * 128].rearrange("p (g i) -> p g i", g=G), ld1, writes=[Bl])
            P.dma("pool", bs_b, bsr[:, l * SGW:(l + 1) * SGW], ld1, writes=[Bl])
            P.op("act", lambda e: e.activation(out=e1, in_=dec_sb[:, l * 8:(l + 1) * 8], func=AF.Exp, scale=-math.log(2.0)), reads=[Bc, Bl], writes=[Bl])
            P.op("dve", lambda e: e.tensor_scalar(out=e1, in0=e1, scalar1=-(2.0 ** -5), scalar2=1.0, op0=ALU.mult, op1=ALU.add), reads=[Bl], writes=[Bl])
            P.op("act", lambda e: e.activation(out=lg, in_=e1, func=AF.Ln), reads=[Bl], writes=[Bl])
            P.op("act", lambda e: e.activation(out=cdec, in_=lg, func=AF.Exp, scale=128.0), reads=[Bl], writes=[Bl])
            for h in range(H):
                P.op("dve", lambda e, h=h: e.tensor_scalar(out=tmpc, in0=diff, scalar1=0.0, scalar2=lg[:, h:h + 1], op0=ALU.max, op1=ALU.mult), reads=[Bc, Bl], writes=[Bl])
                P.op("dve", lambda e, h=h: e.tensor_scalar(out=maskT[:, h, :], in0=diff, scalar1=-1.0, scalar2=0.0, op0=ALU.mult, op1=ALU.max), reads=[Bc, Bl], writes=[Bl])
                P.op("dve", lambda e, h=h: e.scalar_tensor_tensor(out=tmpc, in0=maskT[:, h, :], scalar=lg[:, 4 + h:5 + h], in1=tmpc, op0=ALU.mult, op1=ALU.add), reads=[Bl], writes=[Bl])
                P.op("act", lambda e, h=h: e.activation(out=maskT[:, h, :], in_=tmpc, func=AF.Exp), reads=[Bl], writes=[Bl])
                for dc in range(2):
                    P.op("act", lambda e, h=h, dc=dc: e.activation(out=qdecf[:, 2 * h + dc, :], in_=fidx1, func=AF.Exp, scale=lg[:, h:h + 1]), reads=[Bc, Bl], writes=[Bl])
                P.op("act", lambda e, h=h: e.activation(out=qdecb[:, h:h + 1], in_=c128mp, func=AF.Exp, scale=lg[:, 4 + h:5 + h]), reads=[Bc, Bl], writes=[Bl])
                P.op("act", lambda e, h=h: e.activation(out=kdecf[:, h:h + 1], in_=c127mp, func=AF.Exp, scale=lg[:, h:h + 1]), reads=[Bc, Bl], writes=[Bl])
                P.op("act", lambda e, h=h: e.activation(out=kdecb[:, h:h + 1], in_=pidx, func=AF.Exp, scale=lg[:, 4 + h:5 + h]), reads=[Bc, Bl], writes=[Bl])

        def gain_ap(idx, kc):
            return gains_sb[:, idx * KC + kc: idx * KC + kc + 1]

        def phase_end():
            P.barrier()
            P.release_phase_dsems()
            A.off = A_base
            for b in bankb:
                b.w = None; b.r = {}

        def rstd_from_psum(ps_ap, out_ap, n, reads, writes):
            P.op("dve", lambda e: e.tensor_scalar(out=out_ap, in0=ps_ap, scalar1=1.0 / n, scalar2=EPS, op0=ALU.mult, op1=ALU.add), reads=reads, writes=writes)
            P.op("act", lambda e: e.activation(out=out_ap, in_=out_ap, func=AF.Sqrt), reads=writes, writes=writes)
            P.op("dve", lambda e: e.reciprocal(out=out_ap, in_=out_ap), reads=writes, writes=writes)

        for l in range(DEPTH):
            src_h = xT if l == 0 else hT
            layer_setup(l)

            if True:
                hin = Ring(P, A, 2, [4, T], F32, "hin")
                sqr = Ring(P, A, 4, [T], BF16, "sq", dsem=False)
                hgr = Ring(P, A, 2, [KC, T], BF16, "hg", dsem=False)
                rsb = Ring(P, A, 2, [T], F32, "rsb", dsem=False)
                rst = Ring(P, A, 2, [NCH], F32, "rst", dsem=False)
                tab_ap = A.alloc([4, T], F32); tab_b = Buf("tabs")
                posr = Ring(P, A, 2, [T], F32, "pos")
                wr = Ring(P, A, 3, [KC * 512], BF16, "w")
                tmps = Ring(P, A, 2, [4, T], F32, "ropet", dsem=False)
                trig = A.alloc([2, T], F32); Btrig = Buf("trig")
                trigi = A.alloc([T], I32)
                outr = Ring(P, A, 4, [4 * T], BF16, "out")
                svg = Ring(P, A, 4, [T], F32, "svg", dsem=False)
                svx = Ring(P, A, 4, [T], F32, "svx", dsem=False)
                ut = Ring(P, A, 2, [T], F32, "ut", dsem=False)
                lnst = Ring(P, A, 2, [6, 16], F32, "lnst", dsem=False)
                pacc = [2, 3, 4, 5]
                pacc_i = [0]
                ptr = [6, 7]
                ptr_i = [0]

                def next_acc():
                    b = pacc[pacc_i[0] % 4]
                    pacc_i[0] += 1
                    return b

                norm_state = {}

                def p1_begin(t):
                    pap, pb, pds = posr.next()
                    P.dma("sp", pap, posb[:, t * T:(t + 1) * T], pds, writes=[pb])
                    norm_state[t] = dict(hin={}, pos=(pap, pb), hg=hgr.next(), rsb=rsb.next(), rst=rst.next(), sq={})

                def p1_load(t, part):
                    t0 = t * T
                    ap, b, ds = hin.next()
                    kc0 = part * 4
                    P.dma("sp", ap, src_h[kc0 * 128:(kc0 + 4) * 128, t0:t0 + T].rearrange("(k p) s -> p k s", p=128),
                          ds, reads=[B_hT[t]] if l > 0 else [], writes=[b])
                    norm_state[t]["hin"][part] = (ap, b)

                def p1_norm_elem(t, part):
                    s = norm_state[t]
                    hg_ap, hg_b, _ = s["hg"]
                    hap, hb = s["hin"][part]
                    for j in range(4):
                        kc = part * 4 + j
                        sq_ap, sq_b, _ = sqr.next()
                        s["sq"][kc] = (sq_ap, sq_b)
                        P.op("act", lambda e, hap=hap, j=j, sq_ap=sq_ap: e.activation(out=sq_ap, in_=hap[:, j, :], func=AF.Square), reads=[hb], writes=[sq_b])
                        ga = gain_ap(l * 3 + 0, kc)
                        P.op("act", lambda e, hap=hap, j=j, kc=kc, ga=ga: e.activation(out=hg_ap[:, kc, :], in_=hap[:, j, :], func=AF.Copy, scale=ga), reads=[hb, Bc], writes=[hg_b])

                def p1_norm_pe(t, part):
                    s = norm_state[t]
                    for j in range(4):
                        kc = part * 4 + j
                        sq_ap, sq_b = s["sq"][kc]
                        P.op("pe", lambda e, sq_ap=sq_ap, kc=kc: e.matmul(banks[0][:, :], lhsT=ones, rhs=sq_ap, start=(kc == 0), stop=(kc == KC - 1)), reads=[sq_b, Bc], writes=[bankb[0]], inc=False)
                        for c in range(NCH):
                            P.op("pe", lambda e, sq_ap=sq_ap, kc=kc, c=c: e.matmul(banks[1][:, c:c + 1], lhsT=sq_ap[:, c * 128:(c + 1) * 128], rhs=ones[:, 0:1], start=(kc == 0 and c == 0), stop=(kc == KC - 1 and c == NCH - 1), skip_group_check=True), reads=[sq_b, Bc], writes=[bankb[1]], inc=(c == NCH - 1))

                def p1_norm_fin(t):
                    s = norm_state[t]
                    rsb_ap, rsb_b, _ = s["rsb"]
                    rst_ap, rst_b, _ = s["rst"]
                    pap, pb = s["pos"]
                    rstd_from_psum(banks[0][:, :], rsb_ap, D, [bankb[0]], [rsb_b])
                    rstd_from_psum(banks[1][:, 0:NCH], rst_ap, D, [bankb[1]], [rst_b])
                    u = trig[:, 0, :]
                    f = trig[:, 1, :]
                    for j, sh in ((0, 0.25), (1, 0.0)):
                        P.op("dve", lambda e: e.tensor_scalar(out=u, in0=pap, scalar1=invf_sb[:, 0:1], scalar2=1.0 / (2 * math.pi), op0=ALU.mult, op1=ALU.mult), reads=[pb, Bc], writes=[Btrig])
                        if sh:
                            P.op("dve", lambda e, sh=sh: e.tensor_scalar(out=u, in0=u, scalar1=sh, scalar2=None, op0=ALU.add), reads=[Btrig], writes=[Btrig])
                        P.op("dve", lambda e: e.tensor_copy(out=trigi, in_=u), reads=[Btrig], writes=[Btrig])
                        P.op("dve", lambda e: e.tensor_copy(out=f, in_=trigi), reads=[Btrig], writes=[Btrig])
                        P.op("dve", lambda e: e.tensor_sub(out=u, in0=u, in1=f), reads=[Btrig], writes=[Btrig])
                        P.op("dve", lambda e: e.tensor_scalar(out=f, in0=u, scalar1=0.5, scalar2=None, op0=ALU.is_gt), reads=[Btrig], writes=[Btrig])
                        P.op("dve", lambda e: e.tensor_sub(out=u, in0=u, in1=f), reads=[Btrig], writes=[Btrig])
                        P.op("act", lambda e: e.activation(out=u, in_=u, func=AF.Sin, scale=2 * math.pi), reads=[Btrig], writes=[Btrig])
                        P.op("dve", lambda e, j=j: e.tensor_mul(out=tab_ap[:, j, :], in0=u, in1=rsb_ap), reads=[Btrig, rsb_b], writes=[tab_b])
                        P.op("pool", lambda e, j=j: e.tensor_scalar(out=tab_ap[:, 2 + j, :], in0=tab_ap[:, j, :], scalar1=DH ** -0.5, scalar2=None, op0=ALU.mult), reads=[tab_b], writes=[tab_b])

                wq = []
                GORDER = [0, 4, 1, 5, 2, 6, 3, 7, 10, 8, 11, 9]

                def p1_wload(g):
                    ap, b, ds = wr.next()
                    P.dma("sp", ap, wview(l, "w_in", g), ds, reads=[B_w[(l, "w_in")]], writes=[b])
                    wq.append((ap.rearrange("p (k g) -> p k g", k=KC), b))

                def fm_view(o_ap):
                    return o_ap.rearrange("p (b s) -> p b s", b=4)

                def tm_view(o_ap):
                    return o_ap.rearrange("p (c f) -> p c f", c=NCH)

                def p1_proj(t):
                    s = norm_state[t]
                    t0 = t * T
                    hg_ap, hg_b, _ = s["hg"]
                    rsb_ap, rsb_b, _ = s["rsb"]
                    rst_ap, rst_b, _ = s["rst"]
                    nx = t + 1 < NTILE
                    for gi in range(12):
                        g = GORDER[gi]
                        if nx and 4 <= gi < 8:
                            p1_norm_elem(t + 1, gi - 4)
                        if nx and gi == 1:
                            p1_begin(t + 1)
                        if gi == 0 and t == 0:
                            p1_wload(GORDER[0]); p1_wload(GORDER[1])
                        nxt = t * 12 + gi + 2
                        if nxt < NTILE * 12:
                            p1_wload(GORDER[nxt % 12])
                        if nx and 2 <= gi < 6:
                            p1_load(t + 1, gi - 2)
                        if nx and gi == 10:
                            p1_norm_fin(t + 1)
                        w_ap, w_b = wq.pop(0)
                        o_ap, o_b, o_ds = outr.next()
                        col0 = (g % 2) * 512
                        if g < 4 or 8 <= g < 10:
                            of = fm_view(o_ap)
                            accs = []
                            for blk in range(4):
                                bk = next_acc()
                                for kc in range(KC):
                                    P.op("pe", lambda e, bk=bk, blk=blk, kc=kc, w_ap=w_ap: e.matmul(banks[bk][:, :], lhsT=w_ap[:, kc, blk * 128:(blk + 1) * 128], rhs=hg_ap[:, kc, :], start=(kc == 0), stop=(kc == KC - 1)),
                                         reads=[w_b, hg_b], writes=[bankb[bk]], inc=(kc == KC - 1))
                                accs.append(bk)
                                if g >= 8:
                                    ut_ap, ut_b, _ = ut.next()
                                    P.op("dve", lambda e, bk=bk, ut_ap=ut_ap: e.tensor_mul(out=ut_ap, in0=banks[bk][:, :], in1=rsb_ap), reads=[bankb[bk], rsb_b], writes=[ut_b])
                                    P.op("act", lambda e, ut_ap=ut_ap, blk=blk, of=of: e.activation(out=of[:, blk, :], in_=ut_ap, func=AF.Gelu_apprx_tanh), reads=[ut_b], writes=[o_b])
                            if g < 4:
                                ci, si = (2, 3) if g >= 2 else (0, 1)
                                for pr in range(2):
                                    b0, b1 = accs[2 * pr], accs[2 * pr + 1]
                                    tm_ap, tm_b, _ = tmps.next()
                                    P.op("dve", lambda e, b0=b0, tm_ap=tm_ap, ci=ci: e.tensor_mul(out=tm_ap[:, 0, :], in0=banks[b0][:, :], in1=tab_ap[:, ci, :]), reads=[bankb[b0], tab_b], writes=[tm_b])
                                    P.op("dve", lambda e, b1=b1, tm_ap=tm_ap, si=si: e.tensor_mul(out=tm_ap[:, 1, :], in0=banks[b1][:, :], in1=tab_ap[:, si, :]), reads=[bankb[b1], tab_b], writes=[tm_b])
                                    P.op("dve", lambda e, b0=b0, tm_ap=tm_ap, si=si: e.tensor_mul(out=tm_ap[:, 2, :], in0=banks[b0][:, :], in1=tab_ap[:, si, :]), reads=[bankb[b0], tab_b], writes=[tm_b])
                                    P.op("dve", lambda e, b1=b1, tm_ap=tm_ap, ci=ci: e.tensor_mul(out=tm_ap[:, 3, :], in0=banks[b1][:, :], in1=tab_ap[:, ci, :]), reads=[bankb[b1], tab_b], writes=[tm_b])
                                    P.op("pool", lambda e, tm_ap=tm_ap, pr=pr, of=of: e.tensor_sub(out=of[:, 2 * pr, :], in0=tm_ap[:, 0, :], in1=tm_ap[:, 1, :]), reads=[tm_b], writes=[o_b])
                                    P.op("pool", lambda e, tm_ap=tm_ap, pr=pr, of=of: e.tensor_add(out=of[:, 2 * pr + 1, :], in0=tm_ap[:, 2, :], in1=tm_ap[:, 3, :]), reads=[tm_b], writes=[o_b])
                            dd, db = (qT_d, B_q) if g < 2 else ((kT_d, B_k) if g < 4 else (uT_d, B_u))
                            P.dma("pool", dd[col0:col0 + 512, t0:t0 + T].rearrange("(b p) s -> p b s", p=128), of, o_ds, reads=[o_b], writes=[db[t]])
                            if 2 <= g < 4:
                                o2_ap, o2_b, o2_ds = outr.next()
                                o2 = tm_view(o2_ap)
                                bk = ptr[ptr_i[0] % 2]; ptr_i[0] += 1
                                pv = banks[bk][:, :].bitcast(BF16)
                                for half in range(2):
                                    for cc in range(2):
                                        c = half * 2 + cc
                                        for blk in range(4):
                                            sl = (cc * 4 + blk) * 128
                                            P.op("pe", lambda e, sl=sl, blk=blk, c=c, of=of, pv=pv: e.transpose(out=pv[:, sl:sl + 128], in_=of[:, blk, c * 128:(c + 1) * 128], identity=ident),
                                                 reads=[o_b, Bc], writes=[bankb[bk]], inc=(blk == 3 and cc == 1))
                                    P.op("act", lambda e, half=half, o2=o2, pv=pv: e.activation(out=o2[:, half * 2:half * 2 + 2, :], in_=pv.rearrange("p (c f) -> p c f", c=2), func=AF.Copy), reads=[bankb[bk]], writes=[o2_b])
                                P.dma("pool", ktm_d[t0:t0 + T, col0:col0 + 512].rearrange("(c p) f -> p c f", p=128), o2, o2_ds, reads=[o2_b], writes=[B_ktm[t]])
                        else:
                            ot = tm_view(o_ap)
                            for c in range(NCH):
                                bk = next_acc()
                                for kc in range(KC):
                                    P.op("pe", lambda e, bk=bk, kc=kc, c=c, w_ap=w_ap: e.matmul(banks[bk][:, :], lhsT=hg_ap[:, kc, c * 128:(c + 1) * 128], rhs=w_ap[:, kc, :], start=(kc == 0), stop=(kc == KC - 1)),
                                         reads=[w_b, hg_b], writes=[bankb[bk]], inc=(kc == KC - 1))
                                if g < 6:
                                    P.op("act", lambda e, bk=bk, c=c, ot=ot: e.activation(out=ot[:, c, :], in_=banks[bk][:, :], func=AF.Copy, scale=rst_ap[:, c:c + 1]), reads=[bankb[bk], rst_b], writes=[o_b])
                                elif g < 8:
                                    P.op("act", lambda e, bk=bk, c=c, ot=ot: e.activation(out=ot[:, c, :], in_=banks[bk][:, :], func=AF.Silu, scale=rst_ap[:, c:c + 1]), reads=[bankb[bk], rst_b], writes=[o_b])
                                else:
                                    if c == 0:
                                        st_ap, st_b, _ = lnst.next()
                                        sv_slots = []
                                    sg_ap, sg_b, _ = svg.next()
                                    sx_ap, sx_b, _ = svx.next()
                                    sv_slots.append((sg_ap, sg_b, sx_ap, sx_b))
                                    for g4 in range(4):
                                        P.op("act", lambda e, bk=bk, c=c, g4=g4, sg_ap=sg_ap, st_ap=st_ap: e.activation(out=sg_ap[:, g4 * 128:(g4 + 1) * 128], in_=banks[bk][:, g4 * 128:(g4 + 1) * 128], func=AF.Gelu_apprx_tanh, scale=rst_ap[:, c:c + 1], accum_out=st_ap[:, 0, c * 4 + g4:c * 4 + g4 + 1]), reads=[bankb[bk], rst_b], writes=[sg_b, st_b])
                                    for g4 in range(4):
                                        P.op("act", lambda e, c=c, g4=g4, sg_ap=sg_ap, sx_ap=sx_ap, st_ap=st_ap: e.activation(out=sx_ap[:, g4 * 128:(g4 + 1) * 128], in_=sg_ap[:, g4 * 128:(g4 + 1) * 128], func=AF.Square, accum_out=st_ap[:, 1, c * 4 + g4:c * 4 + g4 + 1]), reads=[sg_b], writes=[sx_b, st_b])
                            if g >= 10:
                                P.op("dve", lambda e, st_ap=st_ap: e.tensor_scalar(out=st_ap[:, 2, :], in0=st_ap[:, 0, :], scalar1=1.0 / 128, scalar2=None, op0=ALU.mult), reads=[st_b], writes=[st_b])
                                P.op("dve", lambda e, st_ap=st_ap: e.tensor_mul(out=st_ap[:, 3, :], in0=st_ap[:, 2, :], in1=st_ap[:, 2, :]), reads=[st_b], writes=[st_b])
                                P.op("dve", lambda e, st_ap=st_ap: e.scalar_tensor_tensor(out=st_ap[:, 3, :], in0=st_ap[:, 1, :], scalar=1.0 / 128, in1=st_ap[:, 3, :], op0=ALU.mult, op1=ALU.subtract), reads=[st_b], writes=[st_b])
                                P.op("dve", lambda e, st_ap=st_ap: e.tensor_scalar(out=st_ap[:, 4, :], in0=st_ap[:, 3, :], scalar1=EPS, scalar2=None, op0=ALU.add), reads=[st_b], writes=[st_b])
                                P.op("act", lambda e, st_ap=st_ap: e.activation(out=st_ap[:, 4, :], in_=st_ap[:, 4, :], func=AF.Sqrt), reads=[st_b], writes=[st_b])
                                P.op("dve", lambda e, st_ap=st_ap: e.reciprocal(out=st_ap[:, 4, :], in_=st_ap[:, 4, :]), reads=[st_b], writes=[st_b])
                                P.op("dve", lambda e, st_ap=st_ap: e.scalar_tensor_tensor(out=st_ap[:, 5, :], in0=st_ap[:, 2, :], scalar=-1.0, in1=st_ap[:, 4, :], op0=ALU.mult, op1=ALU.mult), reads=[st_b], writes=[st_b])
                                for c in range(NCH):
                                    sg_ap, sg_b, sx_ap, sx_b = sv_slots[c]
                                    for g4 in range(4):
                                        P.op("act", lambda e, c=c, g4=g4, sg_ap=sg_ap, sx_ap=sx_ap, st_ap=st_ap: e.activation(out=sx_ap[:, g4 * 128:(g4 + 1) * 128], in_=sg_ap[:, g4 * 128:(g4 + 1) * 128], func=AF.Identity, scale=st_ap[:, 4, c * 4 + g4:c * 4 + g4 + 1], bias=st_ap[:, 5, c * 4 + g4:c * 4 + g4 + 1]), reads=[sg_b, st_b], writes=[sx_b])
                                    P.op("pool", lambda e, sx_ap=sx_ap, col0=col0: e.tensor_mul(out=sx_ap, in0=sx_ap, in1=lng[:, col0:col0 + 512]), reads=[sx_b, Bl], writes=[sx_b])
                                    P.op("pool", lambda e, sx_ap=sx_ap, col0=col0, c=c, ot=ot: e.tensor_add(out=ot[:, c, :], in0=sx_ap, in1=lnb[:, col0:col0 + 512]), reads=[sx_b, Bl], writes=[o_b])
                            dd, db = (vtm_d, B_v) if g < 6 else ((gact_d, B_g) if g < 8 else (svln_d, B_sv))
                            P.dma("pool", dd[t0:t0 + T, col0:col0 + 512].rearrange("(c p) f -> p c f", p=128), ot, o_ds, reads=[o_b], writes=[db[t]])
                        if nx and 4 <= gi < 8:
                            p1_norm_pe(t + 1, gi - 4)

                p1_begin(0)
                p1_load(0, 0); p1_load(0, 1)
                p1_norm_elem(0, 0); p1_norm_pe(0, 0)
                p1_load(0, 2)
                p1_norm_elem(0, 1); p1_norm_pe(0, 1)
                p1_load(0, 3)
                p1_norm_elem(0, 2); p1_norm_pe(0, 2)
                p1_norm_elem(0, 3); p1_norm_pe(0, 3)
                p1_norm_fin(0)
                for t in range(NTILE):
                    if l == 0:
                        issue_casts(n=1)
                    p1_proj(t)
                phase_end()
            if stop_after == (l, 1):
                break

            if True:
                qr = Ring(P, A, 2, [8, T], BF16, "q2")
                kr = Ring(P, A, 2, [8, T], BF16, "k2")
                ktr = Ring(P, A, 2, [NCH, RW], BF16, "kt2")
                vr = Ring(P, A, 2, [NCH, RW], BF16, "v2")
                svr = Ring(P, A, 2, [NCH, SGW], BF16, "sv2")
                ur = Ring(P, A, 2, [8, T], BF16, "u2")
                Sf = A.alloc([8, DH], F32)
                Sfb2 = [A.alloc([8, DH], BF16), A.alloc([8, DH], BF16)]
                B_S = [Buf(f"S{h}") for h in range(H)]
                B_Sb2 = [[Buf(f"Sb{i}{h}") for h in range(H)] for i in range(2)]
                par = [0]
                ptr_ = Ring(P, A, 2, [H, 128], BF16, "PT", dsem=False)
                qfr = Ring(P, A, 2, [8, 128], BF16, "qf", dsem=False)
                kfr = Ring(P, A, 2, [RW], BF16, "kf", dsem=False)
                yp_sb = A.alloc([NCH, RW], F32); Byp = Buf("yp_sb"); dyp = P.new_dsem("dyp")
                ysg_sb = A.alloc([8, T], BF16); Bysg = Buf("ysg_sb"); dysg = P.new_dsem("dysg")
                BK_SC, BK_Y, BK_S, BK_Z = 0, (1, 2), (3, 4), (5, 6)
                maskv = maskT.rearrange("p h i -> p (h i)")

                def p2_load(t):
                    t0 = t * T
                    q_ap, q_b, q_ds = qr.next()
                    P.dma("sp", q_ap, qT_d[:, t0:t0 + T].rearrange("(b p) s -> p b s", p=128), q_ds, reads=[B_q[t]], writes=[q_b])
                    k_ap, k_b, k_ds = kr.next()
                    P.dma("sp", k_ap, kT_d[:, t0:t0 + T].rearrange("(b p) s -> p b s", p=128), k_ds, reads=[B_k[t]], writes=[k_b])
                    kt_ap, kt_b, kt_ds = ktr.next()
                    P.dma("sp", kt_ap, ktm_d[t0:t0 + T, :].rearrange("(c p) f -> p c f", p=128), kt_ds, reads=[B_ktm[t]], writes=[kt_b])
                    v_ap, v_b, v_ds = vr.next()
                    P.dma("sp", v_ap, vtm_d[t0:t0 + T, :].rearrange("(c p) f -> p c f", p=128), v_ds, reads=[B_v[t]], writes=[v_b])
                    sv_ap, sv_b, sv_ds = svr.next()
                    P.dma("sp", sv_ap, svln_d[t0:t0 + T, :].rearrange("(c p) f -> p c f", p=128), sv_ds, reads=[B_sv[t]], writes=[sv_b])
                    u_ap, u_b, u_ds = ur.next()
                    P.dma("sp", u_ap, uT_d[:, t0:t0 + T].rearrange("(b p) s -> p b s", p=128), u_ds, reads=[B_u[t]], writes=[u_b])
                    return (q_ap, q_b, k_ap, k_b, kt_ap, kt_b, v_ap, v_b, sv_ap, sv_b, u_ap, u_b)

                def state_update(Sx, Sxb, BS, BSb, kdec_ap, cd0, kt_ap, kt_b, v_ap, v_b, c, kfr):
                    kf_ap, kf_b, _ = kfr.next()
                    for h in range(H):
                        P.op("act", lambda e, h=h: e.activation(out=kf_ap[:, h * DH:(h + 1) * DH], in_=kt_ap[:, c, h * DH:(h + 1) * DH], func=AF.Copy, scale=kdec_ap[:, h:h + 1]), reads=[kt_b, Bl], writes=[kf_b])
                    for h in range(H):
                        bk = BK_S[h % 2]
                        for dc in range(2):
                            P.op("pe", lambda e, h=h, dc=dc, bk=bk: e.matmul(banks[bk][:, dc * DH:(dc + 1) * DH], lhsT=kf_ap[:, h * DH + dc * 128: h * DH + (dc + 1) * 128], rhs=v_ap[:, c, h * DH:(h + 1) * DH], start=True, stop=True),
                                 reads=[kf_b, v_b], writes=[bankb[bk]], inc=(dc == 1))
                        Sh = Sx[:, 2 * h:2 * h + 2, :].rearrange("p a b -> p (a b)")
                        Shb = Sxb[:, 2 * h:2 * h + 2, :].rearrange("p a b -> p (a b)")
                        P.op("dve", lambda e, h=h, bk=bk, Sh=Sh: e.scalar_tensor_tensor(out=Sh, in0=Sh, scalar=cdec[:, cd0 + h:cd0 + h + 1], in1=banks[bk][:, :], op0=ALU.mult, op1=ALU.add), reads=[bankb[bk], Bl], writes=[BS[h]])
                        P.op("act", lambda e, Sh=Sh, Shb=Shb: e.activation(out=Shb, in_=Sh, func=AF.Copy), reads=[BS[h]], writes=[BSb[h]])

                def state_reset(Sx, Sxb, BS, BSb, zero):
                    Sa = Sx.rearrange("p a b -> p (a b)")
                    Sab = Sxb.rearrange("p a b -> p (a b)")
                    if zero:
                        P.op("pool", lambda e: e.memset(Sa, 0.0), writes=BS)
                    else:
                        P.op("dve", lambda e: e.tensor_scalar(out=Sa, in0=Sa, scalar1=carry_sb[:, 0:1], scalar2=None, op0=ALU.mult), reads=[Bc], writes=BS)
                    P.op("pool", lambda e: e.tensor_copy(out=Sab, in_=Sa), reads=BS, writes=BSb)

                def p2_chunk(t, c, bufs):
                    (q_ap, q_b, k_ap, k_b, kt_ap, kt_b, v_ap, v_b, sv_ap, sv_b, u_ap, u_b) = bufs
                    gc = t * NCH + c
                    cs = slice(c * 128, (c + 1) * 128)
                    if gc == 0:
                        state_reset(Sf, Sfb2[par[0]], B_S, B_Sb2[par[0]], True)
                    elif gc * 128 == SEG:
                        par[0] ^= 1
                        state_reset(Sf, Sfb2[par[0]], B_S, B_Sb2[par[0]], False)
                    Sfb, B_Sb = Sfb2[par[0]], B_Sb2[par[0]]
                    par[0] ^= 1
                    Sfb_n, B_Sb_n = Sfb2[par[0]], B_Sb2[par[0]]
                    for h in range(H):
                        for dc in range(2):
                            P.op("pe", lambda e, h=h, dc=dc: e.matmul(banks[BK_SC][:, h * 128:(h + 1) * 128], lhsT=k_ap[:, 2 * h + dc, cs], rhs=q_ap[:, 2 * h + dc, cs], start=(dc == 0), stop=(dc == 1)),
                                 reads=[k_b, q_b], writes=[bankb[BK_SC]], inc=(h == H - 1 and dc == 1))
                    pt_ap, pt_b, _ = ptr_.next()
                    P.op("dve", lambda e: e.tensor_tensor(out=pt_ap.rearrange("p h i -> p (h i)"), in0=banks[BK_SC][:, :], in1=maskv, op=ALU.mult), reads=[bankb[BK_SC], Bl], writes=[pt_b])
                    qf_ap, qf_b, _ = qfr.next()
                    P.op("pool", lambda e: e.tensor_tensor(out=qf_ap, in0=q_ap[:, :, cs], in1=qdecf, op=ALU.mult), reads=[q_b, Bl], writes=[qf_b])
                    state_update(Sf, Sfb_n, B_S, B_Sb_n, kdecf, 0, kt_ap, kt_b, v_ap, v_b, c, kfr)
                    for g in range(G):
                        bk = BK_Z[g // 4]
                        o = banks[bk][:, (g % 4) * 128:(g % 4 + 1) * 128]
                        P.op("pe", lambda e, g=g, o=o: e.matmul(o, lhsT=sv_ap[:, c, g * 128:(g + 1) * 128], rhs=wsT_b[:, g, :], start=True, stop=False), reads=[sv_b, Bl], writes=[bankb[bk]], inc=False)
                        P.op("pe", lambda e, g=g, o=o: e.matmul(o, lhsT=ones[0:1, :], rhs=bs_b[0:1, g * 128:(g + 1) * 128], start=False, stop=True), reads=[Bc, Bl], writes=[bankb[bk]], inc=(g % 4 == 3))
                    for zb in range(2):
                        bk = BK_Z[zb]
                        P.op("dve", lambda e, zb=zb, bk=bk: e.tensor_tensor(out=ysg_sb[:, zb * 4:(zb + 1) * 4, cs], in0=banks[bk][:, :].rearrange("p (g i) -> p g i", g=4), in1=u_ap[:, zb * 4:(zb + 1) * 4, cs], op=ALU.mult), reads=[bankb[bk], u_b], writes=[Bysg])
                    for h in range(H):
                        bk = BK_Y[h // 2]
                        o = banks[bk][:, (h % 2) * DH:(h % 2 + 1) * DH]
                        P.op("pe", lambda e, h=h, o=o: e.matmul(o, lhsT=pt_ap[:, h, :], rhs=v_ap[:, c, h * DH:(h + 1) * DH], start=True, stop=False), reads=[pt_b, v_b], writes=[bankb[bk]], inc=False)
                        P.op("pe", lambda e, h=h, o=o: e.matmul(o, lhsT=qf_ap[:, 2 * h, :], rhs=Sfb[:, 2 * h, :], start=False, stop=False), reads=[qf_b, B_Sb[h]], writes=[bankb[bk]], inc=False)
                        P.op("pe", lambda e, h=h, o=o: e.matmul(o, lhsT=qf_ap[:, 2 * h + 1, :], rhs=Sfb[:, 2 * h + 1, :], start=False, stop=True), reads=[qf_b, B_Sb[h]], writes=[bankb[bk]], inc=True)
                    for half in range(2):
                        bk = BK_Y[half]
                        P.op("act", lambda e, half=half, bk=bk: e.activation(out=yp_sb[:, c, half * 512:(half + 1) * 512], in_=banks[bk][:, :], func=AF.Copy), reads=[bankb[bk]], writes=[Byp])

                nxt_bufs = p2_load(0)
                for t in range(NTILE):
                    cur = nxt_bufs
                    if t + 1 < NTILE:
                        nxt_bufs = p2_load(t + 1)
                    if l == 0:
                        issue_casts(n=1)
                    for c in range(NCH):
                        p2_chunk(t, c, cur)
                    t0 = t * T
                    P.dma("pool", ypart_d[t0:t0 + T, :].rearrange("(c p) f -> p c f", p=128), yp_sb, dyp, reads=[Byp], writes=[B_yp[t]])
                    P.dma("pool", ymixT_d[RW:2 * RW, t0:t0 + T].rearrange("(b p) s -> p b s", p=128), ysg_sb, dysg, reads=[Bysg], writes=[B_yms[t]])
                phase_end()
            if stop_after == (l, 2):
                break

            if True:
                qr = Ring(P, A, 2, [8, T], BF16, "q3")
                ktr = Ring(P, A, 2, [NCH, RW], BF16, "kt3")
                vr = Ring(P, A, 2, [NCH, RW], BF16, "v3")
                ypr = Ring(P, A, 2, [NCH, RW], F32, "yp3")
                gr = Ring(P, A, 2, [NCH, RW], BF16, "g3")
                Sb_ = A.alloc([8, DH], F32)
                Sbb2 = [A.alloc([8, DH], BF16), A.alloc([8, DH], BF16)]
                B_S = [Buf(f"S{h}") for h in range(H)]
                B_Sb2 = [[Buf(f"Sb{i}{h}") for h in range(H)] for i in range(2)]
                par = [0]
                kfr = Ring(P, A, 2, [RW], BF16, "kb", dsem=False)
                yr_ = Ring(P, A, 3, [RW], F32, "y3", dsem=False)
                ynr = Ring(P, A, 2, [RW], BF16, "yn3", dsem=False)
                junk = A.alloc([DH], F32); Bjunk = Buf("junk")
                ssr = Ring(P, A, 2, [H], F32, "ss3", dsem=False)
                yret_sb = A.alloc([8, T], BF16); Byret = Buf("yret_sb"); dyret = P.new_dsem("dyret")
                BK_Y, BK_S, BK_T = (1, 2), (3, 4), (5, 6)
                tr_i = [0]

                def p3_load(t):
                    t0 = t * T
                    q_ap, q_b, q_ds = qr.next()
                    P.dma("sp", q_ap, qT_d[:, t0:t0 + T].rearrange("(b p) s -> p b s", p=128), q_ds, reads=[B_q[t]], writes=[q_b])
                    kt_ap, kt_b, kt_ds = ktr.next()
                    P.dma("sp", kt_ap, ktm_d[t0:t0 + T, :].rearrange("(c p) f -> p c f", p=128), kt_ds, reads=[B_ktm[t]], writes=[kt_b])
                    v_ap, v_b, v_ds = vr.next()
                    P.dma("sp", v_ap, vtm_d[t0:t0 + T, :].rearrange("(c p) f -> p c f", p=128), v_ds, reads=[B_v[t]], writes=[v_b])
                    yp_ap, yp_b, yp_ds = ypr.next()
                    P.dma("sp", yp_ap, ypart_d[t0:t0 + T, :].rearrange("(c p) f -> p c f", p=128), yp_ds, reads=[B_yp[t]], writes=[yp_b])
                    g_ap, g_b, g_ds = gr.next()
                    P.dma("sp", g_ap, gact_d[t0:t0 + T, :].rearrange("(c p) f -> p c f", p=128), g_ds, reads=[B_g[t]], writes=[g_b])
                    return (q_ap, q_b, kt_ap, kt_b, v_ap, v_b, yp_ap, yp_b, g_ap, g_b)

                def p3_chunk(t, c, bufs):
                    (q_ap, q_b, kt_ap, kt_b, v_ap, v_b, yp_ap, yp_b, g_ap, g_b) = bufs
                    gc = t * NCH + c
                    cs = slice(c * 128, (c + 1) * 128)
                    if gc == NTILE * NCH - 1:
                        state_reset(Sb_, Sbb2[par[0]], B_S, B_Sb2[par[0]], True)
                    elif (gc + 1) * 128 == SEG:
                        par[0] ^= 1
                        state_reset(Sb_, Sbb2[par[0]], B_S, B_Sb2[par[0]], False)
                    Sbb, B_Sb = Sbb2[par[0]], B_Sb2[par[0]]
                    par[0] ^= 1
                    state_update(Sb_, Sbb2[par[0]], B_S, B_Sb2[par[0]], kdecb, 4, kt_ap, kt_b, v_ap, v_b, c, kfr)
                    y_ap, y_b, _ = yr_.next()
                    for h in range(H):
                        bk = BK_Y[h // 2]
                        o = banks[bk][:, (h % 2) * DH:(h % 2 + 1) * DH]
                        for dc in range(2):
                            P.op("pe", lambda e, h=h, dc=dc, o=o: e.matmul(o, lhsT=q_ap[:, 2 * h + dc, cs], rhs=Sbb[:, 2 * h + dc, :], start=(dc == 0), stop=(dc == 1)), reads=[q_b, B_Sb[h]], writes=[bankb[bk]], inc=(dc == 1))
                        P.op("dve", lambda e, h=h, o=o: e.scalar_tensor_tensor(out=y_ap[:, h * DH:(h + 1) * DH], in0=o, scalar=qdecb[:, h:h + 1], in1=yp_ap[:, c, h * DH:(h + 1) * DH], op0=ALU.mult, op1=ALU.add), reads=[bankb[bk], yp_b, Bl], writes=[y_b])
                    return (y_ap, y_b, g_ap, g_b, c, cs)

                def p3_tail(st_):
                    (y_ap, y_b, g_ap, g_b, c, cs) = st_
                    ss_ap, ss_b, _ = ssr.next()
                    for h in range(H):
                        P.op("act", lambda e, h=h: e.activation(out=junk, in_=y_ap[:, h * DH:(h + 1) * DH], func=AF.Square, accum_out=ss_ap[:, h:h + 1]), reads=[y_b], writes=[Bjunk, ss_b])
                    rstd_from_psum(ss_ap, ss_ap, DH, [ss_b], [ss_b])
                    yn_ap, yn_b, _ = ynr.next()
                    for h in range(H):
                        P.op("dve", lambda e, h=h: e.scalar_tensor_tensor(out=yn_ap[:, h * DH:(h + 1) * DH], in0=y_ap[:, h * DH:(h + 1) * DH], scalar=ss_ap[:, h:h + 1], in1=g_ap[:, c, h * DH:(h + 1) * DH], op0=ALU.mult, op1=ALU.mult), reads=[y_b, ss_b, g_b], writes=[yn_b])
                    bk = BK_T[tr_i[0] % 2]; tr_i[0] += 1
                    pv = banks[bk][:, :].bitcast(BF16)
                    for blk in range(8):
                        P.op("pe", lambda e, blk=blk, pv=pv: e.transpose(out=pv[:, blk * 128:(blk + 1) * 128], in_=yn_ap[:, blk * 128:(blk + 1) * 128], identity=ident), reads=[yn_b, Bc], writes=[bankb[bk]], inc=(blk == 7))
                    P.op("act", lambda e, pv=pv: e.activation(out=yret_sb[:, :, cs], in_=pv.rearrange("p (b i) -> p b i", b=8), func=AF.Copy), reads=[bankb[bk]], writes=[Byret])

                nxt_bufs = p3_load(NTILE - 1)
                pend_tail = []

                def flush_tail(keep):
                    while len(pend_tail) > keep:
                        st_, t_, c_ = pend_tail.pop(0)
                        p3_tail(st_)
                        if c_ == 0:
                            t0 = t_ * T
                            P.dma("pool", ymixT_d[0:RW, t0:t0 + T].rearrange("(b p) s -> p b s", p=128), yret_sb, dyret, reads=[Byret], writes=[B_ymr[t_]])

                for t in range(NTILE - 1, -1, -1):
                    cur = nxt_bufs
                    if l == 0:
                        issue_casts(n=1)
                    for c in range(NCH - 1, -1, -1):
                        st_ = p3_chunk(t, c, cur)
                        flush_tail(0)
                        pend_tail.append((st_, t, c))
                        if c == NCH - 1 and t - 1 >= 0:
                            nxt_bufs = p3_load(t - 1)
                flush_tail(0)
                phase_end()
            if stop_after == (l, 3):
                break

            if True:
                last_layer = (l == DEPTH - 1)
                h_sb = A.alloc([KC, T], F32); Bh = [Buf(f"h{i}") for i in range(KC)]; dhq = [P.new_dsem(f"dh4{q}") for q in range(4)]; dhsq = [P.new_dsem(f"dhs4{q}") for q in range(4)]
                ym_sb = A.alloc([KC, T], BF16); Bym = Buf("ym"); dym = P.new_dsem("dym4")
                hg_sb = A.alloc([KC, T], BF16); Bhg = Buf("hg4")
                aT = A.alloc([FC, T], BF16); BaT = [Buf(f"aT{i}") for i in range(FC)]
                wr = Ring(P, A, 6, [KC * 256], BF16, "w4")
                dp = P.new_dsem("dp4")
                p_b = A.alloc([2, T], BF16); Bpb = Buf("pb")
                rsb_ap = A.alloc([T], F32); rsb_b = Buf("rsb4")
                sqr = Ring(P, A, 2, [T], BF16, "sq4", dsem=False)
                tgr = Ring(P, A, 4, [T], F32, "tf4", dsem=False)
                sfr = tgr
                mr = tgr
                sgr = Ring(P, A, 4, [T], BF16, "tb4", dsem=False)
                tur = sgr
                outr = Ring(P, A, 2, [2, T], F32, "o4") if last_layer else None
                pacc = [1, 2, 3, 4, 5, 6, 7]
                pacc_i = [0]

                def next_acc():
                    b = pacc[pacc_i[0] % len(pacc)]
                    pacc_i[0] += 1
                    return b

                wq = []
                wsched = []
                for t in range(NTILE):
                    wsched += [("w_out", g, None) for g in range(8)]
                    if p4_stage == 1:
                        continue
                    for g in range(22):
                        wsched += [("w_ffn_gate", g, None), ("w_ffn_up", g, None)]
                    for g in range(16):
                        wsched += [("w_ffn_down", g, 0), ("w_ffn_down", g, 1)]
                    if p4_stage == 2:
                        continue
                    for g in range(8):
                        wsched += [("w_ple_gate", g, None), ("w_ple_proj", g, None)]
                wi = [0]

                def wprefetch():
                    if wi[0] < len(wsched):
                        name, g, half = wsched[wi[0]]
                        wi[0] += 1
                        ap, b, ds = wr.next()
                        off, kc_, gw, ng = WOFF[name]
                        src = wview(l, name, g)
                        if half is not None:
                            kc_ = kc_ // 2
                            src = src[:, half * kc_ * gw:(half + 1) * kc_ * gw]
                        P.dma("sp", ap[:, 0:kc_ * gw], src, ds, reads=[B_w[(l, name)]], writes=[b])
                        wq.append((ap[:, 0:kc_ * gw].rearrange("p (k g) -> p k g", k=kc_), b))

                def wnext():
                    return wq.pop(0)

                def norm_block(nb, gidx, pend):
                    sq_ap, sq_b, _ = sqr.next()
                    P.op("act", lambda e, nb=nb, sq_ap=sq_ap: e.activation(out=sq_ap, in_=h_sb[:, nb, :], func=AF.Square), reads=[Bh[nb]], writes=[sq_b])
                    P.op("act", lambda e, nb=nb: e.activation(out=hg_sb[:, nb, :], in_=h_sb[:, nb, :], func=AF.Copy, scale=gain_ap(gidx, nb)), reads=[Bh[nb], Bc], writes=[Bhg])
                    pend.append((nb, sq_ap, sq_b))

                def norm_pe(pend, keep):
                    while len(pend) > keep:
                        nb, sq_ap, sq_b = pend.pop(0)
                        P.op("pe", lambda e, nb=nb, sq_ap=sq_ap: e.matmul(banks[0][:, :], lhsT=ones, rhs=sq_ap, start=(nb == 0), stop=(nb == KC - 1)), reads=[sq_b, Bc], writes=[bankb[0]], inc=True)

                def dbg_store(t0):
                    for q4 in range(4):
                        P.dma("pool", hT[q4 * 512:(q4 + 1) * 512, t0:t0 + T].rearrange("(k p) s -> p k s", p=128), h_sb[:, q4 * 4:(q4 + 1) * 4, :], dhsq[q4],
                              reads=Bh[q4 * 4:(q4 + 1) * 4])
                        if t0 // T + 1 < NTILE:
                            h_load(t0 // T + 1, q4)

                def h_load(t_, q4):
                    P.dma("sp", h_sb[:, q4 * 4:(q4 + 1) * 4, :], src_h[q4 * 512:(q4 + 1) * 512, t_ * T:(t_ + 1) * T].rearrange("(k p) s -> p k s", p=128), dhq[q4],
                          reads=[B_hT[t_]] if l > 0 else [], writes=Bh[q4 * 4:(q4 + 1) * 4])

                def ym_load(t_):
                    for q2 in range(2):
                        P.dma("sp", ym_sb[:, q2 * 8:(q2 + 1) * 8, :], ymixT_d[q2 * 1024:(q2 + 1) * 1024, t_ * T:(t_ + 1) * T].rearrange("(k p) s -> p k s", p=128), dym,
                              reads=[B_ymr[t_], B_yms[t_]], writes=[Bym])

                def p4_tile(t):
                    t0 = t * T
                    if t == 0:
                        ym_load(0)
                        for q4 in range(4):
                            h_load(0, q4)
                    P.dma("pool", p_b, pT[l * PLE:(l + 1) * PLE, t0:t0 + T].rearrange("(k p) s -> p k s", p=128), dp, writes=[Bpb])
                    pend = []
                    for g in range(8):
                        w_ap, w_b = wnext()
                        for blk in range(2):
                            nb = g * 2 + blk
                            bk = next_acc()
                            for kc in range(KC):
                                P.op("pe", lambda e, bk=bk, blk=blk, kc=kc, w_ap=w_ap: e.matmul(banks[bk][:, :], lhsT=w_ap[:, kc, blk * 128:(blk + 1) * 128], rhs=ym_sb[:, kc, :], start=(kc == 0), stop=(kc == KC - 1)),
                                     reads=[w_b, Bym], writes=[bankb[bk]], inc=(kc == KC - 1))
                            P.op("dve", lambda e, bk=bk, nb=nb: e.tensor_add(out=h_sb[:, nb, :], in0=banks[bk][:, :], in1=h_sb[:, nb, :]), reads=[bankb[bk]], writes=[Bh[nb]])
                            norm_pe(pend, 0)
                            norm_block(nb, l * 3 + 1, pend)
                        wprefetch()
                    norm_pe(pend, 0)
                    rstd_from_psum(banks[0][:, :], rsb_ap, D, [bankb[0]], [rsb_b])
                    if t + 1 < NTILE:
                        ym_load(t + 1)
                    issue_casts(n=4)
                    if p4_stage == 1:
                        return dbg_store(t0)
                    for g in range(22):
                        wg_ap, wg_b = wnext()
                        wu_ap, wu_b = wnext()
                        for blk in range(2):
                            fb = g * 2 + blk
                            bg = next_acc()
                            for kc in range(KC):
                                P.op("pe", lambda e, bg=bg, blk=blk, kc=kc, wg_ap=wg_ap: e.matmul(banks[bg][:, :], lhsT=wg_ap[:, kc, blk * 128:(blk + 1) * 128], rhs=hg_sb[:, kc, :], start=(kc == 0), stop=(kc == KC - 1)),
                                     reads=[wg_b, Bhg], writes=[bankb[bg]], inc=(kc == KC - 1))
                            bu = next_acc()
                            for kc in range(KC):
                                P.op("pe", lambda e, bu=bu, blk=blk, kc=kc, wu_ap=wu_ap: e.matmul(banks[bu][:, :], lhsT=wu_ap[:, kc, blk * 128:(blk + 1) * 128], rhs=hg_sb[:, kc, :], start=(kc == 0), stop=(kc == KC - 1)),
                                     reads=[wu_b, Bhg], writes=[bankb[bu]], inc=(kc == KC - 1))
                            tg_ap, tg_b, _ = tgr.next()
                            sg_ap, sg_b, _ = sgr.next()
                            tu_ap, tu_b, _ = tur.next()
                            P.op("dve", lambda e, bg=bg, tg_ap=tg_ap: e.tensor_mul(out=tg_ap, in0=banks[bg][:, :], in1=rsb_ap), reads=[bankb[bg], rsb_b], writes=[tg_b])
                            P.op("act", lambda e, tg_ap=tg_ap, sg_ap=sg_ap: e.activation(out=sg_ap, in_=tg_ap, func=AF.Silu), reads=[tg_b], writes=[sg_b])
                            P.op("dve", lambda e, bu=bu, tu_ap=tu_ap: e.tensor_mul(out=tu_ap, in0=banks[bu][:, :], in1=rsb_ap), reads=[bankb[bu], rsb_b], writes=[tu_b])
                            P.op("pool", lambda e, fb=fb, sg_ap=sg_ap, tu_ap=tu_ap: e.tensor_mul(out=aT[:, fb, :], in0=sg_ap, in1=tu_ap), reads=[sg_b, tu_b], writes=[BaT[fb]])
                        wprefetch(); wprefetch()
                    pend = []
                    for nb in range(KC):
                        bk = next_acc()
                        for half in range(2):
                            w_ap, w_b = wnext()
                            for f2 in range(FC // 2):
                                fb = half * (FC // 2) + f2
                                P.op("pe", lambda e, bk=bk, fb=fb, f2=f2, w_ap=w_ap: e.matmul(banks[bk][:, :], lhsT=w_ap[:, f2, :], rhs=aT[:, fb, :], start=(fb == 0), stop=(fb == FC - 1)),
                                     reads=[w_b, BaT[fb]], writes=[bankb[bk]], inc=(fb == FC - 1 or f2 == FC // 2 - 1))
                            wprefetch()
                        P.op("dve", lambda e, bk=bk, nb=nb: e.tensor_add(out=h_sb[:, nb, :], in0=banks[bk][:, :], in1=h_sb[:, nb, :]), reads=[bankb[bk]], writes=[Bh[nb]])
                        norm_pe(pend, 0)
                        norm_block(nb, l * 3 + 2, pend)
                    norm_pe(pend, 0)
                    rstd_from_psum(banks[0][:, :], rsb_ap, D, [bankb[0]], [rsb_b])
                    if p4_stage == 2:
                        return dbg_store(t0)
                    pend = []
                    for g in range(8):
                        wg_ap, wg_b = wnext()
                        wp_ap, wp_b = wnext()
                        for blk in range(2):
                            nb = g * 2 + blk
                            bg = next_acc()
                            for kc in range(KC):
                                P.op("pe", lambda e, bg=bg, blk=blk, kc=kc, wg_ap=wg_ap: e.matmul(banks[bg][:, :], lhsT=wg_ap[:, kc, blk * 128:(blk + 1) * 128], rhs=hg_sb[:, kc, :], start=(kc == 0), stop=(kc == KC - 1)),
                                     reads=[wg_b, Bhg], writes=[bankb[bg]], inc=(kc == KC - 1))
                            bp = next_acc()
                            for kc in range(2):
                                P.op("pe", lambda e, bp=bp, blk=blk, kc=kc, wp_ap=wp_ap: e.matmul(banks[bp][:, :], lhsT=wp_ap[:, kc, blk * 128:(blk + 1) * 128], rhs=p_b[:, kc, :], start=(kc == 0), stop=(kc == 1)),
                                     reads=[wp_b, Bpb], writes=[bankb[bp]], inc=(kc == 1))
                            tg_ap, tg_b, _ = tgr.next()
                            sf_ap, sf_b, _ = sfr.next()
                            m_ap, m_b, _ = mr.next()
                            P.op("dve", lambda e, bg=bg, tg_ap=tg_ap: e.tensor_mul(out=tg_ap, in0=banks[bg][:, :], in1=rsb_ap), reads=[bankb[bg], rsb_b], writes=[tg_b])
                            P.op("act", lambda e, tg_ap=tg_ap, sf_ap=sf_ap: e.activation(out=sf_ap, in_=tg_ap, func=AF.Sigmoid), reads=[tg_b], writes=[sf_b])
                            P.op("dve", lambda e, bp=bp, m_ap=m_ap, sf_ap=sf_ap: e.tensor_mul(out=m_ap, in0=banks[bp][:, :], in1=sf_ap), reads=[bankb[bp], sf_b], writes=[m_b])
                            P.op("pool", lambda e, nb=nb, m_ap=m_ap: e.tensor_add(out=h_sb[:, nb, :], in0=h_sb[:, nb, :], in1=m_ap), reads=[m_b], writes=[Bh[nb]])
                            if not last_layer and nb % 4 == 3:
                                q4 = nb // 4
                                P.dma("pool", hT[q4 * 512:(q4 + 1) * 512, t0:t0 + T].rearrange("(k p) s -> p k s", p=128), h_sb[:, q4 * 4:(q4 + 1) * 4, :], dhsq[q4],
                                      reads=Bh[q4 * 4:(q4 + 1) * 4], writes=[B_hT[t]])
                                if t + 1 < NTILE:
                                    h_load(t + 1, q4)
                            if last_layer and p4_stage is None:
                                sq_ap, sq_b, _ = sqr.next()
                                P.op("act", lambda e, nb=nb, sq_ap=sq_ap: e.activation(out=sq_ap, in_=h_sb[:, nb, :], func=AF.Square), reads=[Bh[nb]], writes=[sq_b])
                                norm_pe(pend, 0)
                                pend.append((nb, sq_ap, sq_b))
                        wprefetch(); wprefetch()
                    if last_layer and p4_stage is None:
                        norm_pe(pend, 0)
                        rstd_from_psum(banks[0][:, :], rsb_ap, D, [bankb[0]], [rsb_b])
                        for q8 in range(8):
                            o_ap, o_b, o_ds = outr.next()
                            for j in range(2):
                                nb = q8 * 2 + j
                                P.op("dve", lambda e, nb=nb, j=j, o_ap=o_ap: e.scalar_tensor_tensor(out=o_ap[:, j, :], in0=h_sb[:, nb, :], scalar=gain_ap(DEPTH * 3, nb), in1=rsb_ap, op0=ALU.mult, op1=ALU.mult), reads=[Bh[nb], rsb_b, Bc], writes=[o_b])
                            P.dma("pool", yT[q8 * 256:(q8 + 1) * 256, t0:t0 + T].rearrange("(k p) s -> p k s", p=128), o_ap, o_ds, reads=[o_b])
                            if q8 % 2 == 1 and t + 1 < NTILE:
                                h_load(t + 1, q8 // 2)


                issue_casts(layer=l)
                for _ in range(6):
                    wprefetch()
                for t in range(NTILE):
                    p4_tile(t)
                issue_casts(layer=DEPTH - 1)
                phase_end()
            if stop_after == (l, 4):
                break

        P.barrier()
        blk = st.enter_context(nc.Block())
        P.emit(blk)
    return nc


def _rearr(W, gw):
    K, N = W.shape
    return np.ascontiguousarray(W.reshape(K // 128, 128, N // gw, gw).transpose(2, 1, 0, 3)).reshape(-1)


def prep_shared(inp, DEPTH):
    f32 = np.float32
    wcat = np.empty((DEPTH * WTOT,), f32)
    for l in range(DEPTH):
        for n, k_, nn_, g_ in WSPEC:
            off = l * WTOT + WOFF[n][0]
            wcat[off: off + k_ * nn_] = _rearr(np.asarray(inp[n][l], f32), g_)
    gl = []
    for l in range(DEPTH):
        for nm in ("norm_mix_g", "norm_ffn_g", "norm_ple_g"):
            gl.append(np.asarray(inp[nm][l], f32).reshape(KC, 128).T)
    gl.append(np.asarray(inp["norm_final_g"], f32).reshape(KC, 128).T)
    gains = np.ascontiguousarray(np.concatenate(gl, axis=1))
    dec = np.ascontiguousarray(np.broadcast_to(np.asarray(inp["ret_decay"], f32)[:DEPTH].reshape(1, DEPTH * 8), (128, DEPTH * 8)))
    sg = []
    for l in range(DEPTH):
        sg.append(np.asarray(inp["sg_ln_g"][l], f32)); sg.append(np.asarray(inp["sg_ln_b"][l], f32))
    sgln = np.ascontiguousarray(np.broadcast_to(np.concatenate(sg)[None, :], (128, DEPTH * 2 * SGW)))
    wsT = np.ascontiguousarray(np.asarray(inp["sg_w"], f32)[:DEPTH].transpose(3, 0, 1, 2)).reshape(128, DEPTH * G * 128)
    bsr = np.ascontiguousarray(np.asarray(inp["sg_b"], f32)[:DEPTH].reshape(1, DEPTH * SGW))
    half = 128
    invf = (np.float32(10000.0) ** (-np.arange(half, dtype=f32) / np.float32(half))).astype(f32).reshape(128, 1)
    return dict(wcat=wcat, gains=gains, dec=dec, sgln=sgln, wsT=wsT, bsr=bsr, invf=invf)


def prep_core(x_rows, p_rows, pos, carry_flag, DEPTH):
    f32 = np.float32
    NT = x_rows.shape[0]
    xT = np.ascontiguousarray(x_rows.T)
    pT = np.ascontiguousarray(p_rows.transpose(0, 2, 1)).reshape(DEPTH * PLE, NT)
    posb = np.ascontiguousarray(np.broadcast_to(pos.astype(f32)[None, :], (128, NT)))
    carry = np.full((128, 1), carry_flag, f32)
    return dict(xT=xT, pT=pT, posb=posb, carry=carry)


_NC_CACHE = {}


def kernel(x_prompt, x_sample, p_prompt, p_sample, norm_mix_g, w_in, ret_decay, sg_ln_g, sg_ln_b,
           sg_w, sg_b, w_out, norm_ffn_g, w_ffn_gate, w_ffn_up, w_ffn_down, norm_ple_g, w_ple_gate,
           w_ple_proj, norm_final_g):
    DEPTH = 2
    NT = 8192
    SEG = NT // 2
    f32 = np.float32
    x_prompt = np.asarray(x_prompt, f32); x_sample = np.asarray(x_sample, f32)
    p_prompt = np.asarray(p_prompt, f32); p_sample = np.asarray(p_sample, f32)
    inp = dict(norm_mix_g=norm_mix_g, w_in=w_in, ret_decay=ret_decay, sg_ln_g=sg_ln_g, sg_ln_b=sg_ln_b, sg_w=sg_w,
               sg_b=sg_b, w_out=w_out, norm_ffn_g=norm_ffn_g, w_ffn_gate=w_ffn_gate, w_ffn_up=w_ffn_up,
               w_ffn_down=w_ffn_down, norm_ple_g=norm_ple_g, w_ple_gate=w_ple_gate, w_ple_proj=w_ple_proj,
               norm_final_g=norm_final_g)
    shared = prep_shared(inp, DEPTH)
    in_maps = []
    plan = {0: ("p", 0), 1: ("s2", 0, 1), 2: ("s1", 2), 3: ("s1", 3), 4: ("p", 1), 5: ("s2", 4, 5), 6: ("s1", 6), 7: ("s1", 7)}
    pos2 = np.concatenate([np.arange(SEG), np.arange(SEG)])
    for c in range(8):
        pl = plan[c]
        if pl[0] == "p":
            xr = x_prompt[pl[1]]; pr = p_prompt[:, pl[1]]; pos = np.arange(NT); cf = 1.0
        elif pl[0] == "s2":
            xr = np.concatenate([x_sample[pl[1]], x_sample[pl[2]]], axis=0)
            pr = np.concatenate([p_sample[:, pl[1]], p_sample[:, pl[2]]], axis=1); pos = pos2; cf = 0.0
        else:
            xr = np.concatenate([x_sample[pl[1]], np.zeros((SEG, D), f32)], axis=0)
            pr = np.concatenate([p_sample[:, pl[1]], np.zeros((DEPTH, SEG, PLE), f32)], axis=1); pos = pos2; cf = 0.0
        m = dict(shared)
        m.update(prep_core(xr, pr, pos, cf, DEPTH))
        in_maps.append(m)
    if "nc" not in _NC_CACHE:
        _NC_CACHE["nc"] = build(NT, DEPTH=DEPTH)
    res = run_bass_kernel_spmd(_NC_CACHE["nc"], in_maps, core_ids=list(range(8)))
    outs = [np.asarray(r["yT"]) for r in res.results]
    y_prompt = np.empty((2, NT, D), f32)
    y_sample = np.empty((8, SEG, D), f32)
    for c in range(8):
        pl = plan[c]
        o = outs[c].T
        if pl[0] == "p":
            y_prompt[pl[1]] = o
        elif pl[0] == "s2":
            y_sample[pl[1]] = o[:SEG]; y_sample[pl[2]] = o[SEG:]
        else:
            y_sample[pl[1]] = o[:SEG]
    return (y_prompt, y_sample)
```

```python
import math
from contextlib import ExitStack

import numpy as np
import concourse.bass as bass
import concourse.mybir as mybir
from concourse.bass_utils import run_bass_kernel_spmd

F32 = mybir.dt.float32
BF16 = mybir.dt.bfloat16
U8 = mybir.dt.uint8
I32 = mybir.dt.int32
AF = mybir.ActivationFunctionType
ALU = mybir.AluOpType
DTSIZE = {F32: 4, BF16: 2, U8: 1, I32: 4}

D = 2048
KC = D // 128
RW = 1024
H = 4
DH = 256
SGW = 1024
G = 8
DFF = 5632
FC = DFF // 128
PLE = 256
INW = 6144
EPS = 1e-6
T = 512
NCH = T // 128
ARENA = 210000

WSPEC = [("w_in", D, INW, 512), ("w_out", D, D, 256), ("w_ffn_gate", D, DFF, 256), ("w_ffn_up", D, DFF, 256),
         ("w_ffn_down", DFF, D, 128), ("w_ple_gate", D, D, 256), ("w_ple_proj", PLE, D, 256)]
WOFF = {}
_o = 0
for _n, _k, _nn, _g in WSPEC:
    WOFF[_n] = (_o, _k // 128, _g, _nn // _g)
    _o += _k * _nn
WTOT = _o


class Buf:
    __slots__ = ("name", "w", "r")

    def __init__(self, name=""):
        self.name = name
        self.w = None
        self.r = {}


class Sem:
    def __init__(self, h, key):
        self.h = h
        self.key = key
        self.cnt = 0


class Eng:
    def __init__(self, name, sem):
        self.name = name
        self.sem = sem
        self.prog = []
        self.waited = {}
        self.pending = False


class Prog:
    def __init__(self, nc, stack, same_engine_sync=False):
        self.nc = nc
        self.stack = stack
        self.sems = {}
        self.nsem = 0
        self.same_engine_sync = same_engine_sync
        self.eng = {}
        for n in ("pe", "act", "dve", "pool", "sp"):
            self.eng[n] = Eng(n, self.new_sem("e_" + n))
        self.dsems = []
        self.free_dsems = []
        self.phase_dsems = []

    def new_sem(self, name):
        h = self.stack.enter_context(self.nc.semaphore(name))
        s = Sem(h, self.nsem)
        self.sems[s.key] = s
        self.nsem += 1
        return s

    def new_dsem(self, name, persistent=False):
        if not persistent and self.free_dsems:
            s = self.free_dsems.pop()
        else:
            s = self.new_sem(name)
            self.dsems.append(s)
        if not persistent:
            self.phase_dsems.append(s)
        return s

    def release_phase_dsems(self):
        self.free_dsems.extend(self.phase_dsems)
        self.phase_dsems = []

    def _wait(self, E, tok, skip_key=None):
        key, val = tok
        if key == skip_key:
            return
        if key == E.sem.key and (not self.same_engine_sync or val > E.sem.cnt or E.name in ("pe", "sp")):
            return
        if E.waited.get(key, 0) >= val:
            return
        E.waited[key] = val
        E.prog.append(("wait", self.sems[key].h, val))

    def _deps(self, E, reads, writes, skip_key=None):
        for b in reads:
            if b.w is not None:
                self._wait(E, b.w, skip_key)
        for b in writes:
            if b.w is not None:
                self._wait(E, b.w, skip_key)
            for k, v in b.r.items():
                self._wait(E, (k, v), skip_key)

    def _update(self, tok, reads, writes):
        for b in writes:
            b.w = tok
            b.r = {}
        k, v = tok
        for b in reads:
            if b.r.get(k, 0) < v:
                b.r[k] = v

    def op(self, en, emit, reads=(), writes=(), inc=True):
        E = self.eng[en]
        self._deps(E, reads, writes)
        if inc:
            E.sem.cnt += 1
            E.pending = False
            E.prog.append(("ins", emit, E.sem.h))
            tok = (E.sem.key, E.sem.cnt)
        else:
            E.pending = True
            E.prog.append(("ins", emit, None))
            tok = (E.sem.key, E.sem.cnt + 1)
        self._update(tok, reads, writes)

    def dma(self, qn, out, in_, ds, reads=(), writes=()):
        E = self.eng[qn]
        self._deps(E, reads, writes, skip_key=ds.key)
        ds.cnt += 16
        E.prog.append(("dma", out, in_, ds.h))
        self._update((ds.key, ds.cnt), reads, writes)

    def cc(self, kind, groups, in_ap, out_ap, ds, reads=(), writes=()):
        E = self.eng["pool"]
        self._deps(E, reads, writes, skip_key=ds.key)
        ds.cnt += 16
        E.prog.append(("cc", kind, groups, in_ap, out_ap, ds.h))
        self._update((ds.key, ds.cnt), reads, writes)

    def barrier(self):
        for E in self.eng.values():
            assert not E.pending, E.name
        for E in self.eng.values():
            for E2 in self.eng.values():
                if E2 is not E and E2.sem.cnt > 0:
                    self._wait(E, (E2.sem.key, E2.sem.cnt))
            for ds in self.dsems:
                if ds.cnt > 0:
                    self._wait(E, (ds.key, ds.cnt))

    def emit(self, block):
        decos = {"pe": block.tensor, "act": block.scalar, "dve": block.vector,
                 "pool": block.gpsimd, "sp": block.sync}
        for n, deco in decos.items():
            E = self.eng[n]

            def body(h, E=E):
                for it in E.prog:
                    if it[0] == "wait":
                        h.wait_ge(it[1], it[2])
                    elif it[0] == "ins":
                        ins = it[1](h)
                        if it[2] is not None:
                            ins.then_inc(it[2], 1)
                    elif it[0] == "cc":
                        h.collective_compute(it[1], ALU.bypass, replica_groups=it[2], ins=[it[3]], outs=[it[4]]).then_inc(it[5], 16)
                    else:
                        h.dma_start(out=it[1], in_=it[2]).then_inc(it[3], 16)
            deco(body)


class Arena:
    def __init__(self, ap_u8, size):
        self.ap = ap_u8
        self.size = size
        self.off = 0

    def alloc(self, free_shape, dtype, parts=128):
        n = int(np.prod(free_shape))
        nb = n * DTSIZE[dtype]
        self.off = (self.off + 63) // 64 * 64
        assert self.off + nb <= self.size, ("arena overflow", self.off, nb, self.size)
        v = self.ap[0:parts, self.off:self.off + nb]
        if dtype != U8:
            v = v.bitcast(dtype)
        self.off += nb
        if len(free_shape) == 2:
            v = v.rearrange("p (a b) -> p a b", a=free_shape[0], b=free_shape[1])
        elif len(free_shape) == 3:
            v = v.rearrange("p (a b c) -> p a b c", a=free_shape[0], b=free_shape[1], c=free_shape[2])
        return v


class Ring:
    def __init__(self, P, A, n, shape, dtype, name, dsem=True):
        self.slots = []
        for i in range(n):
            self.slots.append((A.alloc(shape, dtype), Buf(f"{name}{i}"), P.new_dsem(f"d_{name}{i}") if dsem else None))
        self.i = 0

    def next(self):
        s = self.slots[self.i % len(self.slots)]
        self.i += 1
        return s


def build(NT, DEPTH=2, stop_after=None, debug=False, same_engine_sync=True, p4_stage=None):
    SEG = NT // 2
    NTILE = NT // T
    TPS = SEG // T
    nc = bass.Bass("TRN2", target_bir_lowering=False)
    dkind = "ExternalOutput" if debug else "Internal"

    def din(name, shape, dt=F32):
        return nc.dram_tensor(name, shape, dt, kind="ExternalInput").ap()

    def dscr(name, shape, dt):
        return nc.dram_tensor(name, shape, dt, kind=dkind).ap()

    xT = din("xT", [D, NT])
    pT = din("pT", [DEPTH * PLE, NT])
    wcat = din("wcat", [DEPTH * WTOT])
    gains = din("gains", [128, (DEPTH * 3 + 1) * KC])
    dec = din("dec", [128, DEPTH * 8])
    sgln = din("sgln", [128, DEPTH * 2 * SGW])
    wsT = din("wsT", [128, DEPTH * G * 128])
    bsr = din("bsr", [1, DEPTH * SGW])
    posb = din("posb", [128, NT])
    invf = din("invf", [128, 1])
    carry = din("carry", [128, 1])
    yT = nc.dram_tensor("yT", [D, NT], F32, kind="ExternalOutput").ap()

    wbf = nc.dram_tensor("wbf", [DEPTH * WTOT], BF16, kind="Internal").ap()
    hT = dscr("hT", [D, NT], F32)
    qT_d = dscr("qT_d", [RW, NT], BF16)
    kT_d = dscr("kT_d", [RW, NT], BF16)
    ktm_d = dscr("ktm_d", [NT, RW], BF16)
    vtm_d = dscr("vtm_d", [NT, RW], BF16)
    gact_d = dscr("gact_d", [NT, RW], BF16)
    svln_d = dscr("svln_d", [NT, SGW], BF16)
    uT_d = dscr("uT_d", [SGW, NT], BF16)
    ypart_d = dscr("ypart_d", [NT, RW], F32)
    ymixT_d = dscr("ymixT_d", [D, NT], BF16)

    with ExitStack() as st:
        P = Prog(nc, st, same_engine_sync=same_engine_sync)
        arena_t = st.enter_context(nc.sbuf_tensor("arena", [128, ARENA], U8))
        A = Arena(arena_t[:], ARENA)
        banks = [st.enter_context(nc.psum_tensor(f"bank{i}", [128, 512], F32)) for i in range(8)]
        bankb = [Buf(f"bank{i}") for i in range(8)]

        def tilebufs(name):
            return [Buf(f"{name}{t}") for t in range(NTILE)]
        B_hT = tilebufs("hT")
        B_q = tilebufs("q"); B_k = tilebufs("k"); B_ktm = tilebufs("ktm"); B_v = tilebufs("v")
        B_g = tilebufs("g"); B_sv = tilebufs("sv"); B_u = tilebufs("u"); B_yp = tilebufs("yp")
        B_ymr = tilebufs("ymr"); B_yms = tilebufs("yms")
        B_w = {(l, n): Buf(f"w{l}{n}") for l in range(DEPTH) for n, *_ in WSPEC}

        CH_EL = 128 * 8192
        cast_q = []
        for l in range(DEPTH):
            for n, k_, nn_, g_ in WSPEC:
                off = l * WTOT + WOFF[n][0]
                tot = k_ * nn_
                ds = P.new_dsem(f"wc{l}{n}", persistent=True)
                o = 0
                while o < tot:
                    sz = min(CH_EL, tot - o)
                    src = wcat[off + o: off + o + sz].rearrange("(p f) -> p f", p=128)
                    dst = wbf[off + o: off + o + sz].rearrange("(p f) -> p f", p=128)
                    cast_q.append((l, n, dst, src, ds))
                    o += sz

        def issue_casts(n=None, layer=None, name=None):
            while cast_q:
                l_, n_, dst, src, ds = cast_q[0]
                if layer is not None and (l_ > layer or (name is not None and (l_, n_) != (layer, name))):
                    break
                if layer is None and n is not None and n <= 0:
                    break
                cast_q.pop(0)
                P.dma("pool", dst, src, ds, writes=[B_w[(l_, n_)]])
                if n is not None:
                    n -= 1

        issue_casts(layer=0, name="w_in")

        def wview(l, name, g):
            off, kc, gw, ng = WOFF[name]
            base = l * WTOT + off + g * 128 * kc * gw
            return wbf[base: base + 128 * kc * gw].rearrange("(p f) -> p f", p=128)

        gains_sb = A.alloc([(DEPTH * 3 + 1) * KC], F32); Bc = Buf("const")
        ld0 = P.new_dsem("ld0", persistent=True)
        ld1 = P.new_dsem("ld1", persistent=True)
        ld2 = P.new_dsem("ld2", persistent=True)
        P.dma("sp", gains_sb, gains, ld0, writes=[Bc])
        dec_sb = A.alloc([DEPTH * 8], F32)
        P.dma("sp", dec_sb, dec, ld0, writes=[Bc])
        invf_sb = A.alloc([1], F32)
        P.dma("sp", invf_sb, invf, ld0, writes=[Bc])
        carry_sb = A.alloc([1], F32)
        P.dma("sp", carry_sb, carry, ld0, writes=[Bc])
        diff = A.alloc([128], F32)
        fidx1 = A.alloc([128], F32)
        pidx = A.alloc([1], F32)
        c128mp = A.alloc([1], F32)
        c127mp = A.alloc([1], F32)
        ident = A.alloc([128], BF16)
        ones = A.alloc([128], BF16)
        tmpc = A.alloc([128], F32)
        P.op("pool", lambda e: e.iota(diff, [[1, 128]], base=0, channel_multiplier=-1, allow_small_or_imprecise_dtypes=True), writes=[Bc])
        P.op("pool", lambda e: e.iota(fidx1, [[1, 128]], base=1, channel_multiplier=0, allow_small_or_imprecise_dtypes=True), writes=[Bc])
        P.op("pool", lambda e: e.iota(pidx, [[0, 1]], base=0, channel_multiplier=1, allow_small_or_imprecise_dtypes=True), writes=[Bc])
        P.op("pool", lambda e: e.memset(ones, 1.0), writes=[Bc])
        P.op("dve", lambda e: e.tensor_scalar(out=c128mp, in0=pidx, scalar1=-1.0, scalar2=128.0, op0=ALU.mult, op1=ALU.add), reads=[Bc], writes=[Bc])
        P.op("dve", lambda e: e.tensor_scalar(out=c127mp, in0=pidx, scalar1=-1.0, scalar2=127.0, op0=ALU.mult, op1=ALU.add), reads=[Bc], writes=[Bc])
        P.op("dve", lambda e: e.tensor_scalar(out=tmpc, in0=diff, scalar1=0.0, scalar2=None, op0=ALU.is_equal), reads=[Bc], writes=[Bc])
        P.op("dve", lambda e: e.tensor_copy(out=ident, in_=tmpc), reads=[Bc], writes=[Bc])
        lg = A.alloc([8], F32)
        e1 = A.alloc([8], F32)
        maskT = A.alloc([H, 128], F32)
        qdecf = A.alloc([8, 128], BF16)
        qdecb = A.alloc([H], F32)
        kdecf = A.alloc([H], F32)
        kdecb = A.alloc([H], F32)
        cdec = A.alloc([8], F32)
        lng = A.alloc([SGW], F32)
        lnb = A.alloc([SGW], F32)
        wsT_b = A.alloc([G, 128], BF16)
        bs_b = A.alloc([SGW], BF16, parts=1)
        Bl = Buf("layerconst")
        A_base = A.off

        def layer_setup(l):
            P.dma("sp", lng, sgln[:, (2 * l) * SGW:(2 * l + 1) * SGW], ld2, writes=[Bl])
            P.dma("sp", lnb, sgln[:, (2 * l + 1) * SGW:(2 * l + 2) * SGW], ld2, writes=[Bl])
            P.dma("pool", wsT_b, wsT[:, l * G * 128:(l + 1) * G * 128].rearrange("p (g i) -> p g i", g=G), ld1, writes=[Bl])
            P.dma("pool", bs_b, bsr[:, l * SGW:(l + 1) * SGW], ld1, writes=[Bl])
            P.op("act", lambda e: e.activation(out=e1, in_=dec_sb[:, l * 8:(l + 1) * 8], func=AF.Exp, scale=-math.log(2.0)), reads=[Bc, Bl], writes=[Bl])
            P.op("dve", lambda e: e.tensor_scalar(out=e1, in0=e1, scalar1=-(2.0 ** -5), scalar2=1.0, op0=ALU.mult, op1=ALU.add), reads=[Bl], writes=[Bl])
            P.op("act", lambda e: e.activation(out=lg, in_=e1, func=AF.Ln), reads=[Bl], writes=[Bl])
            P.op("act", lambda e: e.activation(out=cdec, in_=lg, func=AF.Exp, scale=128.0), reads=[Bl], writes=[Bl])
            for h in range(H):
                P.op("dve", lambda e, h=h: e.tensor_scalar(out=tmpc, in0=diff, scalar1=0.0, scalar2=lg[:, h:h + 1], op0=ALU.max, op1=ALU.mult), reads=[Bc, Bl], writes=[Bl])
                P.op("dve", lambda e, h=h: e.tensor_scalar(out=maskT[:, h, :], in0=diff, scalar1=-1.0, scalar2=0.0, op0=ALU.mult, op1=ALU.max), reads=[Bc, Bl], writes=[Bl])
                P.op("dve", lambda e, h=h: e.scalar_tensor_tensor(out=tmpc, in0=maskT[:, h, :], scalar=lg[:, 4 + h:5 + h], in1=tmpc, op0=ALU.mult, op1=ALU.add), reads=[Bl], writes=[Bl])
                P.op("act", lambda e, h=h: e.activation(out=maskT[:, h, :], in_=tmpc, func=AF.Exp), reads=[Bl], writes=[Bl])
                for dc in range(2):
                    P.op("act", lambda e, h=h, dc=dc: e.activation(out=qdecf[:, 2 * h + dc, :], in_=fidx1, func=AF.Exp, scale=lg[:, h:h + 1]), reads=[Bc, Bl], writes=[Bl])
                P.op("act", lambda e, h=h: e.activation(out=qdecb[:, h:h + 1], in_=c128mp, func=AF.Exp, scale=lg[:, 4 + h:5 + h]), reads=[Bc, Bl], writes=[Bl])
                P.op("act", lambda e, h=h: e.activation(out=kdecf[:, h:h + 1], in_=c127mp, func=AF.Exp, scale=lg[:, h:h + 1]), reads=[Bc, Bl], writes=[Bl])
                P.op("act", lambda e, h=h: e.activation(out=kdecb[:, h:h + 1], in_=pidx, func=AF.Exp, scale=lg[:, 4 + h:5 + h]), reads=[Bc, Bl], writes=[Bl])

        def gain_ap(idx, kc):
            return gains_sb[:, idx * KC + kc: idx * KC + kc + 1]

        def phase_end():
            P.barrier()
            P.release_phase_dsems()
            A.off = A_base
            for b in bankb:
                b.w = None; b.r = {}

        def rstd_from_psum(ps_ap, out_ap, n, reads, writes):
            P.op("dve", lambda e: e.tensor_scalar(out=out_ap, in0=ps_ap, scalar1=1.0 / n, scalar2=EPS, op0=ALU.mult, op1=ALU.add), reads=reads, writes=writes)
            P.op("act", lambda e: e.activation(out=out_ap, in_=out_ap, func=AF.Sqrt), reads=writes, writes=writes)
            P.op("dve", lambda e: e.reciprocal(out=out_ap, in_=out_ap), reads=writes, writes=writes)

        for l in range(DEPTH):
            src_h = xT if l == 0 else hT
            layer_setup(l)

            if True:
                hin = Ring(P, A, 2, [4, T], F32, "hin")
                sqr = Ring(P, A, 4, [T], BF16, "sq", dsem=False)
                hgr = Ring(P, A, 2, [KC, T], BF16, "hg", dsem=False)
                rsb = Ring(P, A, 2, [T], F32, "rsb", dsem=False)
                rst = Ring(P, A, 2, [NCH], F32, "rst", dsem=False)
                tab_ap = A.alloc([4, T], F32); tab_b = Buf("tabs")
                posr = Ring(P, A, 2, [T], F32, "pos")
                wr = Ring(P, A, 3, [KC * 512], BF16, "w")
                tmps = Ring(P, A, 2, [4, T], F32, "ropet", dsem=False)
                trig = A.alloc([2, T], F32); Btrig = Buf("trig")
                trigi = A.alloc([T], I32)
                outr = Ring(P, A, 4, [4 * T], BF16, "out")
                svg = Ring(P, A, 4, [T], F32, "svg", dsem=False)
                svx = Ring(P, A, 4, [T], F32, "svx", dsem=False)
                ut = Ring(P, A, 2, [T], F32, "ut", dsem=False)
                lnst = Ring(P, A, 2, [6, 16], F32, "lnst", dsem=False)
                pacc = [2, 3, 4, 5]
                pacc_i = [0]
                ptr = [6, 7]
                ptr_i = [0]

                def next_acc():
                    b = pacc[pacc_i[0] % 4]
                    pacc_i[0] += 1
                    return b

                norm_state = {}

                def p1_begin(t):
                    pap, pb, pds = posr.next()
                    P.dma("sp", pap, posb[:, t * T:(t + 1) * T], pds, writes=[pb])
                    norm_state[t] = dict(hin={}, pos=(pap, pb), hg=hgr.next(), rsb=rsb.next(), rst=rst.next(), sq={})

                def p1_load(t, part):
                    t0 = t * T
                    ap, b, ds = hin.next()
                    kc0 = part * 4
                    P.dma("sp", ap, src_h[kc0 * 128:(kc0 + 4) * 128, t0:t0 + T].rearrange("(k p) s -> p k s", p=128),
                          ds, reads=[B_hT[t]] if l > 0 else [], writes=[b])
                    norm_state[t]["hin"][part] = (ap, b)

                def p1_norm_elem(t, part):
                    s = norm_state[t]
                    hg_ap, hg_b, _ = s["hg"]
                    hap, hb = s["hin"][part]
                    for j in range(4):
                        kc = part * 4 + j
                        sq_ap, sq_b, _ = sqr.next()
                        s["sq"][kc] = (sq_ap, sq_b)
                        P.op("act", lambda e, hap=hap, j=j, sq_ap=sq_ap: e.activation(out=sq_ap, in_=hap[:, j, :], func=AF.Square), reads=[hb], writes=[sq_b])
                        ga = gain_ap(l * 3 + 0, kc)
                        P.op("act", lambda e, hap=hap, j=j, kc=kc, ga=ga: e.activation(out=hg_ap[:, kc, :], in_=hap[:, j, :], func=AF.Copy, scale=ga), reads=[hb, Bc], writes=[hg_b])

                def p1_norm_pe(t, part):
                    s = norm_state[t]
                    for j in range(4):
                        kc = part * 4 + j
                        sq_ap, sq_b = s["sq"][kc]
                        P.op("pe", lambda e, sq_ap=sq_ap, kc=kc: e.matmul(banks[0][:, :], lhsT=ones, rhs=sq_ap, start=(kc == 0), stop=(kc == KC - 1)), reads=[sq_b, Bc], writes=[bankb[0]], inc=False)
                        for c in range(NCH):
                            P.op("pe", lambda e, sq_ap=sq_ap, kc=kc, c=c: e.matmul(banks[1][:, c:c + 1], lhsT=sq_ap[:, c * 128:(c + 1) * 128], rhs=ones[:, 0:1], start=(kc == 0 and c == 0), stop=(kc == KC - 1 and c == NCH - 1), skip_group_check=True), reads=[sq_b, Bc], writes=[bankb[1]], inc=(c == NCH - 1))

                def p1_norm_fin(t):
                    s = norm_state[t]
                    rsb_ap, rsb_b, _ = s["rsb"]
                    rst_ap, rst_b, _ = s["rst"]
                    pap, pb = s["pos"]
                    rstd_from_psum(banks[0][:, :], rsb_ap, D, [bankb[0]], [rsb_b])
                    rstd_from_psum(banks[1][:, 0:NCH], rst_ap, D, [bankb[1]], [rst_b])
                    u = trig[:, 0, :]
                    f = trig[:, 1, :]
                    for j, sh in ((0, 0.25), (1, 0.0)):
                        P.op("dve", lambda e: e.tensor_scalar(out=u, in0=pap, scalar1=invf_sb[:, 0:1], scalar2=1.0 / (2 * math.pi), op0=ALU.mult, op1=ALU.mult), reads=[pb, Bc], writes=[Btrig])
                        if sh:
                            P.op("dve", lambda e, sh=sh: e.tensor_scalar(out=u, in0=u, scalar1=sh, scalar2=None, op0=ALU.add), reads=[Btrig], writes=[Btrig])
                        P.op("dve", lambda e: e.tensor_copy(out=trigi, in_=u), reads=[Btrig], writes=[Btrig])
                        P.op("dve", lambda e: e.tensor_copy(out=f, in_=trigi), reads=[Btrig], writes=[Btrig])
                        P.op("dve", lambda e: e.tensor_sub(out=u, in0=u, in1=f), reads=[Btrig], writes=[Btrig])
                        P.op("dve", lambda e: e.tensor_scalar(out=f, in0=u, scalar1=0.5, scalar2=None, op0=ALU.is_gt), reads=[Btrig], writes=[Btrig])
                        P.op("dve", lambda e: e.tensor_sub(out=u, in0=u, in1=f), reads=[Btrig], writes=[Btrig])
                        P.op("act", lambda e: e.activation(out=u, in_=u, func=AF.Sin, scale=2 * math.pi), reads=[Btrig], writes=[Btrig])
                        P.op("dve", lambda e, j=j: e.tensor_mul(out=tab_ap[:, j, :], in0=u, in1=rsb_ap), reads=[Btrig, rsb_b], writes=[tab_b])
                        P.op("pool", lambda e, j=j: e.tensor_scalar(out=tab_ap[:, 2 + j, :], in0=tab_ap[:, j, :], scalar1=DH ** -0.5, scalar2=None, op0=ALU.mult), reads=[tab_b], writes=[tab_b])

                wq = []
                GORDER = [0, 4, 1, 5, 2, 6, 3, 7, 10, 8, 11, 9]

                def p1_wload(g):
                    ap, b, ds = wr.next()
                    P.dma("sp", ap, wview(l, "w_in", g), ds, reads=[B_w[(l, "w_in")]], writes=[b])
                    wq.append((ap.rearrange("p (k g) -> p k g", k=KC), b))

                def fm_view(o_ap):
                    return o_ap.rearrange("p (b s) -> p b s", b=4)

                def tm_view(o_ap):
                    return o_ap.rearrange("p (c f) -> p c f", c=NCH)

                def p1_proj(t):
                    s = norm_state[t]
                    t0 = t * T
                    hg_ap, hg_b, _ = s["hg"]
                    rsb_ap, rsb_b, _ = s["rsb"]
                    rst_ap, rst_b, _ = s["rst"]
                    nx = t + 1 < NTILE
                    for gi in range(12):
                        g = GORDER[gi]
                        if nx and 4 <= gi < 8:
                            p1_norm_elem(t + 1, gi - 4)
                        if nx and gi == 1:
                            p1_begin(t + 1)
                        if gi == 0 and t == 0:
                            p1_wload(GORDER[0]); p1_wload(GORDER[1])
                        nxt = t * 12 + gi + 2
                        if nxt < NTILE * 12:
                            p1_wload(GORDER[nxt % 12])
                        if nx and 2 <= gi < 6:
                            p1_load(t + 1, gi - 2)
                        if nx and gi == 10:
                            p1_norm_fin(t + 1)
                        w_ap, w_b = wq.pop(0)
                        o_ap, o_b, o_ds = outr.next()
                        col0 = (g % 2) * 512
                        if g < 4 or 8 <= g < 10:
                            of = fm_view(o_ap)
                            accs = []
                            for blk in range(4):
                                bk = next_acc()
                                for kc in range(KC):
                                    P.op("pe", lambda e, bk=bk, blk=blk, kc=kc, w_ap=w_ap: e.matmul(banks[bk][:, :], lhsT=w_ap[:, kc, blk * 128:(blk + 1) * 128], rhs=hg_ap[:, kc, :], start=(kc == 0), stop=(kc == KC - 1)),
                                         reads=[w_b, hg_b], writes=[bankb[bk]], inc=(kc == KC - 1))
                                accs.append(bk)
                                if g >= 8:
                                    ut_ap, ut_b, _ = ut.next()
                                    P.op("dve", lambda e, bk=bk, ut_ap=ut_ap: e.tensor_mul(out=ut_ap, in0=banks[bk][:, :], in1=rsb_ap), reads=[bankb[bk], rsb_b], writes=[ut_b])
                                    P.op("act", lambda e, ut_ap=ut_ap, blk=blk, of=of: e.activation(out=of[:, blk, :], in_=ut_ap, func=AF.Gelu_apprx_tanh), reads=[ut_b], writes=[o_b])
                            if g < 4:
                                ci, si = (2, 3) if g >= 2 else (0, 1)
                                for pr in range(2):
                                    b0, b1 = accs[2 * pr], accs[2 * pr + 1]
                                    tm_ap, tm_b, _ = tmps.next()
                                    P.op("dve", lambda e, b0=b0, tm_ap=tm_ap, ci=ci: e.tensor_mul(out=tm_ap[:, 0, :], in0=banks[b0][:, :], in1=tab_ap[:, ci, :]), reads=[bankb[b0], tab_b], writes=[tm_b])
                                    P.op("dve", lambda e, b1=b1, tm_ap=tm_ap, si=si: e.tensor_mul(out=tm_ap[:, 1, :], in0=banks[b1][:, :], in1=tab_ap[:, si, :]), reads=[bankb[b1], tab_b], writes=[tm_b])
                                    P.op("dve", lambda e, b0=b0, tm_ap=tm_ap, si=si: e.tensor_mul(out=tm_ap[:, 2, :], in0=banks[b0][:, :], in1=tab_ap[:, si, :]), reads=[bankb[b0], tab_b], writes=[tm_b])
                                    P.op("dve", lambda e, b1=b1, tm_ap=tm_ap, ci=ci: e.tensor_mul(out=tm_ap[:, 3, :], in0=banks[b1][:, :], in1=tab_ap[:, ci, :]), reads=[bankb[b1], tab_b], writes=[tm_b])
                                    P.op("pool", lambda e, tm_ap=tm_ap, pr=pr, of=of: e.tensor_sub(out=of[:, 2 * pr, :], in0=tm_ap[:, 0, :], in1=tm_ap[:, 1, :]), reads=[tm_b], writes=[o_b])
                                    P.op("pool", lambda e, tm_ap=tm_ap, pr=pr, of=of: e.tensor_add(out=of[:, 2 * pr + 1, :], in0=tm_ap[:, 2, :], in1=tm_ap[:, 3, :]), reads=[tm_b], writes=[o_b])
                            dd, db = (qT_d, B_q) if g < 2 else ((kT_d, B_k) if g < 4 else (uT_d, B_u))
                            P.dma("pool", dd[col0:col0 + 512, t0:t0 + T].rearrange("(b p) s -> p b s", p=128), of, o_ds, reads=[o_b], writes=[db[t]])
                            if 2 <= g < 4:
                                o2_ap, o2_b, o2_ds = outr.next()
                                o2 = tm_view(o2_ap)
                                bk = ptr[ptr_i[0] % 2]; ptr_i[0] += 1
                                pv = banks[bk][:, :].bitcast(BF16)
                                for half in range(2):
                                    for cc in range(2):
                                        c = half * 2 + cc
                                        for blk in range(4):
                                            sl = (cc * 4 + blk) * 128
                                            P.op("pe", lambda e, sl=sl, blk=blk, c=c, of=of, pv=pv: e.transpose(out=pv[:, sl:sl + 128], in_=of[:, blk, c * 128:(c + 1) * 128], identity=ident),
                                                 reads=[o_b, Bc], writes=[bankb[bk]], inc=(blk == 3 and cc == 1))
                                    P.op("act", lambda e, half=half, o2=o2, pv=pv: e.activation(out=o2[:, half * 2:half * 2 + 2, :], in_=pv.rearrange("p (c f) -> p c f", c=2), func=AF.Copy), reads=[bankb[bk]], writes=[o2_b])
                                P.dma("pool", ktm_d[t0:t0 + T, col0:col0 + 512].rearrange("(c p) f -> p c f", p=128), o2, o2_ds, reads=[o2_b], writes=[B_ktm[t]])
                        else:
                            ot = tm_view(o_ap)
                            for c in range(NCH):
                                bk = next_acc()
                                for kc in range(KC):
                                    P.op("pe", lambda e, bk=bk, kc=kc, c=c, w_ap=w_ap: e.matmul(banks[bk][:, :], lhsT=hg_ap[:, kc, c * 128:(c + 1) * 128], rhs=w_ap[:, kc, :], start=(kc == 0), stop=(kc == KC - 1)),
                                         reads=[w_b, hg_b], writes=[bankb[bk]], inc=(kc == KC - 1))
                                if g < 6:
                                    P.op("act", lambda e, bk=bk, c=c, ot=ot: e.activation(out=ot[:, c, :], in_=banks[bk][:, :], func=AF.Copy, scale=rst_ap[:, c:c + 1]), reads=[bankb[bk], rst_b], writes=[o_b])
                                elif g < 8:
                                    P.op("act", lambda e, bk=bk, c=c, ot=ot: e.activation(out=ot[:, c, :], in_=banks[bk][:, :], func=AF.Silu, scale=rst_ap[:, c:c + 1]), reads=[bankb[bk], rst_b], writes=[o_b])
                                else:
                                    if c == 0:
                                        st_ap, st_b, _ = lnst.next()
                                        sv_slots = []
                                    sg_ap, sg_b, _ = svg.next()
                                    sx_ap, sx_b, _ = svx.next()
                                    sv_slots.append((sg_ap, sg_b, sx_ap, sx_b))
                                    for g4 in range(4):
                                        P.op("act", lambda e, bk=bk, c=c, g4=g4, sg_ap=sg_ap, st_ap=st_ap: e.activation(out=sg_ap[:, g4 * 128:(g4 + 1) * 128], in_=banks[bk][:, g4 * 128:(g4 + 1) * 128], func=AF.Gelu_apprx_tanh, scale=rst_ap[:, c:c + 1], accum_out=st_ap[:, 0, c * 4 + g4:c * 4 + g4 + 1]), reads=[bankb[bk], rst_b], writes=[sg_b, st_b])
                                    for g4 in range(4):
                                        P.op("act", lambda e, c=c, g4=g4, sg_ap=sg_ap, sx_ap=sx_ap, st_ap=st_ap: e.activation(out=sx_ap[:, g4 * 128:(g4 + 1) * 128], in_=sg_ap[:, g4 * 128:(g4 + 1) * 128], func=AF.Square, accum_out=st_ap[:, 1, c * 4 + g4:c * 4 + g4 + 1]), reads=[sg_b], writes=[sx_b, st_b])
                            if g >= 10:
                                P.op("dve", lambda e, st_ap=st_ap: e.tensor_scalar(out=st_ap[:, 2, :], in0=st_ap[:, 0, :], scalar1=1.0 / 128, scalar2=None, op0=ALU.mult), reads=[st_b], writes=[st_b])
                                P.op("dve", lambda e, st_ap=st_ap: e.tensor_mul(out=st_ap[:, 3, :], in0=st_ap[:, 2, :], in1=st_ap[:, 2, :]), reads=[st_b], writes=[st_b])
                                P.op("dve", lambda e, st_ap=st_ap: e.scalar_tensor_tensor(out=st_ap[:, 3, :], in0=st_ap[:, 1, :], scalar=1.0 / 128, in1=st_ap[:, 3, :], op0=ALU.mult, op1=ALU.subtract), reads=[st_b], writes=[st_b])
                                P.op("dve", lambda e, st_ap=st_ap: e.tensor_scalar(out=st_ap[:, 4, :], in0=st_ap[:, 3, :], scalar1=EPS, scalar2=None, op0=ALU.add), reads=[st_b], writes=[st_b])
                                P.op("act", lambda e, st_ap=st_ap: e.activation(out=st_ap[:, 4, :], in_=st_ap[:, 4, :], func=AF.Sqrt), reads=[st_b], writes=[st_b])
                                P.op("dve", lambda e, st_ap=st_ap: e.reciprocal(out=st_ap[:, 4, :], in_=st_ap[:, 4, :]), reads=[st_b], writes=[st_b])
                                P.op("dve", lambda e, st_ap=st_ap: e.scalar_tensor_tensor(out=st_ap[:, 5, :], in0=st_ap[:, 2, :], scalar=-1.0, in1=st_ap[:, 4, :], op0=ALU.mult, op1=ALU.mult), reads=[st_b], writes=[st_b])
                                for c in range(NCH):
                                    sg_ap, sg_b, sx_ap, sx_b = sv_slots[c]
                                    for g4 in range(4):
                                        P.op("dve", lambda e, c=c, g4=g4, sg_ap=sg_ap, sx_ap=sx_ap, st_ap=st_ap: e.tensor_scalar(out=sx_ap[:, g4 * 128:(g4 + 1) * 128], in0=sg_ap[:, g4 * 128:(g4 + 1) * 128], scalar1=st_ap[:, 4, c * 4 + g4:c * 4 + g4 + 1], scalar2=st_ap[:, 5, c * 4 + g4:c * 4 + g4 + 1], op0=ALU.mult, op1=ALU.add), reads=[sg_b, st_b], writes=[sx_b])
                                    P.op("pool", lambda e, sx_ap=sx_ap, col0=col0: e.tensor_mul(out=sx_ap, in0=sx_ap, in1=lng[:, col0:col0 + 512]), reads=[sx_b, Bl], writes=[sx_b])
                                    P.op("pool", lambda e, sx_ap=sx_ap, col0=col0, c=c, ot=ot: e.tensor_add(out=ot[:, c, :], in0=sx_ap, in1=lnb[:, col0:col0 + 512]), reads=[sx_b, Bl], writes=[o_b])
                            dd, db = (vtm_d, B_v) if g < 6 else ((gact_d, B_g) if g < 8 else (svln_d, B_sv))
                            P.dma("pool", dd[t0:t0 + T, col0:col0 + 512].rearrange("(c p) f -> p c f", p=128), ot, o_ds, reads=[o_b], writes=[db[t]])
                        if nx and 4 <= gi < 8:
                            p1_norm_pe(t + 1, gi - 4)

                p1_begin(0)
                p1_load(0, 0); p1_load(0, 1)
                p1_norm_elem(0, 0); p1_norm_pe(0, 0)
                p1_load(0, 2)
                p1_norm_elem(0, 1); p1_norm_pe(0, 1)
                p1_load(0, 3)
                p1_norm_elem(0, 2); p1_norm_pe(0, 2)
                p1_norm_elem(0, 3); p1_norm_pe(0, 3)
                p1_norm_fin(0)
                for t in range(NTILE):
                    if l == 0:
                        issue_casts(n=1)
                    p1_proj(t)
                phase_end()
            if stop_after == (l, 1):
                break

            if True:
                qr = Ring(P, A, 2, [8, T], BF16, "q2")
                kr = Ring(P, A, 2, [8, T], BF16, "k2")
                ktr = Ring(P, A, 2, [NCH, RW], BF16, "kt2")
                vr = Ring(P, A, 2, [NCH, RW], BF16, "v2")
                svr = Ring(P, A, 2, [NCH, SGW], BF16, "sv2")
                ur = Ring(P, A, 2, [8, T], BF16, "u2")
                Sf = A.alloc([8, DH], F32)
                Sfb2 = [A.alloc([8, DH], BF16), A.alloc([8, DH], BF16)]
                B_S = [Buf(f"S{h}") for h in range(H)]
                B_Sb2 = [[Buf(f"Sb{i}{h}") for h in range(H)] for i in range(2)]
                par = [0]
                ptr_ = Ring(P, A, 2, [H, 128], BF16, "PT", dsem=False)
                qfr = Ring(P, A, 2, [8, 128], BF16, "qf", dsem=False)
                kfr = Ring(P, A, 2, [RW], BF16, "kf", dsem=False)
                yp_sb = A.alloc([NCH, RW], F32); Byp = Buf("yp_sb"); dyp = P.new_dsem("dyp")
                ysg_sb = A.alloc([8, T], BF16); Bysg = Buf("ysg_sb"); dysg = P.new_dsem("dysg")
                BK_SC, BK_Y, BK_S, BK_Z = 0, (1, 2), (3, 4), (5, 6)
                maskv = maskT.rearrange("p h i -> p (h i)")

                def p2_load(t):
                    t0 = t * T
                    q_ap, q_b, q_ds = qr.next()
                    P.dma("sp", q_ap, qT_d[:, t0:t0 + T].rearrange("(b p) s -> p b s", p=128), q_ds, reads=[B_q[t]], writes=[q_b])
                    k_ap, k_b, k_ds = kr.next()
                    P.dma("sp", k_ap, kT_d[:, t0:t0 + T].rearrange("(b p) s -> p b s", p=128), k_ds, reads=[B_k[t]], writes=[k_b])
                    kt_ap, kt_b, kt_ds = ktr.next()
                    P.dma("sp", kt_ap, ktm_d[t0:t0 + T, :].rearrange("(c p) f -> p c f", p=128), kt_ds, reads=[B_ktm[t]], writes=[kt_b])
                    v_ap, v_b, v_ds = vr.next()
                    P.dma("sp", v_ap, vtm_d[t0:t0 + T, :].rearrange("(c p) f -> p c f", p=128), v_ds, reads=[B_v[t]], writes=[v_b])
                    sv_ap, sv_b, sv_ds = svr.next()
                    P.dma("sp", sv_ap, svln_d[t0:t0 + T, :].rearrange("(c p) f -> p c f", p=128), sv_ds, reads=[B_sv[t]], writes=[sv_b])
                    u_ap, u_b, u_ds = ur.next()
                    P.dma("sp", u_ap, uT_d[:, t0:t0 + T].rearrange("(b p) s -> p b s", p=128), u_ds, reads=[B_u[t]], writes=[u_b])
                    return (q_ap, q_b, k_ap, k_b, kt_ap, kt_b, v_ap, v_b, sv_ap, sv_b, u_ap, u_b)

                def state_update(Sx, Sxb, BS, BSb, kdec_ap, cd0, kt_ap, kt_b, v_ap, v_b, c, kfr):
                    kf_ap, kf_b, _ = kfr.next()
                    for h in range(H):
                        P.op("act", lambda e, h=h: e.activation(out=kf_ap[:, h * DH:(h + 1) * DH], in_=kt_ap[:, c, h * DH:(h + 1) * DH], func=AF.Copy, scale=kdec_ap[:, h:h + 1]), reads=[kt_b, Bl], writes=[kf_b])
                    for h in range(H):
                        bk = BK_S[h % 2]
                        for dc in range(2):
                            P.op("pe", lambda e, h=h, dc=dc, bk=bk: e.matmul(banks[bk][:, dc * DH:(dc + 1) * DH], lhsT=kf_ap[:, h * DH + dc * 128: h * DH + (dc + 1) * 128], rhs=v_ap[:, c, h * DH:(h + 1) * DH], start=True, stop=True),
                                 reads=[kf_b, v_b], writes=[bankb[bk]], inc=(dc == 1))
                        Sh = Sx[:, 2 * h:2 * h + 2, :].rearrange("p a b -> p (a b)")
                        Shb = Sxb[:, 2 * h:2 * h + 2, :].rearrange("p a b -> p (a b)")
                        P.op("dve", lambda e, h=h, bk=bk, Sh=Sh: e.scalar_tensor_tensor(out=Sh, in0=Sh, scalar=cdec[:, cd0 + h:cd0 + h + 1], in1=banks[bk][:, :], op0=ALU.mult, op1=ALU.add), reads=[bankb[bk], Bl], writes=[BS[h]])
                        P.op("act", lambda e, Sh=Sh, Shb=Shb: e.activation(out=Shb, in_=Sh, func=AF.Copy), reads=[BS[h]], writes=[BSb[h]])

                def state_reset(Sx, Sxb, BS, BSb, zero):
                    Sa = Sx.rearrange("p a b -> p (a b)")
                    Sab = Sxb.rearrange("p a b -> p (a b)")
                    if zero:
                        P.op("pool", lambda e: e.memset(Sa, 0.0), writes=BS)
                    else:
                        P.op("dve", lambda e: e.tensor_scalar(out=Sa, in0=Sa, scalar1=carry_sb[:, 0:1], scalar2=None, op0=ALU.mult), reads=[Bc], writes=BS)
                    P.op("pool", lambda e: e.tensor_copy(out=Sab, in_=Sa), reads=BS, writes=BSb)

                def p2_chunk(t, c, bufs):
                    (q_ap, q_b, k_ap, k_b, kt_ap, kt_b, v_ap, v_b, sv_ap, sv_b, u_ap, u_b) = bufs
                    gc = t * NCH + c
                    cs = slice(c * 128, (c + 1) * 128)
                    if gc == 0:
                        state_reset(Sf, Sfb2[par[0]], B_S, B_Sb2[par[0]], True)
                    elif gc * 128 == SEG:
                        par[0] ^= 1
                        state_reset(Sf, Sfb2[par[0]], B_S, B_Sb2[par[0]], False)
                    Sfb, B_Sb = Sfb2[par[0]], B_Sb2[par[0]]
                    par[0] ^= 1
                    Sfb_n, B_Sb_n = Sfb2[par[0]], B_Sb2[par[0]]
                    for h in range(H):
                        for dc in range(2):
                            P.op("pe", lambda e, h=h, dc=dc: e.matmul(banks[BK_SC][:, h * 128:(h + 1) * 128], lhsT=k_ap[:, 2 * h + dc, cs], rhs=q_ap[:, 2 * h + dc, cs], start=(dc == 0), stop=(dc == 1)),
                                 reads=[k_b, q_b], writes=[bankb[BK_SC]], inc=(h == H - 1 and dc == 1))
                    pt_ap, pt_b, _ = ptr_.next()
                    P.op("dve", lambda e: e.tensor_tensor(out=pt_ap.rearrange("p h i -> p (h i)"), in0=banks[BK_SC][:, :], in1=maskv, op=ALU.mult), reads=[bankb[BK_SC], Bl], writes=[pt_b])
                    qf_ap, qf_b, _ = qfr.next()
                    P.op("pool", lambda e: e.tensor_tensor(out=qf_ap, in0=q_ap[:, :, cs], in1=qdecf, op=ALU.mult), reads=[q_b, Bl], writes=[qf_b])
                    state_update(Sf, Sfb_n, B_S, B_Sb_n, kdecf, 0, kt_ap, kt_b, v_ap, v_b, c, kfr)
                    for g in range(G):
                        bk = BK_Z[g // 4]
                        o = banks[bk][:, (g % 4) * 128:(g % 4 + 1) * 128]
                        P.op("pe", lambda e, g=g, o=o: e.matmul(o, lhsT=sv_ap[:, c, g * 128:(g + 1) * 128], rhs=wsT_b[:, g, :], start=True, stop=False), reads=[sv_b, Bl], writes=[bankb[bk]], inc=False)
                        P.op("pe", lambda e, g=g, o=o: e.matmul(o, lhsT=ones[0:1, :], rhs=bs_b[0:1, g * 128:(g + 1) * 128], start=False, stop=True), reads=[Bc, Bl], writes=[bankb[bk]], inc=(g % 4 == 3))
                    for zb in range(2):
                        bk = BK_Z[zb]
                        P.op("dve", lambda e, zb=zb, bk=bk: e.tensor_tensor(out=ysg_sb[:, zb * 4:(zb + 1) * 4, cs], in0=banks[bk][:, :].rearrange("p (g i) -> p g i", g=4), in1=u_ap[:, zb * 4:(zb + 1) * 4, cs], op=ALU.mult), reads=[bankb[bk], u_b], writes=[Bysg])
                    for h in range(H):
                        bk = BK_Y[h // 2]
                        o = banks[bk][:, (h % 2) * DH:(h % 2 + 1) * DH]
                        P.op("pe", lambda e, h=h, o=o: e.matmul(o, lhsT=pt_ap[:, h, :], rhs=v_ap[:, c, h * DH:(h + 1) * DH], start=True, stop=False), reads=[pt_b, v_b], writes=[bankb[bk]], inc=False)
                        P.op("pe", lambda e, h=h, o=o: e.matmul(o, lhsT=qf_ap[:, 2 * h, :], rhs=Sfb[:, 2 * h, :], start=False, stop=False), reads=[qf_b, B_Sb[h]], writes=[bankb[bk]], inc=False)
                        P.op("pe", lambda e, h=h, o=o: e.matmul(o, lhsT=qf_ap[:, 2 * h + 1, :], rhs=Sfb[:, 2 * h + 1, :], start=False, stop=True), reads=[qf_b, B_Sb[h]], writes=[bankb[bk]], inc=True)
                    for half in range(2):
                        bk = BK_Y[half]
                        P.op("act", lambda e, half=half, bk=bk: e.activation(out=yp_sb[:, c, half * 512:(half + 1) * 512], in_=banks[bk][:, :], func=AF.Copy), reads=[bankb[bk]], writes=[Byp])

                nxt_bufs = p2_load(0)
                for t in range(NTILE):
                    cur = nxt_bufs
                    if t + 1 < NTILE:
                        nxt_bufs = p2_load(t + 1)
                    if l == 0:
                        issue_casts(n=1)
                    for c in range(NCH):
                        p2_chunk(t, c, cur)
                    t0 = t * T
                    P.dma("pool", ypart_d[t0:t0 + T, :].rearrange("(c p) f -> p c f", p=128), yp_sb, dyp, reads=[Byp], writes=[B_yp[t]])
                    P.dma("pool", ymixT_d[RW:2 * RW, t0:t0 + T].rearrange("(b p) s -> p b s", p=128), ysg_sb, dysg, reads=[Bysg], writes=[B_yms[t]])
                phase_end()
            if stop_after == (l, 2):
                break

            if True:
                qr = Ring(P, A, 2, [8, T], BF16, "q3")
                ktr = Ring(P, A, 2, [NCH, RW], BF16, "kt3")
                vr = Ring(P, A, 2, [NCH, RW], BF16, "v3")
                ypr = Ring(P, A, 2, [NCH, RW], F32, "yp3")
                gr = Ring(P, A, 2, [NCH, RW], BF16, "g3")
                Sb_ = A.alloc([8, DH], F32)
                Sbb2 = [A.alloc([8, DH], BF16), A.alloc([8, DH], BF16)]
                B_S = [Buf(f"S{h}") for h in range(H)]
                B_Sb2 = [[Buf(f"Sb{i}{h}") for h in range(H)] for i in range(2)]
                par = [0]
                kfr = Ring(P, A, 2, [RW], BF16, "kb", dsem=False)
                yr_ = Ring(P, A, 3, [RW], F32, "y3", dsem=False)
                ynr = Ring(P, A, 2, [RW], BF16, "yn3", dsem=False)
                junk = A.alloc([DH], F32); Bjunk = Buf("junk")
                ssr = Ring(P, A, 2, [H], F32, "ss3", dsem=False)
                yret_sb = A.alloc([8, T], BF16); Byret = Buf("yret_sb"); dyret = P.new_dsem("dyret")
                BK_Y, BK_S, BK_T = (1, 2), (3, 4), (5, 6)
                tr_i = [0]

                def p3_load(t):
                    t0 = t * T
                    q_ap, q_b, q_ds = qr.next()
                    P.dma("sp", q_ap, qT_d[:, t0:t0 + T].rearrange("(b p) s -> p b s", p=128), q_ds, reads=[B_q[t]], writes=[q_b])
                    kt_ap, kt_b, kt_ds = ktr.next()
                    P.dma("sp", kt_ap, ktm_d[t0:t0 + T, :].rearrange("(c p) f -> p c f", p=128), kt_ds, reads=[B_ktm[t]], writes=[kt_b])
                    v_ap, v_b, v_ds = vr.next()
                    P.dma("sp", v_ap, vtm_d[t0:t0 + T, :].rearrange("(c p) f -> p c f", p=128), v_ds, reads=[B_v[t]], writes=[v_b])
                    yp_ap, yp_b, yp_ds = ypr.next()
                    P.dma("sp", yp_ap, ypart_d[t0:t0 + T, :].rearrange("(c p) f -> p c f", p=128), yp_ds, reads=[B_yp[t]], writes=[yp_b])
                    g_ap, g_b, g_ds = gr.next()
                    P.dma("sp", g_ap, gact_d[t0:t0 + T, :].rearrange("(c p) f -> p c f", p=128), g_ds, reads=[B_g[t]], writes=[g_b])
                    return (q_ap, q_b, kt_ap, kt_b, v_ap, v_b, yp_ap, yp_b, g_ap, g_b)

                def p3_chunk(t, c, bufs):
                    (q_ap, q_b, kt_ap, kt_b, v_ap, v_b, yp_ap, yp_b, g_ap, g_b) = bufs
                    gc = t * NCH + c
                    cs = slice(c * 128, (c + 1) * 128)
                    if gc == NTILE * NCH - 1:
                        state_reset(Sb_, Sbb2[par[0]], B_S, B_Sb2[par[0]], True)
                    elif (gc + 1) * 128 == SEG:
                        par[0] ^= 1
                        state_reset(Sb_, Sbb2[par[0]], B_S, B_Sb2[par[0]], False)
                    Sbb, B_Sb = Sbb2[par[0]], B_Sb2[par[0]]
                    par[0] ^= 1
                    state_update(Sb_, Sbb2[par[0]], B_S, B_Sb2[par[0]], kdecb, 4, kt_ap, kt_b, v_ap, v_b, c, kfr)
                    y_ap, y_b, _ = yr_.next()
                    for h in range(H):
                        bk = BK_Y[h // 2]
                        o = banks[bk][:, (h % 2) * DH:(h % 2 + 1) * DH]
                        for dc in range(2):
                            P.op("pe", lambda e, h=h, dc=dc, o=o: e.matmul(o, lhsT=q_ap[:, 2 * h + dc, cs], rhs=Sbb[:, 2 * h + dc, :], start=(dc == 0), stop=(dc == 1)), reads=[q_b, B_Sb[h]], writes=[bankb[bk]], inc=(dc == 1))
                        P.op("dve", lambda e, h=h, o=o: e.scalar_tensor_tensor(out=y_ap[:, h * DH:(h + 1) * DH], in0=o, scalar=qdecb[:, h:h + 1], in1=yp_ap[:, c, h * DH:(h + 1) * DH], op0=ALU.mult, op1=ALU.add), reads=[bankb[bk], yp_b, Bl], writes=[y_b])
                    return (y_ap, y_b, g_ap, g_b, c, cs)

                def p3_tail(st_):
                    (y_ap, y_b, g_ap, g_b, c, cs) = st_
                    ss_ap, ss_b, _ = ssr.next()
                    for h in range(H):
                        P.op("act", lambda e, h=h: e.activation(out=junk, in_=y_ap[:, h * DH:(h + 1) * DH], func=AF.Square, accum_out=ss_ap[:, h:h + 1]), reads=[y_b], writes=[Bjunk, ss_b])
                    rstd_from_psum(ss_ap, ss_ap, DH, [ss_b], [ss_b])
                    yn_ap, yn_b, _ = ynr.next()
                    for h in range(H):
                        P.op("dve", lambda e, h=h: e.scalar_tensor_tensor(out=yn_ap[:, h * DH:(h + 1) * DH], in0=y_ap[:, h * DH:(h + 1) * DH], scalar=ss_ap[:, h:h + 1], in1=g_ap[:, c, h * DH:(h + 1) * DH], op0=ALU.mult, op1=ALU.mult), reads=[y_b, ss_b, g_b], writes=[yn_b])
                    bk = BK_T[tr_i[0] % 2]; tr_i[0] += 1
                    pv = banks[bk][:, :].bitcast(BF16)
                    for blk in range(8):
                        P.op("pe", lambda e, blk=blk, pv=pv: e.transpose(out=pv[:, blk * 128:(blk + 1) * 128], in_=yn_ap[:, blk * 128:(blk + 1) * 128], identity=ident), reads=[yn_b, Bc], writes=[bankb[bk]], inc=(blk == 7))
                    P.op("act", lambda e, pv=pv: e.activation(out=yret_sb[:, :, cs], in_=pv.rearrange("p (b i) -> p b i", b=8), func=AF.Copy), reads=[bankb[bk]], writes=[Byret])

                nxt_bufs = p3_load(NTILE - 1)
                pend_tail = []

                def flush_tail(keep):
                    while len(pend_tail) > keep:
                        st_, t_, c_ = pend_tail.pop(0)
                        p3_tail(st_)
                        if c_ == 0:
                            t0 = t_ * T
                            P.dma("pool", ymixT_d[0:RW, t0:t0 + T].rearrange("(b p) s -> p b s", p=128), yret_sb, dyret, reads=[Byret], writes=[B_ymr[t_]])

                for t in range(NTILE - 1, -1, -1):
                    cur = nxt_bufs
                    if l == 0:
                        issue_casts(n=1)
                    for c in range(NCH - 1, -1, -1):
                        st_ = p3_chunk(t, c, cur)
                        flush_tail(0)
                        pend_tail.append((st_, t, c))
                        if c == NCH - 1 and t - 1 >= 0:
                            nxt_bufs = p3_load(t - 1)
                flush_tail(0)
                phase_end()
            if stop_after == (l, 3):
                break

            if True:
                last_layer = (l == DEPTH - 1)
                h_sb = A.alloc([KC, T], F32); Bh = [Buf(f"h{i}") for i in range(KC)]; dhq = [P.new_dsem(f"dh4{q}") for q in range(4)]; dhsq = [P.new_dsem(f"dhs4{q}") for q in range(4)]
                ym_sb = A.alloc([KC, T], BF16); Bym = Buf("ym"); dym = P.new_dsem("dym4")
                hg_sb = A.alloc([KC, T], BF16); Bhg = Buf("hg4")
                aT = A.alloc([FC, T], BF16); BaT = [Buf(f"aT{i}") for i in range(FC)]
                wr = Ring(P, A, 6, [KC * 256], BF16, "w4")
                dp = P.new_dsem("dp4")
                p_b = A.alloc([2, T], BF16); Bpb = Buf("pb")
                rsb_ap = A.alloc([T], F32); rsb_b = Buf("rsb4")
                sqr = Ring(P, A, 2, [T], BF16, "sq4", dsem=False)
                tgr = Ring(P, A, 4, [T], F32, "tf4", dsem=False)
                sfr = tgr
                mr = tgr
                sgr = Ring(P, A, 4, [T], BF16, "tb4", dsem=False)
                tur = sgr
                outr = Ring(P, A, 2, [2, T], F32, "o4") if last_layer else None
                pacc = [1, 2, 3, 4, 5, 6, 7]
                pacc_i = [0]

                def next_acc():
                    b = pacc[pacc_i[0] % len(pacc)]
                    pacc_i[0] += 1
                    return b

                wq = []
                wsched = []
                for t in range(NTILE):
                    wsched += [("w_out", g, None) for g in range(8)]
                    if p4_stage == 1:
                        continue
                    for g in range(22):
                        wsched += [("w_ffn_gate", g, None), ("w_ffn_up", g, None)]
                    for g in range(16):
                        wsched += [("w_ffn_down", g, 0), ("w_ffn_down", g, 1)]
                    if p4_stage == 2:
                        continue
                    for g in range(8):
                        wsched += [("w_ple_gate", g, None), ("w_ple_proj", g, None)]
                wi = [0]

                def wprefetch():
                    if wi[0] < len(wsched):
                        name, g, half = wsched[wi[0]]
                        wi[0] += 1
                        ap, b, ds = wr.next()
                        off, kc_, gw, ng = WOFF[name]
                        src = wview(l, name, g)
                        if half is not None:
                            kc_ = kc_ // 2
                            src = src[:, half * kc_ * gw:(half + 1) * kc_ * gw]
                        P.dma("sp", ap[:, 0:kc_ * gw], src, ds, reads=[B_w[(l, name)]], writes=[b])
                        wq.append((ap[:, 0:kc_ * gw].rearrange("p (k g) -> p k g", k=kc_), b))

                def wnext():
                    return wq.pop(0)

                def norm_block(nb, gidx, pend):
                    sq_ap, sq_b, _ = sqr.next()
                    P.op("act", lambda e, nb=nb, sq_ap=sq_ap: e.activation(out=sq_ap, in_=h_sb[:, nb, :], func=AF.Square), reads=[Bh[nb]], writes=[sq_b])
                    P.op("act", lambda e, nb=nb: e.activation(out=hg_sb[:, nb, :], in_=h_sb[:, nb, :], func=AF.Copy, scale=gain_ap(gidx, nb)), reads=[Bh[nb], Bc], writes=[Bhg])
                    pend.append((nb, sq_ap, sq_b))

                def norm_pe(pend, keep):
                    while len(pend) > keep:
                        nb, sq_ap, sq_b = pend.pop(0)
                        P.op("pe", lambda e, nb=nb, sq_ap=sq_ap: e.matmul(banks[0][:, :], lhsT=ones, rhs=sq_ap, start=(nb == 0), stop=(nb == KC - 1)), reads=[sq_b, Bc], writes=[bankb[0]], inc=True)

                def dbg_store(t0):
                    for q4 in range(4):
                        P.dma("pool", hT[q4 * 512:(q4 + 1) * 512, t0:t0 + T].rearrange("(k p) s -> p k s", p=128), h_sb[:, q4 * 4:(q4 + 1) * 4, :], dhsq[q4],
                              reads=Bh[q4 * 4:(q4 + 1) * 4])
                        if t0 // T + 1 < NTILE:
                            h_load(t0 // T + 1, q4)

                def h_load(t_, q4):
                    P.dma("pool", h_sb[:, q4 * 4:(q4 + 1) * 4, :], src_h[q4 * 512:(q4 + 1) * 512, t_ * T:(t_ + 1) * T].rearrange("(k p) s -> p k s", p=128), dhq[q4],
                          reads=[B_hT[t_]] if l > 0 else [], writes=Bh[q4 * 4:(q4 + 1) * 4])

                def ym_load(t_):
                    for q2 in range(2):
                        P.dma("sp", ym_sb[:, q2 * 8:(q2 + 1) * 8, :], ymixT_d[q2 * 1024:(q2 + 1) * 1024, t_ * T:(t_ + 1) * T].rearrange("(k p) s -> p k s", p=128), dym,
                              reads=[B_ymr[t_], B_yms[t_]], writes=[Bym])

                def p4_tile(t):
                    t0 = t * T
                    if t == 0:
                        ym_load(0)
                        for q4 in range(4):
                            h_load(0, q4)
                    P.dma("pool", p_b, pT[l * PLE:(l + 1) * PLE, t0:t0 + T].rearrange("(k p) s -> p k s", p=128), dp, writes=[Bpb])
                    pend = []
                    for g in range(8):
                        w_ap, w_b = wnext()
                        for blk in range(2):
                            nb = g * 2 + blk
                            bk = next_acc()
                            for kc in range(KC):
                                P.op("pe", lambda e, bk=bk, blk=blk, kc=kc, w_ap=w_ap: e.matmul(banks[bk][:, :], lhsT=w_ap[:, kc, blk * 128:(blk + 1) * 128], rhs=ym_sb[:, kc, :], start=(kc == 0), stop=(kc == KC - 1)),
                                     reads=[w_b, Bym], writes=[bankb[bk]], inc=(kc == KC - 1))
                            P.op("dve", lambda e, bk=bk, nb=nb: e.tensor_add(out=h_sb[:, nb, :], in0=banks[bk][:, :], in1=h_sb[:, nb, :]), reads=[bankb[bk]], writes=[Bh[nb]])
                            norm_pe(pend, 0)
                            norm_block(nb, l * 3 + 1, pend)
                        wprefetch()
                    norm_pe(pend, 0)
                    rstd_from_psum(banks[0][:, :], rsb_ap, D, [bankb[0]], [rsb_b])
                    if t + 1 < NTILE:
                        ym_load(t + 1)
                    issue_casts(n=4)
                    if p4_stage == 1:
                        return dbg_store(t0)
                    for g in range(22):
                        wg_ap, wg_b = wnext()
                        wu_ap, wu_b = wnext()
                        for blk in range(2):
                            fb = g * 2 + blk
                            bg = next_acc()
                            for kc in range(KC):
                                P.op("pe", lambda e, bg=bg, blk=blk, kc=kc, wg_ap=wg_ap: e.matmul(banks[bg][:, :], lhsT=wg_ap[:, kc, blk * 128:(blk + 1) * 128], rhs=hg_sb[:, kc, :], start=(kc == 0), stop=(kc == KC - 1)),
                                     reads=[wg_b, Bhg], writes=[bankb[bg]], inc=(kc == KC - 1))
                            bu = next_acc()
                            for kc in range(KC):
                                P.op("pe", lambda e, bu=bu, blk=blk, kc=kc, wu_ap=wu_ap: e.matmul(banks[bu][:, :], lhsT=wu_ap[:, kc, blk * 128:(blk + 1) * 128], rhs=hg_sb[:, kc, :], start=(kc == 0), stop=(kc == KC - 1)),
                                     reads=[wu_b, Bhg], writes=[bankb[bu]], inc=(kc == KC - 1))
                            tg_ap, tg_b, _ = tgr.next()
                            sg_ap, sg_b, _ = sgr.next()
                            tu_ap, tu_b, _ = tur.next()
                            P.op("dve", lambda e, bg=bg, tg_ap=tg_ap: e.tensor_mul(out=tg_ap, in0=banks[bg][:, :], in1=rsb_ap), reads=[bankb[bg], rsb_b], writes=[tg_b])
                            P.op("act", lambda e, tg_ap=tg_ap, sg_ap=sg_ap: e.activation(out=sg_ap, in_=tg_ap, func=AF.Silu), reads=[tg_b], writes=[sg_b])
                            P.op("dve", lambda e, bu=bu, tu_ap=tu_ap: e.tensor_mul(out=tu_ap, in0=banks[bu][:, :], in1=rsb_ap), reads=[bankb[bu], rsb_b], writes=[tu_b])
                            P.op("pool", lambda e, fb=fb, sg_ap=sg_ap, tu_ap=tu_ap: e.tensor_mul(out=aT[:, fb, :], in0=sg_ap, in1=tu_ap), reads=[sg_b, tu_b], writes=[BaT[fb]])
                        wprefetch(); wprefetch()
                    pend = []
                    for nb in range(KC):
                        bk = next_acc()
                        for half in range(2):
                            w_ap, w_b = wnext()
                            for f2 in range(FC // 2):
                                fb = half * (FC // 2) + f2
                                P.op("pe", lambda e, bk=bk, fb=fb, f2=f2, w_ap=w_ap: e.matmul(banks[bk][:, :], lhsT=w_ap[:, f2, :], rhs=aT[:, fb, :], start=(fb == 0), stop=(fb == FC - 1)),
                                     reads=[w_b, BaT[fb]], writes=[bankb[bk]], inc=(fb == FC - 1 or f2 == FC // 2 - 1))
                            wprefetch()
                        P.op("dve", lambda e, bk=bk, nb=nb: e.tensor_add(out=h_sb[:, nb, :], in0=banks[bk][:, :], in1=h_sb[:, nb, :]), reads=[bankb[bk]], writes=[Bh[nb]])
                        norm_pe(pend, 0)
                        norm_block(nb, l * 3 + 2, pend)
                    norm_pe(pend, 0)
                    rstd_from_psum(banks[0][:, :], rsb_ap, D, [bankb[0]], [rsb_b])
                    if p4_stage == 2:
                        return dbg_store(t0)
                    pend = []
                    for g in range(8):
                        wg_ap, wg_b = wnext()
                        wp_ap, wp_b = wnext()
                        for blk in range(2):
                            nb = g * 2 + blk
                            bg = next_acc()
                            for kc in range(KC):
                                P.op("pe", lambda e, bg=bg, blk=blk, kc=kc, wg_ap=wg_ap: e.matmul(banks[bg][:, :], lhsT=wg_ap[:, kc, blk * 128:(blk + 1) * 128], rhs=hg_sb[:, kc, :], start=(kc == 0), stop=(kc == KC - 1)),
                                     reads=[wg_b, Bhg], writes=[bankb[bg]], inc=(kc == KC - 1))
                            bp = next_acc()
                            for kc in range(2):
                                P.op("pe", lambda e, bp=bp, blk=blk, kc=kc, wp_ap=wp_ap: e.matmul(banks[bp][:, :], lhsT=wp_ap[:, kc, blk * 128:(blk + 1) * 128], rhs=p_b[:, kc, :], start=(kc == 0), stop=(kc == 1)),
                                     reads=[wp_b, Bpb], writes=[bankb[bp]], inc=(kc == 1))
                            tg_ap, tg_b, _ = tgr.next()
                            sf_ap, sf_b, _ = sfr.next()
                            m_ap, m_b, _ = mr.next()
                            P.op("dve", lambda e, bg=bg, tg_ap=tg_ap: e.tensor_mul(out=tg_ap, in0=banks[bg][:, :], in1=rsb_ap), reads=[bankb[bg], rsb_b], writes=[tg_b])
                            P.op("act", lambda e, tg_ap=tg_ap, sf_ap=sf_ap: e.activation(out=sf_ap, in_=tg_ap, func=AF.Sigmoid), reads=[tg_b], writes=[sf_b])
                            P.op("dve", lambda e, bp=bp, m_ap=m_ap, sf_ap=sf_ap: e.tensor_mul(out=m_ap, in0=banks[bp][:, :], in1=sf_ap), reads=[bankb[bp], sf_b], writes=[m_b])
                            P.op("pool", lambda e, nb=nb, m_ap=m_ap: e.tensor_add(out=h_sb[:, nb, :], in0=h_sb[:, nb, :], in1=m_ap), reads=[m_b], writes=[Bh[nb]])
                            if not last_layer and nb % 4 == 3:
                                q4 = nb // 4
                                P.dma("pool", hT[q4 * 512:(q4 + 1) * 512, t0:t0 + T].rearrange("(k p) s -> p k s", p=128), h_sb[:, q4 * 4:(q4 + 1) * 4, :], dhsq[q4],
                                      reads=Bh[q4 * 4:(q4 + 1) * 4], writes=[B_hT[t]])
                                if t + 1 < NTILE:
                                    h_load(t + 1, q4)
                            if last_layer and p4_stage is None:
                                sq_ap, sq_b, _ = sqr.next()
                                P.op("act", lambda e, nb=nb, sq_ap=sq_ap: e.activation(out=sq_ap, in_=h_sb[:, nb, :], func=AF.Square), reads=[Bh[nb]], writes=[sq_b])
                                norm_pe(pend, 0)
                                pend.append((nb, sq_ap, sq_b))
                        wprefetch(); wprefetch()
                    if last_layer and p4_stage is None:
                        norm_pe(pend, 0)
                        rstd_from_psum(banks[0][:, :], rsb_ap, D, [bankb[0]], [rsb_b])
                        for q8 in range(8):
                            o_ap, o_b, o_ds = outr.next()
                            for j in range(2):
                                nb = q8 * 2 + j
                                P.op("dve", lambda e, nb=nb, j=j, o_ap=o_ap: e.scalar_tensor_tensor(out=o_ap[:, j, :], in0=h_sb[:, nb, :], scalar=gain_ap(DEPTH * 3, nb), in1=rsb_ap, op0=ALU.mult, op1=ALU.mult), reads=[Bh[nb], rsb_b, Bc], writes=[o_b])
                            P.dma("pool", yT[q8 * 256:(q8 + 1) * 256, t0:t0 + T].rearrange("(k p) s -> p k s", p=128), o_ap, o_ds, reads=[o_b])
                            if q8 % 2 == 1 and t + 1 < NTILE:
                                h_load(t + 1, q8 // 2)


                issue_casts(layer=l)
                for _ in range(6):
                    wprefetch()
                for t in range(NTILE):
                    p4_tile(t)
                issue_casts(layer=DEPTH - 1)
                phase_end()
            if stop_after == (l, 4):
                break

        P.barrier()
        blk = st.enter_context(nc.Block())
        P.emit(blk)
    return nc


def _rearr(W, gw):
    K, N = W.shape
    return np.ascontiguousarray(W.reshape(K // 128, 128, N // gw, gw).transpose(2, 1, 0, 3)).reshape(-1)


def prep_shared(inp, DEPTH):
    f32 = np.float32
    wcat = np.empty((DEPTH * WTOT,), f32)
    for l in range(DEPTH):
        for n, k_, nn_, g_ in WSPEC:
            off = l * WTOT + WOFF[n][0]
            wcat[off: off + k_ * nn_] = _rearr(np.asarray(inp[n][l], f32), g_)
    gl = []
    for l in range(DEPTH):
        for nm in ("norm_mix_g", "norm_ffn_g", "norm_ple_g"):
            gl.append(np.asarray(inp[nm][l], f32).reshape(KC, 128).T)
    gl.append(np.asarray(inp["norm_final_g"], f32).reshape(KC, 128).T)
    gains = np.ascontiguousarray(np.concatenate(gl, axis=1))
    dec = np.ascontiguousarray(np.broadcast_to(np.asarray(inp["ret_decay"], f32)[:DEPTH].reshape(1, DEPTH * 8), (128, DEPTH * 8)))
    sg = []
    for l in range(DEPTH):
        sg.append(np.asarray(inp["sg_ln_g"][l], f32)); sg.append(np.asarray(inp["sg_ln_b"][l], f32))
    sgln = np.ascontiguousarray(np.broadcast_to(np.concatenate(sg)[None, :], (128, DEPTH * 2 * SGW)))
    wsT = np.ascontiguousarray(np.asarray(inp["sg_w"], f32)[:DEPTH].transpose(3, 0, 1, 2)).reshape(128, DEPTH * G * 128)
    bsr = np.ascontiguousarray(np.asarray(inp["sg_b"], f32)[:DEPTH].reshape(1, DEPTH * SGW))
    half = 128
    invf = (np.float32(10000.0) ** (-np.arange(half, dtype=f32) / np.float32(half))).astype(f32).reshape(128, 1)
    return dict(wcat=wcat, gains=gains, dec=dec, sgln=sgln, wsT=wsT, bsr=bsr, invf=invf)


def prep_core(x_rows, p_rows, pos, carry_flag, DEPTH):
    f32 = np.float32
    NT = x_rows.shape[0]
    xT = np.ascontiguousarray(x_rows.T)
    pT = np.ascontiguousarray(p_rows.transpose(0, 2, 1)).reshape(DEPTH * PLE, NT)
    posb = np.ascontiguousarray(np.broadcast_to(pos.astype(f32)[None, :], (128, NT)))
    carry = np.full((128, 1), carry_flag, f32)
    return dict(xT=xT, pT=pT, posb=posb, carry=carry)


_NC_CACHE = {}


def kernel(x_prompt, x_sample, p_prompt, p_sample, norm_mix_g, w_in, ret_decay, sg_ln_g, sg_ln_b,
           sg_w, sg_b, w_out, norm_ffn_g, w_ffn_gate, w_ffn_up, w_ffn_down, norm_ple_g, w_ple_gate,
           w_ple_proj, norm_final_g):
    DEPTH = 2
    NT = 8192
    SEG = NT // 2
    f32 = np.float32
    x_prompt = np.asarray(x_prompt, f32); x_sample = np.asarray(x_sample, f32)
    p_prompt = np.asarray(p_prompt, f32); p_sample = np.asarray(p_sample, f32)
    inp = dict(norm_mix_g=norm_mix_g, w_in=w_in, ret_decay=ret_decay, sg_ln_g=sg_ln_g, sg_ln_b=sg_ln_b, sg_w=sg_w,
               sg_b=sg_b, w_out=w_out, norm_ffn_g=norm_ffn_g, w_ffn_gate=w_ffn_gate, w_ffn_up=w_ffn_up,
               w_ffn_down=w_ffn_down, norm_ple_g=norm_ple_g, w_ple_gate=w_ple_gate, w_ple_proj=w_ple_proj,
               norm_final_g=norm_final_g)
    shared = prep_shared(inp, DEPTH)
    in_maps = []
    plan = {0: ("p", 0), 1: ("s2", 0, 1), 2: ("s1", 2), 3: ("s1", 3), 4: ("p", 1), 5: ("s2", 4, 5), 6: ("s1", 6), 7: ("s1", 7)}
    pos2 = np.concatenate([np.arange(SEG), np.arange(SEG)])
    for c in range(8):
        pl = plan[c]
        if pl[0] == "p":
            xr = x_prompt[pl[1]]; pr = p_prompt[:, pl[1]]; pos = np.arange(NT); cf = 1.0
        elif pl[0] == "s2":
            xr = np.concatenate([x_sample[pl[1]], x_sample[pl[2]]], axis=0)
            pr = np.concatenate([p_sample[:, pl[1]], p_sample[:, pl[2]]], axis=1); pos = pos2; cf = 0.0
        else:
            xr = np.concatenate([x_sample[pl[1]], np.zeros((SEG, D), f32)], axis=0)
            pr = np.concatenate([p_sample[:, pl[1]], np.zeros((DEPTH, SEG, PLE), f32)], axis=1); pos = pos2; cf = 0.0
        m = dict(shared)
        m.update(prep_core(xr, pr, pos, cf, DEPTH))
        in_maps.append(m)
    if "nc" not in _NC_CACHE:
        _NC_CACHE["nc"] = build(NT, DEPTH=DEPTH)
    res = run_bass_kernel_spmd(_NC_CACHE["nc"], in_maps, core_ids=list(range(8)))
    outs = [np.asarray(r["yT"]) for r in res.results]
    y_prompt = np.empty((2, NT, D), f32)
    y_sample = np.empty((8, SEG, D), f32)
    for c in range(8):
        pl = plan[c]
        o = outs[c].T
        if pl[0] == "p":
            y_prompt[pl[1]] = o
        elif pl[0] == "s2":
            y_sample[pl[1]] = o[:SEG]; y_sample[pl[2]] = o[SEG:]
        else:
            y_sample[pl[1]] = o[:SEG]
    return (y_prompt, y_sample)
```

```python
import math
from contextlib import ExitStack

import numpy as np
import concourse.bass as bass
import concourse.mybir as mybir
from concourse.bass_utils import run_bass_kernel_spmd

F32 = mybir.dt.float32
BF16 = mybir.dt.bfloat16
U8 = mybir.dt.uint8
I32 = mybir.dt.int32
AF = mybir.ActivationFunctionType
ALU = mybir.AluOpType
DTSIZE = {F32: 4, BF16: 2, U8: 1, I32: 4}

D = 2048
KC = D // 128
RW = 1024
H = 4
DH = 256
SGW = 1024
G = 8
DFF = 5632
FC = DFF // 128
PLE = 256
INW = 6144
EPS = 1e-6
T = 512
NCH = T // 128
ARENA = 210000

WSPEC = [("w_in", D, INW, 512), ("w_out", D, D, 256), ("w_ffn_gate", D, DFF, 256), ("w_ffn_up", D, DFF, 256),
         ("w_ffn_down", DFF, D, 128), ("w_ple_gate", D, D, 256), ("w_ple_proj", PLE, D, 256)]
WOFF = {}
_o = 0
for _n, _k, _nn, _g in WSPEC:
    WOFF[_n] = (_o, _k // 128, _g, _nn // _g)
    _o += _k * _nn
WTOT = _o


class Buf:
    __slots__ = ("name", "w", "r")

    def __init__(self, name=""):
        self.name = name
        self.w = None
        self.r = {}


class Sem:
    def __init__(self, h, key):
        self.h = h
        self.key = key
        self.cnt = 0


class Eng:
    def __init__(self, name, sem):
        self.name = name
        self.sem = sem
        self.prog = []
        self.waited = {}
        self.pending = False


class Prog:
    def __init__(self, nc, stack, same_engine_sync=False):
        self.nc = nc
        self.stack = stack
        self.sems = {}
        self.nsem = 0
        self.same_engine_sync = same_engine_sync
        self.eng = {}
        for n in ("pe", "act", "dve", "pool", "sp"):
            self.eng[n] = Eng(n, self.new_sem("e_" + n))
        self.dsems = []
        self.free_dsems = []
        self.phase_dsems = []

    def new_sem(self, name):
        h = self.stack.enter_context(self.nc.semaphore(name))
        s = Sem(h, self.nsem)
        self.sems[s.key] = s
        self.nsem += 1
        return s

    def new_dsem(self, name, persistent=False):
        if not persistent and self.free_dsems:
            s = self.free_dsems.pop()
        else:
            s = self.new_sem(name)
            self.dsems.append(s)
        if not persistent:
            self.phase_dsems.append(s)
        return s

    def release_phase_dsems(self):
        self.free_dsems.extend(self.phase_dsems)
        self.phase_dsems = []

    def _wait(self, E, tok, skip_key=None):
        key, val = tok
        if key == skip_key:
            return
        if key == E.sem.key and (not self.same_engine_sync or val > E.sem.cnt or E.name in ("pe", "sp")):
            return
        if E.waited.get(key, 0) >= val:
            return
        E.waited[key] = val
        E.prog.append(("wait", self.sems[key].h, val))

    def _deps(self, E, reads, writes, skip_key=None):
        for b in reads:
            if b.w is not None:
                self._wait(E, b.w, skip_key)
        for b in writes:
            if b.w is not None:
                self._wait(E, b.w, skip_key)
            for k, v in b.r.items():
                self._wait(E, (k, v), skip_key)

    def _update(self, tok, reads, writes):
        for b in writes:
            b.w = tok
            b.r = {}
        k, v = tok
        for b in reads:
            if b.r.get(k, 0) < v:
                b.r[k] = v

    def op(self, en, emit, reads=(), writes=(), inc=True):
        E = self.eng[en]
        self._deps(E, reads, writes)
        if inc:
            E.sem.cnt += 1
            E.pending = False
            E.prog.append(("ins", emit, E.sem.h))
            tok = (E.sem.key, E.sem.cnt)
        else:
            E.pending = True
            E.prog.append(("ins", emit, None))
            tok = (E.sem.key, E.sem.cnt + 1)
        self._update(tok, reads, writes)

    def dma(self, qn, out, in_, ds, reads=(), writes=()):
        E = self.eng[qn]
        self._deps(E, reads, writes, skip_key=ds.key)
        ds.cnt += 16
        E.prog.append(("dma", out, in_, ds.h))
        self._update((ds.key, ds.cnt), reads, writes)

    def cc(self, kind, groups, in_ap, out_ap, ds, reads=(), writes=()):
        E = self.eng["pool"]
        self._deps(E, reads, writes, skip_key=ds.key)
        ds.cnt += 16
        E.prog.append(("cc", kind, groups, in_ap, out_ap, ds.h))
        self._update((ds.key, ds.cnt), reads, writes)

    def barrier(self):
        for E in self.eng.values():
            assert not E.pending, E.name
        for E in self.eng.values():
            for E2 in self.eng.values():
                if E2 is not E and E2.sem.cnt > 0:
                    self._wait(E, (E2.sem.key, E2.sem.cnt))
            for ds in self.dsems:
                if ds.cnt > 0:
                    self._wait(E, (ds.key, ds.cnt))

    def emit(self, block):
        decos = {"pe": block.tensor, "act": block.scalar, "dve": block.vector,
                 "pool": block.gpsimd, "sp": block.sync}
        for n, deco in decos.items():
            E = self.eng[n]

            def body(h, E=E):
                for it in E.prog:
                    if it[0] == "wait":
                        h.wait_ge(it[1], it[2])
                    elif it[0] == "ins":
                        ins = it[1](h)
                        if it[2] is not None:
                            ins.then_inc(it[2], 1)
                    elif it[0] == "cc":
                        h.collective_compute(it[1], ALU.bypass, replica_groups=it[2], ins=[it[3]], outs=[it[4]]).then_inc(it[5], 16)
                    else:
                        h.dma_start(out=it[1], in_=it[2]).then_inc(it[3], 16)
            deco(body)


class Arena:
    def __init__(self, ap_u8, size):
        self.ap = ap_u8
        self.size = size
        self.off = 0

    def alloc(self, free_shape, dtype, parts=128):
        n = int(np.prod(free_shape))
        nb = n * DTSIZE[dtype]
        self.off = (self.off + 63) // 64 * 64
        assert self.off + nb <= self.size, ("arena overflow", self.off, nb, self.size)
        v = self.ap[0:parts, self.off:self.off + nb]
        if dtype != U8:
            v = v.bitcast(dtype)
        self.off += nb
        if len(free_shape) == 2:
            v = v.rearrange("p (a b) -> p a b", a=free_shape[0], b=free_shape[1])
        elif len(free_shape) == 3:
            v = v.rearrange("p (a b c) -> p a b c", a=free_shape[0], b=free_shape[1], c=free_shape[2])
        return v


class Ring:
    def __init__(self, P, A, n, shape, dtype, name, dsem=True):
        self.slots = []
        for i in range(n):
            self.slots.append((A.alloc(shape, dtype), Buf(f"{name}{i}"), P.new_dsem(f"d_{name}{i}") if dsem else None))
        self.i = 0

    def next(self):
        s = self.slots[self.i % len(self.slots)]
        self.i += 1
        return s


def build(NT, DEPTH=2, stop_after=None, debug=False, same_engine_sync=True, p4_stage=None):
    SEG = NT // 2
    NTILE = NT // T
    TPS = SEG // T
    nc = bass.Bass("TRN2", target_bir_lowering=False)
    dkind = "ExternalOutput" if debug else "Internal"

    def din(name, shape, dt=F32):
        return nc.dram_tensor(name, shape, dt, kind="ExternalInput").ap()

    def dscr(name, shape, dt):
        return nc.dram_tensor(name, shape, dt, kind=dkind).ap()

    xT = din("xT", [D, NT])
    pT = din("pT", [DEPTH * PLE, NT])
    wcat = din("wcat", [DEPTH * WTOT])
    gains = din("gains", [128, (DEPTH * 3 + 1) * KC])
    dec = din("dec", [128, DEPTH * 8])
    sgln = din("sgln", [128, DEPTH * 2 * SGW])
    wsT = din("wsT", [128, DEPTH * G * 128])
    bsr = din("bsr", [1, DEPTH * SGW])
    posb = din("posb", [128, NT])
    invf = din("invf", [128, 1])
    carry = din("carry", [128, 1])
    yT = nc.dram_tensor("yT", [D, NT], F32, kind="ExternalOutput").ap()

    wbf = nc.dram_tensor("wbf", [DEPTH * WTOT], BF16, kind="Internal").ap()
    hT = dscr("hT", [D, NT], F32)
    qT_d = dscr("qT_d", [RW, NT], BF16)
    kT_d = dscr("kT_d", [RW, NT], BF16)
    ktm_d = dscr("ktm_d", [NT, RW], BF16)
    vtm_d = dscr("vtm_d", [NT, RW], BF16)
    gact_d = dscr("gact_d", [NT, RW], BF16)
    svln_d = dscr("svln_d", [NT, SGW], BF16)
    uT_d = dscr("uT_d", [SGW, NT], BF16)
    ypart_d = dscr("ypart_d", [NT, RW], F32)
    ymixT_d = dscr("ymixT_d", [D, NT], BF16)

    with ExitStack() as st:
        P = Prog(nc, st, same_engine_sync=same_engine_sync)
        arena_t = st.enter_context(nc.sbuf_tensor("arena", [128, ARENA], U8))
        A = Arena(arena_t[:], ARENA)
        banks = [st.enter_context(nc.psum_tensor(f"bank{i}", [128, 512], F32)) for i in range(8)]
        bankb = [Buf(f"bank{i}") for i in range(8)]

        def tilebufs(name):
            return [Buf(f"{name}{t}") for t in range(NTILE)]
        B_hT = tilebufs("hT")
        B_q = tilebufs("q"); B_k = tilebufs("k"); B_ktm = tilebufs("ktm"); B_v = tilebufs("v")
        B_g = tilebufs("g"); B_sv = tilebufs("sv"); B_u = tilebufs("u"); B_yp = tilebufs("yp")
        B_ymr = tilebufs("ymr"); B_yms = tilebufs("yms")
        B_w = {(l, n): Buf(f"w{l}{n}") for l in range(DEPTH) for n, *_ in WSPEC}

        CH_EL = 128 * 8192
        cast_q = []
        for l in range(DEPTH):
            for n, k_, nn_, g_ in WSPEC:
                off = l * WTOT + WOFF[n][0]
                tot = k_ * nn_
                ds = P.new_dsem(f"wc{l}{n}", persistent=True)
                o = 0
                while o < tot:
                    sz = min(CH_EL, tot - o)
                    src = wcat[off + o: off + o + sz].rearrange("(p f) -> p f", p=128)
                    dst = wbf[off + o: off + o + sz].rearrange("(p f) -> p f", p=128)
                    cast_q.append((l, n, dst, src, ds))
                    o += sz

        def issue_casts(n=None, layer=None, name=None):
            while cast_q:
                l_, n_, dst, src, ds = cast_q[0]
                if layer is not None and (l_ > layer or (name is not None and (l_, n_) != (layer, name))):
                    break
                if layer is None and n is not None and n <= 0:
                    break
                cast_q.pop(0)
                P.dma("pool", dst, src, ds, writes=[B_w[(l_, n_)]])
                if n is not None:
                    n -= 1

        issue_casts(layer=0, name="w_in")

        def wview(l, name, g):
            off, kc, gw, ng = WOFF[name]
            base = l * WTOT + off + g * 128 * kc * gw
            return wbf[base: base + 128 * kc * gw].rearrange("(p f) -> p f", p=128)

        gains_sb = A.alloc([(DEPTH * 3 + 1) * KC], F32); Bc = Buf("const")
        ld0 = P.new_dsem("ld0", persistent=True)
        ld1 = P.new_dsem("ld1", persistent=True)
        ld2 = P.new_dsem("ld2", persistent=True)
        P.dma("sp", gains_sb, gains, ld0, writes=[Bc])
        dec_sb = A.alloc([DEPTH * 8], F32)
        P.dma("sp", dec_sb, dec, ld0, writes=[Bc])
        invf_sb = A.alloc([1], F32)
        P.dma("sp", invf_sb, invf, ld0, writes=[Bc])
        carry_sb = A.alloc([1], F32)
        P.dma("sp", carry_sb, carry, ld0, writes=[Bc])
        diff = A.alloc([128], F32)
        fidx1 = A.alloc([128], F32)
        pidx = A.alloc([1], F32)
        c128mp = A.alloc([1], F32)
        c127mp = A.alloc([1], F32)
        ident = A.alloc([128], BF16)
        ones = A.alloc([128], BF16)
        tmpc = A.alloc([128], F32)
        P.op("pool", lambda e: e.iota(diff, [[1, 128]], base=0, channel_multiplier=-1, allow_small_or_imprecise_dtypes=True), writes=[Bc])
        P.op("pool", lambda e: e.iota(fidx1, [[1, 128]], base=1, channel_multiplier=0, allow_small_or_imprecise_dtypes=True), writes=[Bc])
        P.op("pool", lambda e: e.iota(pidx, [[0, 1]], base=0, channel_multiplier=1, allow_small_or_imprecise_dtypes=True), writes=[Bc])
        P.op("pool", lambda e: e.memset(ones, 1.0), writes=[Bc])
        P.op("dve", lambda e: e.tensor_scalar(out=c128mp, in0=pidx, scalar1=-1.0, scalar2=128.0, op0=ALU.mult, op1=ALU.add), reads=[Bc], writes=[Bc])
        P.op("dve", lambda e: e.tensor_scalar(out=c127mp, in0=pidx, scalar1=-1.0, scalar2=127.0, op0=ALU.mult, op1=ALU.add), reads=[Bc], writes=[Bc])
        P.op("dve", lambda e: e.tensor_scalar(out=tmpc, in0=diff, scalar1=0.0, scalar2=None, op0=ALU.is_equal), reads=[Bc], writes=[Bc])
        P.op("dve", lambda e: e.tensor_copy(out=ident, in_=tmpc), reads=[Bc], writes=[Bc])
        lg = A.alloc([8], F32)
        e1 = A.alloc([8], F32)
        maskT = A.alloc([H, 128], F32)
        qdecf = A.alloc([8, 128], BF16)
        qdecb = A.alloc([H], F32)
        kdecf = A.alloc([H], F32)
        kdecb = A.alloc([H], F32)
        cdec = A.alloc([8], F32)
        lng = A.alloc([SGW], F32)
        lnb = A.alloc([SGW], F32)
        wsT_b = A.alloc([G, 128], BF16)
        bs_b = A.alloc([SGW], BF16, parts=1)
        Bl = Buf("layerconst")
        A_base = A.off

        def layer_setup(l):
            P.dma("sp", lng, sgln[:, (2 * l) * SGW:(2 * l + 1) * SGW], ld2, writes=[Bl])
            P.dma("sp", lnb, sgln[:, (2 * l + 1) * SGW:(2 * l + 2) * SGW], ld2, writes=[Bl])
            P.dma("pool", wsT_b, wsT[:, l * G * 128:(l + 1) * G * 128].rearrange("p (g i) -> p g i", g=G), ld1, writes=[Bl])
            P.dma("pool", bs_b, bsr[:, l * SGW:(l + 1) * SGW], ld1, writes=[Bl])
            P.op("act", lambda e: e.activation(out=e1, in_=dec_sb[:, l * 8:(l + 1) * 8], func=AF.Exp, scale=-math.log(2.0)), reads=[Bc, Bl], writes=[Bl])
            P.op("dve", lambda e: e.tensor_scalar(out=e1, in0=e1, scalar1=-(2.0 ** -5), scalar2=1.0, op0=ALU.mult, op1=ALU.add), reads=[Bl], writes=[Bl])
            P.op("act", lambda e: e.activation(out=lg, in_=e1, func=AF.Ln), reads=[Bl], writes=[Bl])
            P.op("act", lambda e: e.activation(out=cdec, in_=lg, func=AF.Exp, scale=128.0), reads=[Bl], writes=[Bl])
            for h in range(H):
                P.op("dve", lambda e, h=h: e.tensor_scalar(out=tmpc, in0=diff, scalar1=0.0, scalar2=lg[:, h:h + 1], op0=ALU.max, op1=ALU.mult), reads=[Bc, Bl], writes=[Bl])
                P.op("dve", lambda e, h=h: e.tensor_scalar(out=maskT[:, h, :], in0=diff, scalar1=-1.0, scalar2=0.0, op0=ALU.mult, op1=ALU.max), reads=[Bc, Bl], writes=[Bl])
                P.op("dve", lambda e, h=h: e.scalar_tensor_tensor(out=tmpc, in0=maskT[:, h, :], scalar=lg[:, 4 + h:5 + h], in1=tmpc, op0=ALU.mult, op1=ALU.add), reads=[Bl], writes=[Bl])
                P.op("act", lambda e, h=h: e.activation(out=maskT[:, h, :], in_=tmpc, func=AF.Exp), reads=[Bl], writes=[Bl])
                for dc in range(2):
                    P.op("act", lambda e, h=h, dc=dc: e.activation(out=qdecf[:, 2 * h + dc, :], in_=fidx1, func=AF.Exp, scale=lg[:, h:h + 1]), reads=[Bc, Bl], writes=[Bl])
                P.op("act", lambda e, h=h: e.activation(out=qdecb[:, h:h + 1], in_=c128mp, func=AF.Exp, scale=lg[:, 4 + h:5 + h]), reads=[Bc, Bl], writes=[Bl])
                P.op("act", lambda e, h=h: e.activation(out=kdecf[:, h:h + 1], in_=c127mp, func=AF.Exp, scale=lg[:, h:h + 1]), reads=[Bc, Bl], writes=[Bl])
                P.op("act", lambda e, h=h: e.activation(out=kdecb[:, h:h + 1], in_=pidx, func=AF.Exp, scale=lg[:, 4 + h:5 + h]), reads=[Bc, Bl], writes=[Bl])

        def gain_ap(idx, kc):
            return gains_sb[:, idx * KC + kc: idx * KC + kc + 1]

        def phase_end():
            P.barrier()
            P.release_phase_dsems()
            A.off = A_base
            for b in bankb:
                b.w = None; b.r = {}

        def rstd_from_psum(ps_ap, out_ap, n, reads, writes):
            P.op("dve", lambda e: e.tensor_scalar(out=out_ap, in0=ps_ap, scalar1=1.0 / n, scalar2=EPS, op0=ALU.mult, op1=ALU.add), reads=reads, writes=writes)
            P.op("act", lambda e: e.activation(out=out_ap, in_=out_ap, func=AF.Sqrt), reads=writes, writes=writes)
            P.op("dve", lambda e: e.reciprocal(out=out_ap, in_=out_ap), reads=writes, writes=writes)

        for l in range(DEPTH):
            src_h = xT if l == 0 else hT
            layer_setup(l)

            if True:
                hin = Ring(P, A, 2, [4, T], F32, "hin")
                sqr = Ring(P, A, 4, [T], BF16, "sq", dsem=False)
                hgr = Ring(P, A, 2, [KC, T], BF16, "hg", dsem=False)
                rsb = Ring(P, A, 2, [T], F32, "rsb", dsem=False)
                rst = Ring(P, A, 2, [NCH], F32, "rst", dsem=False)
                tab_ap = A.alloc([4, T], F32); tab_b = Buf("tabs")
                posr = Ring(P, A, 2, [T], F32, "pos")
                wr = Ring(P, A, 3, [KC * 512], BF16, "w")
                tmps = Ring(P, A, 2, [4, T], F32, "ropet", dsem=False)
                trig = A.alloc([2, T], F32); Btrig = Buf("trig")
                trigi = A.alloc([T], I32)
                outr = Ring(P, A, 4, [4 * T], BF16, "out")
                svg = Ring(P, A, 4, [T], F32, "svg", dsem=False)
                svx = Ring(P, A, 4, [T], F32, "svx", dsem=False)
                ut = Ring(P, A, 2, [T], F32, "ut", dsem=False)
                lnst = Ring(P, A, 2, [6, 16], F32, "lnst", dsem=False)
                pacc = [2, 3, 4, 5]
                pacc_i = [0]
                ptr = [6, 7]
                ptr_i = [0]

                def next_acc():
                    b = pacc[pacc_i[0] % 4]
                    pacc_i[0] += 1
                    return b

                norm_state = {}

                def p1_begin(t):
                    pap, pb, pds = posr.next()
                    P.dma("sp", pap, posb[:, t * T:(t + 1) * T], pds, writes=[pb])
                    norm_state[t] = dict(hin={}, pos=(pap, pb), hg=hgr.next(), rsb=rsb.next(), rst=rst.next(), sq={})

                def p1_load(t, part):
                    t0 = t * T
                    ap, b, ds = hin.next()
                    kc0 = part * 4
                    P.dma("sp", ap, src_h[kc0 * 128:(kc0 + 4) * 128, t0:t0 + T].rearrange("(k p) s -> p k s", p=128),
                          ds, reads=[B_hT[t]] if l > 0 else [], writes=[b])
                    norm_state[t]["hin"][part] = (ap, b)

                def p1_norm_elem(t, part):
                    s = norm_state[t]
                    hg_ap, hg_b, _ = s["hg"]
                    hap, hb = s["hin"][part]
                    for j in range(4):
                        kc = part * 4 + j
                        sq_ap, sq_b, _ = sqr.next()
                        s["sq"][kc] = (sq_ap, sq_b)
                        P.op("act", lambda e, hap=hap, j=j, sq_ap=sq_ap: e.activation(out=sq_ap, in_=hap[:, j, :], func=AF.Square), reads=[hb], writes=[sq_b])
                        ga = gain_ap(l * 3 + 0, kc)
                        P.op("act", lambda e, hap=hap, j=j, kc=kc, ga=ga: e.activation(out=hg_ap[:, kc, :], in_=hap[:, j, :], func=AF.Copy, scale=ga), reads=[hb, Bc], writes=[hg_b])

                def p1_norm_pe(t, part):
                    s = norm_state[t]
                    for j in range(4):
                        kc = part * 4 + j
                        sq_ap, sq_b = s["sq"][kc]
                        P.op("pe", lambda e, sq_ap=sq_ap, kc=kc: e.matmul(banks[0][:, :], lhsT=ones, rhs=sq_ap, start=(kc == 0), stop=(kc == KC - 1)), reads=[sq_b, Bc], writes=[bankb[0]], inc=False)
                        for c in range(NCH):
                            P.op("pe", lambda e, sq_ap=sq_ap, kc=kc, c=c: e.matmul(banks[1][:, c:c + 1], lhsT=sq_ap[:, c * 128:(c + 1) * 128], rhs=ones[:, 0:1], start=(kc == 0 and c == 0), stop=(kc == KC - 1 and c == NCH - 1), skip_group_check=True), reads=[sq_b, Bc], writes=[bankb[1]], inc=(c == NCH - 1))

                def p1_norm_fin(t):
                    s = norm_state[t]
                    rsb_ap, rsb_b, _ = s["rsb"]
                    rst_ap, rst_b, _ = s["rst"]
                    pap, pb = s["pos"]
                    rstd_from_psum(banks[0][:, :], rsb_ap, D, [bankb[0]], [rsb_b])
                    rstd_from_psum(banks[1][:, 0:NCH], rst_ap, D, [bankb[1]], [rst_b])
                    u = trig[:, 0, :]
                    f = trig[:, 1, :]
                    for j, sh in ((0, 0.25), (1, 0.0)):
                        P.op("dve", lambda e: e.tensor_scalar(out=u, in0=pap, scalar1=invf_sb[:, 0:1], scalar2=1.0 / (2 * math.pi), op0=ALU.mult, op1=ALU.mult), reads=[pb, Bc], writes=[Btrig])
                        if sh:
                            P.op("dve", lambda e, sh=sh: e.tensor_scalar(out=u, in0=u, scalar1=sh, scalar2=None, op0=ALU.add), reads=[Btrig], writes=[Btrig])
                        P.op("dve", lambda e: e.tensor_copy(out=trigi, in_=u), reads=[Btrig], writes=[Btrig])
                        P.op("dve", lambda e: e.tensor_copy(out=f, in_=trigi), reads=[Btrig], writes=[Btrig])
                        P.op("dve", lambda e: e.tensor_sub(out=u, in0=u, in1=f), reads=[Btrig], writes=[Btrig])
                        P.op("dve", lambda e: e.tensor_scalar(out=f, in0=u, scalar1=0.5, scalar2=None, op0=ALU.is_gt), reads=[Btrig], writes=[Btrig])
                        P.op("dve", lambda e: e.tensor_sub(out=u, in0=u, in1=f), reads=[Btrig], writes=[Btrig])
                        P.op("act", lambda e: e.activation(out=u, in_=u, func=AF.Sin, scale=2 * math.pi), reads=[Btrig], writes=[Btrig])
                        P.op("dve", lambda e, j=j: e.tensor_mul(out=tab_ap[:, j, :], in0=u, in1=rsb_ap), reads=[Btrig, rsb_b], writes=[tab_b])
                        P.op("pool", lambda e, j=j: e.tensor_scalar(out=tab_ap[:, 2 + j, :], in0=tab_ap[:, j, :], scalar1=DH ** -0.5, scalar2=None, op0=ALU.mult), reads=[tab_b], writes=[tab_b])

                wq = []
                GORDER = [10, 0, 4, 1, 11, 5, 2, 6, 8, 3, 7, 9]

                def p1_wload(g):
                    ap, b, ds = wr.next()
                    P.dma("sp", ap, wview(l, "w_in", g), ds, reads=[B_w[(l, "w_in")]], writes=[b])
                    wq.append((ap.rearrange("p (k g) -> p k g", k=KC), b))

                def fm_view(o_ap):
                    return o_ap.rearrange("p (b s) -> p b s", b=4)

                def tm_view(o_ap):
                    return o_ap.rearrange("p (c f) -> p c f", c=NCH)

                def p1_proj(t):
                    s = norm_state[t]
                    t0 = t * T
                    hg_ap, hg_b, _ = s["hg"]
                    rsb_ap, rsb_b, _ = s["rsb"]
                    rst_ap, rst_b, _ = s["rst"]
                    nx = t + 1 < NTILE
                    for gi in range(12):
                        g = GORDER[gi]
                        if nx and 4 <= gi < 8:
                            p1_norm_elem(t + 1, gi - 4)
                        if nx and gi == 1:
                            p1_begin(t + 1)
                        if gi == 0 and t == 0:
                            p1_wload(GORDER[0]); p1_wload(GORDER[1])
                        nxt = t * 12 + gi + 2
                        if nxt < NTILE * 12:
                            p1_wload(GORDER[nxt % 12])
                        if nx and 2 <= gi < 6:
                            p1_load(t + 1, gi - 2)
                        if nx and gi == 10:
                            p1_norm_fin(t + 1)
                        w_ap, w_b = wq.pop(0)
                        o_ap, o_b, o_ds = outr.next()
                        col0 = (g % 2) * 512
                        if g < 4 or 8 <= g < 10:
                            of = fm_view(o_ap)
                            accs = []
                            for blk in range(4):
                                bk = next_acc()
                                for kc in range(KC):
                                    P.op("pe", lambda e, bk=bk, blk=blk, kc=kc, w_ap=w_ap: e.matmul(banks[bk][:, :], lhsT=w_ap[:, kc, blk * 128:(blk + 1) * 128], rhs=hg_ap[:, kc, :], start=(kc == 0), stop=(kc == KC - 1)),
                                         reads=[w_b, hg_b], writes=[bankb[bk]], inc=(kc == KC - 1))
                                accs.append(bk)
                                if g >= 8:
                                    ut_ap, ut_b, _ = ut.next()
                                    P.op("dve", lambda e, bk=bk, ut_ap=ut_ap: e.tensor_mul(out=ut_ap, in0=banks[bk][:, :], in1=rsb_ap), reads=[bankb[bk], rsb_b], writes=[ut_b])
                                    P.op("act", lambda e, ut_ap=ut_ap, blk=blk, of=of: e.activation(out=of[:, blk, :], in_=ut_ap, func=AF.Gelu_apprx_tanh), reads=[ut_b], writes=[o_b])
                            if g < 4:
                                ci, si = (2, 3) if g >= 2 else (0, 1)
                                for pr in range(2):
                                    b0, b1 = accs[2 * pr], accs[2 * pr + 1]
                                    tm_ap, tm_b, _ = tmps.next()
                                    P.op("dve", lambda e, b0=b0, tm_ap=tm_ap, ci=ci: e.tensor_mul(out=tm_ap[:, 0, :], in0=banks[b0][:, :], in1=tab_ap[:, ci, :]), reads=[bankb[b0], tab_b], writes=[tm_b])
                                    P.op("dve", lambda e, b1=b1, tm_ap=tm_ap, si=si: e.tensor_mul(out=tm_ap[:, 1, :], in0=banks[b1][:, :], in1=tab_ap[:, si, :]), reads=[bankb[b1], tab_b], writes=[tm_b])
                                    P.op("dve", lambda e, b0=b0, tm_ap=tm_ap, si=si: e.tensor_mul(out=tm_ap[:, 2, :], in0=banks[b0][:, :], in1=tab_ap[:, si, :]), reads=[bankb[b0], tab_b], writes=[tm_b])
                                    P.op("dve", lambda e, b1=b1, tm_ap=tm_ap, ci=ci: e.tensor_mul(out=tm_ap[:, 3, :], in0=banks[b1][:, :], in1=tab_ap[:, ci, :]), reads=[bankb[b1], tab_b], writes=[tm_b])
                                    P.op("pool", lambda e, tm_ap=tm_ap, pr=pr, of=of: e.tensor_sub(out=of[:, 2 * pr, :], in0=tm_ap[:, 0, :], in1=tm_ap[:, 1, :]), reads=[tm_b], writes=[o_b])
                                    P.op("pool", lambda e, tm_ap=tm_ap, pr=pr, of=of: e.tensor_add(out=of[:, 2 * pr + 1, :], in0=tm_ap[:, 2, :], in1=tm_ap[:, 3, :]), reads=[tm_b], writes=[o_b])
                            dd, db = (qT_d, B_q) if g < 2 else ((kT_d, B_k) if g < 4 else (uT_d, B_u))
                            P.dma("pool", dd[col0:col0 + 512, t0:t0 + T].rearrange("(b p) s -> p b s", p=128), of, o_ds, reads=[o_b], writes=[db[t]])
                            if 2 <= g < 4:
                                o2_ap, o2_b, o2_ds = outr.next()
                                o2 = tm_view(o2_ap)
                                bk = ptr[ptr_i[0] % 2]; ptr_i[0] += 1
                                pv = banks[bk][:, :].bitcast(BF16)
                                for half in range(2):
                                    for cc in range(2):
                                        c = half * 2 + cc
                                        for blk in range(4):
                                            sl = (cc * 4 + blk) * 128
                                            P.op("pe", lambda e, sl=sl, blk=blk, c=c, of=of, pv=pv: e.transpose(out=pv[:, sl:sl + 128], in_=of[:, blk, c * 128:(c + 1) * 128], identity=ident),
                                                 reads=[o_b, Bc], writes=[bankb[bk]], inc=(blk == 3 and cc == 1))
                                    P.op("act", lambda e, half=half, o2=o2, pv=pv: e.activation(out=o2[:, half * 2:half * 2 + 2, :], in_=pv.rearrange("p (c f) -> p c f", c=2), func=AF.Copy), reads=[bankb[bk]], writes=[o2_b])
                                P.dma("pool", ktm_d[t0:t0 + T, col0:col0 + 512].rearrange("(c p) f -> p c f", p=128), o2, o2_ds, reads=[o2_b], writes=[B_ktm[t]])
                        else:
                            ot = tm_view(o_ap)
                            for c in range(NCH):
                                bk = next_acc()
                                for kc in range(KC):
                                    P.op("pe", lambda e, bk=bk, kc=kc, c=c, w_ap=w_ap: e.matmul(banks[bk][:, :], lhsT=hg_ap[:, kc, c * 128:(c + 1) * 128], rhs=w_ap[:, kc, :], start=(kc == 0), stop=(kc == KC - 1)),
                                         reads=[w_b, hg_b], writes=[bankb[bk]], inc=(kc == KC - 1))
                                if g < 6:
                                    P.op("act", lambda e, bk=bk, c=c, ot=ot: e.activation(out=ot[:, c, :], in_=banks[bk][:, :], func=AF.Copy, scale=rst_ap[:, c:c + 1]), reads=[bankb[bk], rst_b], writes=[o_b])
                                elif g < 8:
                                    P.op("act", lambda e, bk=bk, c=c, ot=ot: e.activation(out=ot[:, c, :], in_=banks[bk][:, :], func=AF.Silu, scale=rst_ap[:, c:c + 1]), reads=[bankb[bk], rst_b], writes=[o_b])
                                else:
                                    if c == 0:
                                        st_ap, st_b, _ = lnst.next()
                                        sv_slots = []
                                    sg_ap, sg_b, _ = svg.next()
                                    sx_ap, sx_b, _ = svx.next()
                                    sv_slots.append((sg_ap, sg_b, sx_ap, sx_b))
                                    for g4 in range(4):
                                        P.op("act", lambda e, bk=bk, c=c, g4=g4, sg_ap=sg_ap, st_ap=st_ap: e.activation(out=sg_ap[:, g4 * 128:(g4 + 1) * 128], in_=banks[bk][:, g4 * 128:(g4 + 1) * 128], func=AF.Gelu_apprx_tanh, scale=rst_ap[:, c:c + 1], accum_out=st_ap[:, 0, c * 4 + g4:c * 4 + g4 + 1]), reads=[bankb[bk], rst_b], writes=[sg_b, st_b])
                                    for g4 in range(4):
                                        P.op("act", lambda e, c=c, g4=g4, sg_ap=sg_ap, sx_ap=sx_ap, st_ap=st_ap: e.activation(out=sx_ap[:, g4 * 128:(g4 + 1) * 128], in_=sg_ap[:, g4 * 128:(g4 + 1) * 128], func=AF.Square, accum_out=st_ap[:, 1, c * 4 + g4:c * 4 + g4 + 1]), reads=[sg_b], writes=[sx_b, st_b])
                            if g >= 10:
                                P.op("dve", lambda e, st_ap=st_ap: e.tensor_scalar(out=st_ap[:, 2, :], in0=st_ap[:, 0, :], scalar1=1.0 / 128, scalar2=None, op0=ALU.mult), reads=[st_b], writes=[st_b])
                                P.op("dve", lambda e, st_ap=st_ap: e.tensor_mul(out=st_ap[:, 3, :], in0=st_ap[:, 2, :], in1=st_ap[:, 2, :]), reads=[st_b], writes=[st_b])
                                P.op("dve", lambda e, st_ap=st_ap: e.scalar_tensor_tensor(out=st_ap[:, 3, :], in0=st_ap[:, 1, :], scalar=1.0 / 128, in1=st_ap[:, 3, :], op0=ALU.mult, op1=ALU.subtract), reads=[st_b], writes=[st_b])
                                P.op("dve", lambda e, st_ap=st_ap: e.tensor_scalar(out=st_ap[:, 4, :], in0=st_ap[:, 3, :], scalar1=EPS, scalar2=None, op0=ALU.add), reads=[st_b], writes=[st_b])
                                P.op("act", lambda e, st_ap=st_ap: e.activation(out=st_ap[:, 4, :], in_=st_ap[:, 4, :], func=AF.Sqrt), reads=[st_b], writes=[st_b])
                                P.op("dve", lambda e, st_ap=st_ap: e.reciprocal(out=st_ap[:, 4, :], in_=st_ap[:, 4, :]), reads=[st_b], writes=[st_b])
                                P.op("dve", lambda e, st_ap=st_ap: e.scalar_tensor_tensor(out=st_ap[:, 5, :], in0=st_ap[:, 2, :], scalar=-1.0, in1=st_ap[:, 4, :], op0=ALU.mult, op1=ALU.mult), reads=[st_b], writes=[st_b])
                                for c in range(NCH):
                                    sg_ap, sg_b, sx_ap, sx_b = sv_slots[c]
                                    for g4 in range(4):
                                        P.op("act", lambda e, c=c, g4=g4, sg_ap=sg_ap, sx_ap=sx_ap, st_ap=st_ap: e.activation(out=sx_ap[:, g4 * 128:(g4 + 1) * 128], in_=sg_ap[:, g4 * 128:(g4 + 1) * 128], func=AF.Identity, scale=st_ap[:, 4, c * 4 + g4:c * 4 + g4 + 1], bias=st_ap[:, 5, c * 4 + g4:c * 4 + g4 + 1]), reads=[sg_b, st_b], writes=[sx_b])
                                    P.op("pool", lambda e, sx_ap=sx_ap, col0=col0: e.tensor_mul(out=sx_ap, in0=sx_ap, in1=lng[:, col0:col0 + 512]), reads=[sx_b, Bl], writes=[sx_b])
                                    P.op("pool", lambda e, sx_ap=sx_ap, col0=col0, c=c, ot=ot: e.tensor_add(out=ot[:, c, :], in0=sx_ap, in1=lnb[:, col0:col0 + 512]), reads=[sx_b, Bl], writes=[o_b])
                            dd, db = (vtm_d, B_v) if g < 6 else ((gact_d, B_g) if g < 8 else (svln_d, B_sv))
                            P.dma("pool", dd[t0:t0 + T, col0:col0 + 512].rearrange("(c p) f -> p c f", p=128), ot, o_ds, reads=[o_b], writes=[db[t]])
                        if nx and 4 <= gi < 8:
                            p1_norm_pe(t + 1, gi - 4)

                p1_begin(0)
                p1_load(0, 0); p1_load(0, 1)
                p1_norm_elem(0, 0); p1_norm_pe(0, 0)
                p1_load(0, 2)
                p1_norm_elem(0, 1); p1_norm_pe(0, 1)
                p1_load(0, 3)
                p1_norm_elem(0, 2); p1_norm_pe(0, 2)
                p1_norm_elem(0, 3); p1_norm_pe(0, 3)
                p1_norm_fin(0)
                for t in range(NTILE):
                    if l == 0:
                        issue_casts(n=1)
                    p1_proj(t)
                phase_end()
            if stop_after == (l, 1):
                break

            if True:
                qr = Ring(P, A, 2, [8, T], BF16, "q2")
                kr = Ring(P, A, 2, [8, T], BF16, "k2")
                ktr = Ring(P, A, 2, [NCH, RW], BF16, "kt2")
                vr = Ring(P, A, 2, [NCH, RW], BF16, "v2")
                svr = Ring(P, A, 2, [NCH, SGW], BF16, "sv2")
                ur = Ring(P, A, 2, [8, T], BF16, "u2")
                Sf = A.alloc([8, DH], F32)
                Sfb2 = [A.alloc([8, DH], BF16), A.alloc([8, DH], BF16)]
                B_S = [Buf(f"S{h}") for h in range(H)]
                B_Sb2 = [[Buf(f"Sb{i}{h}") for h in range(H)] for i in range(2)]
                par = [0]
                ptr_ = Ring(P, A, 2, [H, 128], BF16, "PT", dsem=False)
                qfr = Ring(P, A, 2, [8, 128], BF16, "qf", dsem=False)
                kfr = Ring(P, A, 2, [RW], BF16, "kf", dsem=False)
                yp_sb = A.alloc([NCH, RW], F32); Byp = Buf("yp_sb"); dyp = P.new_dsem("dyp")
                ysg_sb = A.alloc([8, T], BF16); Bysg = Buf("ysg_sb"); dysg = P.new_dsem("dysg")
                BK_SC, BK_Y, BK_S, BK_Z = 0, (1, 2), (3, 4), (5, 6)
                maskv = maskT.rearrange("p h i -> p (h i)")

                def p2_load(t):
                    t0 = t * T
                    q_ap, q_b, q_ds = qr.next()
                    P.dma("sp", q_ap, qT_d[:, t0:t0 + T].rearrange("(b p) s -> p b s", p=128), q_ds, reads=[B_q[t]], writes=[q_b])
                    k_ap, k_b, k_ds = kr.next()
                    P.dma("sp", k_ap, kT_d[:, t0:t0 + T].rearrange("(b p) s -> p b s", p=128), k_ds, reads=[B_k[t]], writes=[k_b])
                    kt_ap, kt_b, kt_ds = ktr.next()
                    P.dma("sp", kt_ap, ktm_d[t0:t0 + T, :].rearrange("(c p) f -> p c f", p=128), kt_ds, reads=[B_ktm[t]], writes=[kt_b])
                    v_ap, v_b, v_ds = vr.next()
                    P.dma("sp", v_ap, vtm_d[t0:t0 + T, :].rearrange("(c p) f -> p c f", p=128), v_ds, reads=[B_v[t]], writes=[v_b])
                    sv_ap, sv_b, sv_ds = svr.next()
                    P.dma("sp", sv_ap, svln_d[t0:t0 + T, :].rearrange("(c p) f -> p c f", p=128), sv_ds, reads=[B_sv[t]], writes=[sv_b])
                    u_ap, u_b, u_ds = ur.next()
                    P.dma("sp", u_ap, uT_d[:, t0:t0 + T].rearrange("(b p) s -> p b s", p=128), u_ds, reads=[B_u[t]], writes=[u_b])
                    return (q_ap, q_b, k_ap, k_b, kt_ap, kt_b, v_ap, v_b, sv_ap, sv_b, u_ap, u_b)

                def state_update(Sx, Sxb, BS, BSb, kdec_ap, cd0, kt_ap, kt_b, v_ap, v_b, c, kfr):
                    kf_ap, kf_b, _ = kfr.next()
                    for h in range(H):
                        P.op("act", lambda e, h=h: e.activation(out=kf_ap[:, h * DH:(h + 1) * DH], in_=kt_ap[:, c, h * DH:(h + 1) * DH], func=AF.Copy, scale=kdec_ap[:, h:h + 1]), reads=[kt_b, Bl], writes=[kf_b])
                    for h in range(H):
                        bk = BK_S[h % 2]
                        for dc in range(2):
                            P.op("pe", lambda e, h=h, dc=dc, bk=bk: e.matmul(banks[bk][:, dc * DH:(dc + 1) * DH], lhsT=kf_ap[:, h * DH + dc * 128: h * DH + (dc + 1) * 128], rhs=v_ap[:, c, h * DH:(h + 1) * DH], start=True, stop=True),
                                 reads=[kf_b, v_b], writes=[bankb[bk]], inc=(dc == 1))
                        Sh = Sx[:, 2 * h:2 * h + 2, :].rearrange("p a b -> p (a b)")
                        Shb = Sxb[:, 2 * h:2 * h + 2, :].rearrange("p a b -> p (a b)")
                        P.op("dve", lambda e, h=h, bk=bk, Sh=Sh: e.scalar_tensor_tensor(out=Sh, in0=Sh, scalar=cdec[:, cd0 + h:cd0 + h + 1], in1=banks[bk][:, :], op0=ALU.mult, op1=ALU.add), reads=[bankb[bk], Bl], writes=[BS[h]])
                        P.op("act", lambda e, Sh=Sh, Shb=Shb: e.activation(out=Shb, in_=Sh, func=AF.Copy), reads=[BS[h]], writes=[BSb[h]])

                def state_reset(Sx, Sxb, BS, BSb, zero):
                    Sa = Sx.rearrange("p a b -> p (a b)")
                    Sab = Sxb.rearrange("p a b -> p (a b)")
                    if zero:
                        P.op("pool", lambda e: e.memset(Sa, 0.0), writes=BS)
                    else:
                        P.op("dve", lambda e: e.tensor_scalar(out=Sa, in0=Sa, scalar1=carry_sb[:, 0:1], scalar2=None, op0=ALU.mult), reads=[Bc], writes=BS)
                    P.op("pool", lambda e: e.tensor_copy(out=Sab, in_=Sa), reads=BS, writes=BSb)

                def p2_chunk(t, c, bufs):
                    (q_ap, q_b, k_ap, k_b, kt_ap, kt_b, v_ap, v_b, sv_ap, sv_b, u_ap, u_b) = bufs
                    gc = t * NCH + c
                    cs = slice(c * 128, (c + 1) * 128)
                    if gc == 0:
                        state_reset(Sf, Sfb2[par[0]], B_S, B_Sb2[par[0]], True)
                    elif gc * 128 == SEG:
                        par[0] ^= 1
                        state_reset(Sf, Sfb2[par[0]], B_S, B_Sb2[par[0]], False)
                    Sfb, B_Sb = Sfb2[par[0]], B_Sb2[par[0]]
                    par[0] ^= 1
                    Sfb_n, B_Sb_n = Sfb2[par[0]], B_Sb2[par[0]]
                    for h in range(H):
                        for dc in range(2):
                            P.op("pe", lambda e, h=h, dc=dc: e.matmul(banks[BK_SC][:, h * 128:(h + 1) * 128], lhsT=k_ap[:, 2 * h + dc, cs], rhs=q_ap[:, 2 * h + dc, cs], start=(dc == 0), stop=(dc == 1)),
                                 reads=[k_b, q_b], writes=[bankb[BK_SC]], inc=(h == H - 1 and dc == 1))
                    pt_ap, pt_b, _ = ptr_.next()
                    P.op("dve", lambda e: e.tensor_tensor(out=pt_ap.rearrange("p h i -> p (h i)"), in0=banks[BK_SC][:, :], in1=maskv, op=ALU.mult), reads=[bankb[BK_SC], Bl], writes=[pt_b])
                    qf_ap, qf_b, _ = qfr.next()
                    P.op("pool", lambda e: e.tensor_tensor(out=qf_ap, in0=q_ap[:, :, cs], in1=qdecf, op=ALU.mult), reads=[q_b, Bl], writes=[qf_b])
                    state_update(Sf, Sfb_n, B_S, B_Sb_n, kdecf, 0, kt_ap, kt_b, v_ap, v_b, c, kfr)
                    for g in range(G):
                        bk = BK_Z[g // 4]
                        o = banks[bk][:, (g % 4) * 128:(g % 4 + 1) * 128]
                        P.op("pe", lambda e, g=g, o=o: e.matmul(o, lhsT=sv_ap[:, c, g * 128:(g + 1) * 128], rhs=wsT_b[:, g, :], start=True, stop=False), reads=[sv_b, Bl], writes=[bankb[bk]], inc=False)
                        P.op("pe", lambda e, g=g, o=o: e.matmul(o, lhsT=ones[0:1, :], rhs=bs_b[0:1, g * 128:(g + 1) * 128], start=False, stop=True), reads=[Bc, Bl], writes=[bankb[bk]], inc=(g % 4 == 3))
                    for zb in range(2):
                        bk = BK_Z[zb]
                        P.op("dve", lambda e, zb=zb, bk=bk: e.tensor_tensor(out=ysg_sb[:, zb * 4:(zb + 1) * 4, cs], in0=banks[bk][:, :].rearrange("p (g i) -> p g i", g=4), in1=u_ap[:, zb * 4:(zb + 1) * 4, cs], op=ALU.mult), reads=[bankb[bk], u_b], writes=[Bysg])
                    for h in range(H):
                        bk = BK_Y[h // 2]
                        o = banks[bk][:, (h % 2) * DH:(h % 2 + 1) * DH]
                        P.op("pe", lambda e, h=h, o=o: e.matmul(o, lhsT=pt_ap[:, h, :], rhs=v_ap[:, c, h * DH:(h + 1) * DH], start=True, stop=False), reads=[pt_b, v_b], writes=[bankb[bk]], inc=False)
                        P.op("pe", lambda e, h=h, o=o: e.matmul(o, lhsT=qf_ap[:, 2 * h, :], rhs=Sfb[:, 2 * h, :], start=False, stop=False), reads=[qf_b, B_Sb[h]], writes=[bankb[bk]], inc=False)
                        P.op("pe", lambda e, h=h, o=o: e.matmul(o, lhsT=qf_ap[:, 2 * h + 1, :], rhs=Sfb[:, 2 * h + 1, :], start=False, stop=True), reads=[qf_b, B_Sb[h]], writes=[bankb[bk]], inc=True)
                    for half in range(2):
                        bk = BK_Y[half]
                        P.op("act", lambda e, half=half, bk=bk: e.activation(out=yp_sb[:, c, half * 512:(half + 1) * 512], in_=banks[bk][:, :], func=AF.Copy), reads=[bankb[bk]], writes=[Byp])

                nxt_bufs = p2_load(0)
                for t in range(NTILE):
                    cur = nxt_bufs
                    if t + 1 < NTILE:
                        nxt_bufs = p2_load(t + 1)
                    if l == 0:
                        issue_casts(n=1)
                    for c in range(NCH):
                        p2_chunk(t, c, cur)
                    t0 = t * T
                    P.dma("pool", ypart_d[t0:t0 + T, :].rearrange("(c p) f -> p c f", p=128), yp_sb, dyp, reads=[Byp], writes=[B_yp[t]])
                    P.dma("pool", ymixT_d[RW:2 * RW, t0:t0 + T].rearrange("(b p) s -> p b s", p=128), ysg_sb, dysg, reads=[Bysg], writes=[B_yms[t]])
                phase_end()
            if stop_after == (l, 2):
                break

            if True:
                qr = Ring(P, A, 2, [8, T], BF16, "q3")
                ktr = Ring(P, A, 2, [NCH, RW], BF16, "kt3")
                vr = Ring(P, A, 2, [NCH, RW], BF16, "v3")
                ypr = Ring(P, A, 2, [NCH, RW], F32, "yp3")
                gr = Ring(P, A, 2, [NCH, RW], BF16, "g3")
                Sb_ = A.alloc([8, DH], F32)
                Sbb2 = [A.alloc([8, DH], BF16), A.alloc([8, DH], BF16)]
                B_S = [Buf(f"S{h}") for h in range(H)]
                B_Sb2 = [[Buf(f"Sb{i}{h}") for h in range(H)] for i in range(2)]
                par = [0]
                kfr = Ring(P, A, 2, [RW], BF16, "kb", dsem=False)
                yr_ = Ring(P, A, 3, [RW], F32, "y3", dsem=False)
                ynr = Ring(P, A, 2, [RW], BF16, "yn3", dsem=False)
                junk = A.alloc([DH], F32); Bjunk = Buf("junk")
                ssr = Ring(P, A, 2, [H], F32, "ss3", dsem=False)
                yret_sb = A.alloc([8, T], BF16); Byret = Buf("yret_sb"); dyret = P.new_dsem("dyret")
                BK_Y, BK_S, BK_T = (1, 2), (3, 4), (5, 6)
                tr_i = [0]

                def p3_load(t):
                    t0 = t * T
                    q_ap, q_b, q_ds = qr.next()
                    P.dma("sp", q_ap, qT_d[:, t0:t0 + T].rearrange("(b p) s -> p b s", p=128), q_ds, reads=[B_q[t]], writes=[q_b])
                    kt_ap, kt_b, kt_ds = ktr.next()
                    P.dma("sp", kt_ap, ktm_d[t0:t0 + T, :].rearrange("(c p) f -> p c f", p=128), kt_ds, reads=[B_ktm[t]], writes=[kt_b])
                    v_ap, v_b, v_ds = vr.next()
                    P.dma("sp", v_ap, vtm_d[t0:t0 + T, :].rearrange("(c p) f -> p c f", p=128), v_ds, reads=[B_v[t]], writes=[v_b])
                    yp_ap, yp_b, yp_ds = ypr.next()
                    P.dma("sp", yp_ap, ypart_d[t0:t0 + T, :].rearrange("(c p) f -> p c f", p=128), yp_ds, reads=[B_yp[t]], writes=[yp_b])
                    g_ap, g_b, g_ds = gr.next()
                    P.dma("sp", g_ap, gact_d[t0:t0 + T, :].rearrange("(c p) f -> p c f", p=128), g_ds, reads=[B_g[t]], writes=[g_b])
                    return (q_ap, q_b, kt_ap, kt_b, v_ap, v_b, yp_ap, yp_b, g_ap, g_b)

                def p3_chunk(t, c, bufs):
                    (q_ap, q_b, kt_ap, kt_b, v_ap, v_b, yp_ap, yp_b, g_ap, g_b) = bufs
                    gc = t * NCH + c
                    cs = slice(c * 128, (c + 1) * 128)
                    if gc == NTILE * NCH - 1:
                        state_reset(Sb_, Sbb2[par[0]], B_S, B_Sb2[par[0]], True)
                    elif (gc + 1) * 128 == SEG:
                        par[0] ^= 1
                        state_reset(Sb_, Sbb2[par[0]], B_S, B_Sb2[par[0]], False)
                    Sbb, B_Sb = Sbb2[par[0]], B_Sb2[par[0]]
                    par[0] ^= 1
                    state_update(Sb_, Sbb2[par[0]], B_S, B_Sb2[par[0]], kdecb, 4, kt_ap, kt_b, v_ap, v_b, c, kfr)
                    y_ap, y_b, _ = yr_.next()
                    for h in range(H):
                        bk = BK_Y[h // 2]
                        o = banks[bk][:, (h % 2) * DH:(h % 2 + 1) * DH]
                        for dc in range(2):
                            P.op("pe", lambda e, h=h, dc=dc, o=o: e.matmul(o, lhsT=q_ap[:, 2 * h + dc, cs], rhs=Sbb[:, 2 * h + dc, :], start=(dc == 0), stop=(dc == 1)), reads=[q_b, B_Sb[h]], writes=[bankb[bk]], inc=(dc == 1))
                        P.op("dve", lambda e, h=h, o=o: e.scalar_tensor_tensor(out=y_ap[:, h * DH:(h + 1) * DH], in0=o, scalar=qdecb[:, h:h + 1], in1=yp_ap[:, c, h * DH:(h + 1) * DH], op0=ALU.mult, op1=ALU.add), reads=[bankb[bk], yp_b, Bl], writes=[y_b])
                    return (y_ap, y_b, g_ap, g_b, c, cs)

                def p3_tail(st_):
                    (y_ap, y_b, g_ap, g_b, c, cs) = st_
                    ss_ap, ss_b, _ = ssr.next()
                    for h in range(H):
                        P.op("act", lambda e, h=h: e.activation(out=junk, in_=y_ap[:, h * DH:(h + 1) * DH], func=AF.Square, accum_out=ss_ap[:, h:h + 1]), reads=[y_b], writes=[Bjunk, ss_b])
                    rstd_from_psum(ss_ap, ss_ap, DH, [ss_b], [ss_b])
                    yn_ap, yn_b, _ = ynr.next()
                    for h in range(H):
                        P.op("dve", lambda e, h=h: e.scalar_tensor_tensor(out=yn_ap[:, h * DH:(h + 1) * DH], in0=y_ap[:, h * DH:(h + 1) * DH], scalar=ss_ap[:, h:h + 1], in1=g_ap[:, c, h * DH:(h + 1) * DH], op0=ALU.mult, op1=ALU.mult), reads=[y_b, ss_b, g_b], writes=[yn_b])
                    bk = BK_T[tr_i[0] % 2]; tr_i[0] += 1
                    pv = banks[bk][:, :].bitcast(BF16)
                    for blk in range(8):
                        P.op("pe", lambda e, blk=blk, pv=pv: e.transpose(out=pv[:, blk * 128:(blk + 1) * 128], in_=yn_ap[:, blk * 128:(blk + 1) * 128], identity=ident), reads=[yn_b, Bc], writes=[bankb[bk]], inc=(blk == 7))
                    P.op("act", lambda e, pv=pv: e.activation(out=yret_sb[:, :, cs], in_=pv.rearrange("p (b i) -> p b i", b=8), func=AF.Copy), reads=[bankb[bk]], writes=[Byret])

                nxt_bufs = p3_load(NTILE - 1)
                pend_tail = []

                def flush_tail(keep):
                    while len(pend_tail) > keep:
                        st_, t_, c_ = pend_tail.pop(0)
                        p3_tail(st_)
                        if c_ == 0:
                            t0 = t_ * T
                            P.dma("pool", ymixT_d[0:RW, t0:t0 + T].rearrange("(b p) s -> p b s", p=128), yret_sb, dyret, reads=[Byret], writes=[B_ymr[t_]])

                for t in range(NTILE - 1, -1, -1):
                    cur = nxt_bufs
                    if l == 0:
                        issue_casts(n=1)
                    for c in range(NCH - 1, -1, -1):
                        st_ = p3_chunk(t, c, cur)
                        flush_tail(0)
                        pend_tail.append((st_, t, c))
                        if c == NCH - 1 and t - 1 >= 0:
                            nxt_bufs = p3_load(t - 1)
                flush_tail(0)
                phase_end()
            if stop_after == (l, 3):
                break

            if True:
                last_layer = (l == DEPTH - 1)
                h_sb = A.alloc([KC, T], F32); Bh = [Buf(f"h{i}") for i in range(KC)]; dhq = [P.new_dsem(f"dh4{q}") for q in range(4)]; dhsq = [P.new_dsem(f"dhs4{q}") for q in range(4)]
                ym_sb = A.alloc([KC, T], BF16); Bym = Buf("ym"); dym = P.new_dsem("dym4")
                hg_sb = A.alloc([KC, T], BF16); Bhg = Buf("hg4")
                aT = A.alloc([FC, T], BF16); BaT = [Buf(f"aT{i}") for i in range(FC)]
                wr = Ring(P, A, 6, [KC * 256], BF16, "w4")
                dp = P.new_dsem("dp4")
                p_b = A.alloc([2, T], BF16); Bpb = Buf("pb")
                rsb_ap = A.alloc([T], F32); rsb_b = Buf("rsb4")
                sqr = Ring(P, A, 2, [T], BF16, "sq4", dsem=False)
                tgr = Ring(P, A, 4, [T], F32, "tf4", dsem=False)
                sfr = tgr
                mr = tgr
                sgr = Ring(P, A, 4, [T], BF16, "tb4", dsem=False)
                tur = sgr
                outr = Ring(P, A, 2, [2, T], F32, "o4") if last_layer else None
                pacc = [1, 2, 3, 4, 5, 6, 7]
                pacc_i = [0]

                def next_acc():
                    b = pacc[pacc_i[0] % len(pacc)]
                    pacc_i[0] += 1
                    return b

                wq = []
                wsched = []
                for t in range(NTILE):
                    wsched += [("w_out", g, None) for g in range(8)]
                    if p4_stage == 1:
                        continue
                    for g in range(22):
                        wsched += [("w_ffn_gate", g, None), ("w_ffn_up", g, None)]
                    for g in range(16):
                        wsched += [("w_ffn_down", g, 0), ("w_ffn_down", g, 1)]
                    if p4_stage == 2:
                        continue
                    for g in range(8):
                        wsched += [("w_ple_gate", g, None), ("w_ple_proj", g, None)]
                wi = [0]

                def wprefetch():
                    if wi[0] < len(wsched):
                        name, g, half = wsched[wi[0]]
                        wi[0] += 1
                        ap, b, ds = wr.next()
                        off, kc_, gw, ng = WOFF[name]
                        src = wview(l, name, g)
                        if half is not None:
                            kc_ = kc_ // 2
                            src = src[:, half * kc_ * gw:(half + 1) * kc_ * gw]
                        P.dma("sp", ap[:, 0:kc_ * gw], src, ds, reads=[B_w[(l, name)]], writes=[b])
                        wq.append((ap[:, 0:kc_ * gw].rearrange("p (k g) -> p k g", k=kc_), b))

                def wnext():
                    return wq.pop(0)

                def norm_block(nb, gidx, pend):
                    sq_ap, sq_b, _ = sqr.next()
                    P.op("act", lambda e, nb=nb, sq_ap=sq_ap: e.activation(out=sq_ap, in_=h_sb[:, nb, :], func=AF.Square), reads=[Bh[nb]], writes=[sq_b])
                    P.op("act", lambda e, nb=nb: e.activation(out=hg_sb[:, nb, :], in_=h_sb[:, nb, :], func=AF.Copy, scale=gain_ap(gidx, nb)), reads=[Bh[nb], Bc], writes=[Bhg])
                    pend.append((nb, sq_ap, sq_b))

                def norm_pe(pend, keep):
                    while len(pend) > keep:
                        nb, sq_ap, sq_b = pend.pop(0)
                        P.op("pe", lambda e, nb=nb, sq_ap=sq_ap: e.matmul(banks[0][:, :], lhsT=ones, rhs=sq_ap, start=(nb == 0), stop=(nb == KC - 1)), reads=[sq_b, Bc], writes=[bankb[0]], inc=True)

                def dbg_store(t0):
                    for q4 in range(4):
                        P.dma("pool", hT[q4 * 512:(q4 + 1) * 512, t0:t0 + T].rearrange("(k p) s -> p k s", p=128), h_sb[:, q4 * 4:(q4 + 1) * 4, :], dhsq[q4],
                              reads=Bh[q4 * 4:(q4 + 1) * 4])
                        if t0 // T + 1 < NTILE:
                            h_load(t0 // T + 1, q4)

                def h_load(t_, q4):
                    P.dma("pool", h_sb[:, q4 * 4:(q4 + 1) * 4, :], src_h[q4 * 512:(q4 + 1) * 512, t_ * T:(t_ + 1) * T].rearrange("(k p) s -> p k s", p=128), dhq[q4],
                          reads=[B_hT[t_]] if l > 0 else [], writes=Bh[q4 * 4:(q4 + 1) * 4])

                def ym_load(t_):
                    for q2 in range(2):
                        P.dma("sp", ym_sb[:, q2 * 8:(q2 + 1) * 8, :], ymixT_d[q2 * 1024:(q2 + 1) * 1024, t_ * T:(t_ + 1) * T].rearrange("(k p) s -> p k s", p=128), dym,
                              reads=[B_ymr[t_], B_yms[t_]], writes=[Bym])

                def p4_tile(t):
                    t0 = t * T
                    if t == 0:
                        ym_load(0)
                        for q4 in range(4):
                            h_load(0, q4)
                    P.dma("pool", p_b, pT[l * PLE:(l + 1) * PLE, t0:t0 + T].rearrange("(k p) s -> p k s", p=128), dp, writes=[Bpb])
                    pend = []
                    for g in range(8):
                        w_ap, w_b = wnext()
                        for blk in range(2):
                            nb = g * 2 + blk
                            bk = next_acc()
                            for kc in range(KC):
                                P.op("pe", lambda e, bk=bk, blk=blk, kc=kc, w_ap=w_ap: e.matmul(banks[bk][:, :], lhsT=w_ap[:, kc, blk * 128:(blk + 1) * 128], rhs=ym_sb[:, kc, :], start=(kc == 0), stop=(kc == KC - 1)),
                                     reads=[w_b, Bym], writes=[bankb[bk]], inc=(kc == KC - 1))
                            P.op("dve", lambda e, bk=bk, nb=nb: e.tensor_add(out=h_sb[:, nb, :], in0=banks[bk][:, :], in1=h_sb[:, nb, :]), reads=[bankb[bk]], writes=[Bh[nb]])
                            norm_pe(pend, 0)
                            norm_block(nb, l * 3 + 1, pend)
                        wprefetch()
                    norm_pe(pend, 0)
                    rstd_from_psum(banks[0][:, :], rsb_ap, D, [bankb[0]], [rsb_b])
                    if t + 1 < NTILE:
                        ym_load(t + 1)
                    issue_casts(n=4)
                    if p4_stage == 1:
                        return dbg_store(t0)
                    for g in range(22):
                        wg_ap, wg_b = wnext()
                        wu_ap, wu_b = wnext()
                        for blk in range(2):
                            fb = g * 2 + blk
                            bg = next_acc()
                            for kc in range(KC):
                                P.op("pe", lambda e, bg=bg, blk=blk, kc=kc, wg_ap=wg_ap: e.matmul(banks[bg][:, :], lhsT=wg_ap[:, kc, blk * 128:(blk + 1) * 128], rhs=hg_sb[:, kc, :], start=(kc == 0), stop=(kc == KC - 1)),
                                     reads=[wg_b, Bhg], writes=[bankb[bg]], inc=(kc == KC - 1))
                            bu = next_acc()
                            for kc in range(KC):
                                P.op("pe", lambda e, bu=bu, blk=blk, kc=kc, wu_ap=wu_ap: e.matmul(banks[bu][:, :], lhsT=wu_ap[:, kc, blk * 128:(blk + 1) * 128], rhs=hg_sb[:, kc, :], start=(kc == 0), stop=(kc == KC - 1)),
                                     reads=[wu_b, Bhg], writes=[bankb[bu]], inc=(kc == KC - 1))
                            tg_ap, tg_b, _ = tgr.next()
                            sg_ap, sg_b, _ = sgr.next()
                            tu_ap, tu_b, _ = tur.next()
                            P.op("dve", lambda e, bg=bg, tg_ap=tg_ap: e.tensor_mul(out=tg_ap, in0=banks[bg][:, :], in1=rsb_ap), reads=[bankb[bg], rsb_b], writes=[tg_b])
                            P.op("act", lambda e, tg_ap=tg_ap, sg_ap=sg_ap: e.activation(out=sg_ap, in_=tg_ap, func=AF.Silu), reads=[tg_b], writes=[sg_b])
                            P.op("dve", lambda e, bu=bu, tu_ap=tu_ap: e.tensor_mul(out=tu_ap, in0=banks[bu][:, :], in1=rsb_ap), reads=[bankb[bu], rsb_b], writes=[tu_b])
                            P.op("pool", lambda e, fb=fb, sg_ap=sg_ap, tu_ap=tu_ap: e.tensor_mul(out=aT[:, fb, :], in0=sg_ap, in1=tu_ap), reads=[sg_b, tu_b], writes=[BaT[fb]])
                        wprefetch(); wprefetch()
                    pend = []
                    for nb in range(KC):
                        bk = next_acc()
                        for half in range(2):
                            w_ap, w_b = wnext()
                            for f2 in range(FC // 2):
                                fb = half * (FC // 2) + f2
                                P.op("pe", lambda e, bk=bk, fb=fb, f2=f2, w_ap=w_ap: e.matmul(banks[bk][:, :], lhsT=w_ap[:, f2, :], rhs=aT[:, fb, :], start=(fb == 0), stop=(fb == FC - 1)),
                                     reads=[w_b, BaT[fb]], writes=[bankb[bk]], inc=(fb == FC - 1 or f2 == FC // 2 - 1))
                            wprefetch()
                        P.op("dve", lambda e, bk=bk, nb=nb: e.tensor_add(out=h_sb[:, nb, :], in0=banks[bk][:, :], in1=h_sb[:, nb, :]), reads=[bankb[bk]], writes=[Bh[nb]])
                        norm_pe(pend, 0)
                        norm_block(nb, l * 3 + 2, pend)
                    norm_pe(pend, 0)
                    rstd_from_psum(banks[0][:, :], rsb_ap, D, [bankb[0]], [rsb_b])
                    if p4_stage == 2:
                        return dbg_store(t0)
                    pend = []
                    for g in range(8):
                        wg_ap, wg_b = wnext()
                        wp_ap, wp_b = wnext()
                        for blk in range(2):
                            nb = g * 2 + blk
                            bg = next_acc()
                            for kc in range(KC):
                                P.op("pe", lambda e, bg=bg, blk=blk, kc=kc, wg_ap=wg_ap: e.matmul(banks[bg][:, :], lhsT=wg_ap[:, kc, blk * 128:(blk + 1) * 128], rhs=hg_sb[:, kc, :], start=(kc == 0), stop=(kc == KC - 1)),
                                     reads=[wg_b, Bhg], writes=[bankb[bg]], inc=(kc == KC - 1))
                            bp = next_acc()
                            for kc in range(2):
                                P.op("pe", lambda e, bp=bp, blk=blk, kc=kc, wp_ap=wp_ap: e.matmul(banks[bp][:, :], lhsT=wp_ap[:, kc, blk * 128:(blk + 1) * 128], rhs=p_b[:, kc, :], start=(kc == 0), stop=(kc == 1)),
                                     reads=[wp_b, Bpb], writes=[bankb[bp]], inc=(kc == 1))
                            tg_ap, tg_b, _ = tgr.next()
                            sf_ap, sf_b, _ = sfr.next()
                            m_ap, m_b, _ = mr.next()
                            P.op("dve", lambda e, bg=bg, tg_ap=tg_ap: e.tensor_mul(out=tg_ap, in0=banks[bg][:, :], in1=rsb_ap), reads=[bankb[bg], rsb_b], writes=[tg_b])
                            P.op("act", lambda e, tg_ap=tg_ap, sf_ap=sf_ap: e.activation(out=sf_ap, in_=tg_ap, func=AF.Sigmoid), reads=[tg_b], writes=[sf_b])
                            P.op("dve", lambda e, bp=bp, m_ap=m_ap, sf_ap=sf_ap: e.tensor_mul(out=m_ap, in0=banks[bp][:, :], in1=sf_ap), reads=[bankb[bp], sf_b], writes=[m_b])
                            P.op("pool", lambda e, nb=nb, m_ap=m_ap: e.tensor_add(out=h_sb[:, nb, :], in0=h_sb[:, nb, :], in1=m_ap), reads=[m_b], writes=[Bh[nb]])
                            if not last_layer and nb % 4 == 3:
                                q4 = nb // 4
                                P.dma("pool", hT[q4 * 512:(q4 + 1) * 512, t0:t0 + T].rearrange("(k p) s -> p k s", p=128), h_sb[:, q4 * 4:(q4 + 1) * 4, :], dhsq[q4],
                                      reads=Bh[q4 * 4:(q4 + 1) * 4], writes=[B_hT[t]])
                                if t + 1 < NTILE:
                                    h_load(t + 1, q4)
                            if last_layer and p4_stage is None:
                                sq_ap, sq_b, _ = sqr.next()
                                P.op("act", lambda e, nb=nb, sq_ap=sq_ap: e.activation(out=sq_ap, in_=h_sb[:, nb, :], func=AF.Square), reads=[Bh[nb]], writes=[sq_b])
                                norm_pe(pend, 0)
                                pend.append((nb, sq_ap, sq_b))
                        wprefetch(); wprefetch()
                    if last_layer and p4_stage is None:
                        norm_pe(pend, 0)
                        rstd_from_psum(banks[0][:, :], rsb_ap, D, [bankb[0]], [rsb_b])
                        for q8 in range(8):
                            o_ap, o_b, o_ds = outr.next()
                            for j in range(2):
                                nb = q8 * 2 + j
                                P.op("dve", lambda e, nb=nb, j=j, o_ap=o_ap: e.scalar_tensor_tensor(out=o_ap[:, j, :], in0=h_sb[:, nb, :], scalar=gain_ap(DEPTH * 3, nb), in1=rsb_ap, op0=ALU.mult, op1=ALU.mult), reads=[Bh[nb], rsb_b, Bc], writes=[o_b])
                            P.dma("pool", yT[q8 * 256:(q8 + 1) * 256, t0:t0 + T].rearrange("(k p) s -> p k s", p=128), o_ap, o_ds, reads=[o_b])
                            if q8 % 2 == 1 and t + 1 < NTILE:
                                h_load(t + 1, q8 // 2)


                issue_casts(layer=l)
                for _ in range(6):
                    wprefetch()
                for t in range(NTILE):
                    p4_tile(t)
                issue_casts(layer=DEPTH - 1)
                phase_end()
            if stop_after == (l, 4):
                break

        P.barrier()
        blk = st.enter_context(nc.Block())
        P.emit(blk)
    return nc


def _rearr(W, gw):
    K, N = W.shape
    return np.ascontiguousarray(W.reshape(K // 128, 128, N // gw, gw).transpose(2, 1, 0, 3)).reshape(-1)


def prep_shared(inp, DEPTH):
    f32 = np.float32
    wcat = np.empty((DEPTH * WTOT,), f32)
    for l in range(DEPTH):
        for n, k_, nn_, g_ in WSPEC:
            off = l * WTOT + WOFF[n][0]
            wcat[off: off + k_ * nn_] = _rearr(np.asarray(inp[n][l], f32), g_)
    gl = []
    for l in range(DEPTH):
        for nm in ("norm_mix_g", "norm_ffn_g", "norm_ple_g"):
            gl.append(np.asarray(inp[nm][l], f32).reshape(KC, 128).T)
    gl.append(np.asarray(inp["norm_final_g"], f32).reshape(KC, 128).T)
    gains = np.ascontiguousarray(np.concatenate(gl, axis=1))
    dec = np.ascontiguousarray(np.broadcast_to(np.asarray(inp["ret_decay"], f32)[:DEPTH].reshape(1, DEPTH * 8), (128, DEPTH * 8)))
    sg = []
    for l in range(DEPTH):
        sg.append(np.asarray(inp["sg_ln_g"][l], f32)); sg.append(np.asarray(inp["sg_ln_b"][l], f32))
    sgln = np.ascontiguousarray(np.broadcast_to(np.concatenate(sg)[None, :], (128, DEPTH * 2 * SGW)))
    wsT = np.ascontiguousarray(np.asarray(inp["sg_w"], f32)[:DEPTH].transpose(3, 0, 1, 2)).reshape(128, DEPTH * G * 128)
    bsr = np.ascontiguousarray(np.asarray(inp["sg_b"], f32)[:DEPTH].reshape(1, DEPTH * SGW))
    half = 128
    invf = (np.float32(10000.0) ** (-np.arange(half, dtype=f32) / np.float32(half))).astype(f32).reshape(128, 1)
    return dict(wcat=wcat, gains=gains, dec=dec, sgln=sgln, wsT=wsT, bsr=bsr, invf=invf)


def prep_core(x_rows, p_rows, pos, carry_flag, DEPTH):
    f32 = np.float32
    NT = x_rows.shape[0]
    xT = np.ascontiguousarray(x_rows.T)
    pT = np.ascontiguousarray(p_rows.transpose(0, 2, 1)).reshape(DEPTH * PLE, NT)
    posb = np.ascontiguousarray(np.broadcast_to(pos.astype(f32)[None, :], (128, NT)))
    carry = np.full((128, 1), carry_flag, f32)
    return dict(xT=xT, pT=pT, posb=posb, carry=carry)


_NC_CACHE = {}


def kernel(x_prompt, x_sample, p_prompt, p_sample, norm_mix_g, w_in, ret_decay, sg_ln_g, sg_ln_b,
           sg_w, sg_b, w_out, norm_ffn_g, w_ffn_gate, w_ffn_up, w_ffn_down, norm_ple_g, w_ple_gate,
           w_ple_proj, norm_final_g):
    DEPTH = 2
    NT = 8192
    SEG = NT // 2
    f32 = np.float32
    x_prompt = np.asarray(x_prompt, f32); x_sample = np.asarray(x_sample, f32)
    p_prompt = np.asarray(p_prompt, f32); p_sample = np.asarray(p_sample, f32)
    inp = dict(norm_mix_g=norm_mix_g, w_in=w_in, ret_decay=ret_decay, sg_ln_g=sg_ln_g, sg_ln_b=sg_ln_b, sg_w=sg_w,
               sg_b=sg_b, w_out=w_out, norm_ffn_g=norm_ffn_g, w_ffn_gate=w_ffn_gate, w_ffn_up=w_ffn_up,
               w_ffn_down=w_ffn_down, norm_ple_g=norm_ple_g, w_ple_gate=w_ple_gate, w_ple_proj=w_ple_proj,
               norm_final_g=norm_final_g)
    shared = prep_shared(inp, DEPTH)
    in_maps = []
    plan = {0: ("p", 0), 1: ("s2", 0, 1), 2: ("s1", 2), 3: ("s1", 3), 4: ("p", 1), 5: ("s2", 4, 5), 6: ("s1", 6), 7: ("s1", 7)}
    pos2 = np.concatenate([np.arange(SEG), np.arange(SEG)])
    for c in range(8):
        pl = plan[c]
        if pl[0] == "p":
            xr = x_prompt[pl[1]]; pr = p_prompt[:, pl[1]]; pos = np.arange(NT); cf = 1.0
        elif pl[0] == "s2":
            xr = np.concatenate([x_sample[pl[1]], x_sample[pl[2]]], axis=0)
            pr = np.concatenate([p_sample[:, pl[1]], p_sample[:, pl[2]]], axis=1); pos = pos2; cf = 0.0
        else:
            xr = np.concatenate([x_sample[pl[1]], np.zeros((SEG, D), f32)], axis=0)
            pr = np.concatenate([p_sample[:, pl[1]], np.zeros((DEPTH, SEG, PLE), f32)], axis=1); pos = pos2; cf = 0.0
        m = dict(shared)
        m.update(prep_core(xr, pr, pos, cf, DEPTH))
        in_maps.append(m)
    if "nc" not in _NC_CACHE:
        _NC_CACHE["nc"] = build(NT, DEPTH=DEPTH)
    res = run_bass_kernel_spmd(_NC_CACHE["nc"], in_maps, core_ids=list(range(8)))
    outs = [np.asarray(r["yT"]) for r in res.results]
    y_prompt = np.empty((2, NT, D), f32)
    y_sample = np.empty((8, SEG, D), f32)
    for c in range(8):
        pl = plan[c]
        o = outs[c].T
        if pl[0] == "p":
            y_prompt[pl[1]] = o
        elif pl[0] == "s2":
            y_sample[pl[1]] = o[:SEG]; y_sample[pl[2]] = o[SEG:]
        else:
            y_sample[pl[1]] = o[:SEG]
    return (y_prompt, y_sample)
```

```python
import math
from contextlib import ExitStack

import numpy as np
import concourse.bass as bass
import concourse.mybir as mybir
from concourse.bass_utils import run_bass_kernel_spmd

F32 = mybir.dt.float32
BF16 = mybir.dt.bfloat16
U8 = mybir.dt.uint8
I32 = mybir.dt.int32
AF = mybir.ActivationFunctionType
ALU = mybir.AluOpType
DTSIZE = {F32: 4, BF16: 2, U8: 1, I32: 4}

D = 2048
KC = D // 128
RW = 1024
H = 4
DH = 256
SGW = 1024
G = 8
DFF = 5632
FC = DFF // 128
PLE = 256
INW = 6144
EPS = 1e-6
T = 512
NCH = T // 128
ARENA = 210000

WSPEC = [("w_in", D, INW, 512), ("w_out", D, D, 256), ("w_ffn_gate", D, DFF, 256), ("w_ffn_up", D, DFF, 256),
         ("w_ffn_down", DFF, D, 128), ("w_ple_gate", D, D, 256), ("w_ple_proj", PLE, D, 256)]
WOFF = {}
_o = 0
for _n, _k, _nn, _g in WSPEC:
    WOFF[_n] = (_o, _k // 128, _g, _nn // _g)
    _o += _k * _nn
WTOT = _o


class Buf:
    __slots__ = ("name", "w", "r")

    def __init__(self, name=""):
        self.name = name
        self.w = None
        self.r = {}


class Sem:
    def __init__(self, h, key):
        self.h = h
        self.key = key
        self.cnt = 0


class Eng:
    def __init__(self, name, sem):
        self.name = name
        self.sem = sem
        self.prog = []
        self.waited = {}
        self.pending = False


class Prog:
    def __init__(self, nc, stack, same_engine_sync=False):
        self.nc = nc
        self.stack = stack
        self.sems = {}
        self.nsem = 0
        self.same_engine_sync = same_engine_sync
        self.eng = {}
        for n in ("pe", "act", "dve", "pool", "sp"):
            self.eng[n] = Eng(n, self.new_sem("e_" + n))
        self.dsems = []
        self.free_dsems = []
        self.phase_dsems = []

    def new_sem(self, name):
        h = self.stack.enter_context(self.nc.semaphore(name))
        s = Sem(h, self.nsem)
        self.sems[s.key] = s
        self.nsem += 1
        return s

    def new_dsem(self, name, persistent=False):
        if not persistent and self.free_dsems:
            s = self.free_dsems.pop()
        else:
            s = self.new_sem(name)
            self.dsems.append(s)
        if not persistent:
            self.phase_dsems.append(s)
        return s

    def release_phase_dsems(self):
        self.free_dsems.extend(self.phase_dsems)
        self.phase_dsems = []

    def _wait(self, E, tok, skip_key=None):
        key, val = tok
        if key == skip_key:
            return
        if key == E.sem.key and (not self.same_engine_sync or val > E.sem.cnt or E.name in ("pe", "sp")):
            return
        if E.waited.get(key, 0) >= val:
            return
        E.waited[key] = val
        E.prog.append(("wait", self.sems[key].h, val))

    def _deps(self, E, reads, writes, skip_key=None):
        for b in reads:
            if b.w is not None:
                self._wait(E, b.w, skip_key)
        for b in writes:
            if b.w is not None:
                self._wait(E, b.w, skip_key)
            for k, v in b.r.items():
                self._wait(E, (k, v), skip_key)

    def _update(self, tok, reads, writes):
        for b in writes:
            b.w = tok
            b.r = {}
        k, v = tok
        for b in reads:
            if b.r.get(k, 0) < v:
                b.r[k] = v

    def op(self, en, emit, reads=(), writes=(), inc=True):
        E = self.eng[en]
        self._deps(E, reads, writes)
        if inc:
            E.sem.cnt += 1
            E.pending = False
            E.prog.append(("ins", emit, E.sem.h))
            tok = (E.sem.key, E.sem.cnt)
        else:
            E.pending = True
            E.prog.append(("ins", emit, None))
            tok = (E.sem.key, E.sem.cnt + 1)
        self._update(tok, reads, writes)

    def dma(self, qn, out, in_, ds, reads=(), writes=()):
        E = self.eng[qn]
        self._deps(E, reads, writes, skip_key=ds.key)
        ds.cnt += 16
        E.prog.append(("dma", out, in_, ds.h))
        self._update((ds.key, ds.cnt), reads, writes)

    def cc(self, kind, groups, in_ap, out_ap, ds, reads=(), writes=()):
        E = self.eng["pool"]
        self._deps(E, reads, writes, skip_key=ds.key)
        ds.cnt += 16
        E.prog.append(("cc", kind, groups, in_ap, out_ap, ds.h))
        self._update((ds.key, ds.cnt), reads, writes)

    def barrier(self):
        for E in self.eng.values():
            assert not E.pending, E.name
        for E in self.eng.values():
            for E2 in self.eng.values():
                if E2 is not E and E2.sem.cnt > 0:
                    self._wait(E, (E2.sem.key, E2.sem.cnt))
            for ds in self.dsems:
                if ds.cnt > 0:
                    self._wait(E, (ds.key, ds.cnt))

    def emit(self, block):
        decos = {"pe": block.tensor, "act": block.scalar, "dve": block.vector,
                 "pool": block.gpsimd, "sp": block.sync}
        for n, deco in decos.items():
            E = self.eng[n]

            def body(h, E=E):
                for it in E.prog:
                    if it[0] == "wait":
                        h.wait_ge(it[1], it[2])
                    elif it[0] == "ins":
                        ins = it[1](h)
                        if it[2] is not None:
                            ins.then_inc(it[2], 1)
                    elif it[0] == "cc":
                        h.collective_compute(it[1], ALU.bypass, replica_groups=it[2], ins=[it[3]], outs=[it[4]]).then_inc(it[5], 16)
                    else:
                        h.dma_start(out=it[1], in_=it[2]).then_inc(it[3], 16)
            deco(body)


class Arena:
    def __init__(self, ap_u8, size):
        self.ap = ap_u8
        self.size = size
        self.off = 0

    def alloc(self, free_shape, dtype, parts=128):
        n = int(np.prod(free_shape))
        nb = n * DTSIZE[dtype]
        self.off = (self.off + 63) // 64 * 64
        assert self.off + nb <= self.size, ("arena overflow", self.off, nb, self.size)
        v = self.ap[0:parts, self.off:self.off + nb]
        if dtype != U8:
            v = v.bitcast(dtype)
        self.off += nb
        if len(free_shape) == 2:
            v = v.rearrange("p (a b) -> p a b", a=free_shape[0], b=free_shape[1])
        elif len(free_shape) == 3:
            v = v.rearrange("p (a b c) -> p a b c", a=free_shape[0], b=free_shape[1], c=free_shape[2])
        return v


class Ring:
    def __init__(self, P, A, n, shape, dtype, name, dsem=True):
        self.slots = []
        for i in range(n):
            self.slots.append((A.alloc(shape, dtype), Buf(f"{name}{i}"), P.new_dsem(f"d_{name}{i}") if dsem else None))
        self.i = 0

    def next(self):
        s = self.slots[self.i % len(self.slots)]
        self.i += 1
        return s


def build(NT, DEPTH=2, stop_after=None, debug=False, same_engine_sync=True, p4_stage=None):
    SEG = NT // 2
    NTILE = NT // T
    TPS = SEG // T
    nc = bass.Bass("TRN2", target_bir_lowering=False)
    dkind = "ExternalOutput" if debug else "Internal"

    def din(name, shape, dt=F32):
        return nc.dram_tensor(name, shape, dt, kind="ExternalInput").ap()

    def dscr(name, shape, dt):
        return nc.dram_tensor(name, shape, dt, kind=dkind).ap()

    xT = din("xT", [D, NT])
    pT = din("pT", [DEPTH * PLE, NT])
    wcat = din("wcat", [DEPTH * WTOT])
    gains = din("gains", [128, (DEPTH * 3 + 1) * KC])
    dec = din("dec", [128, DEPTH * 8])
    sgln = din("sgln", [128, DEPTH * 2 * SGW])
    wsT = din("wsT", [128, DEPTH * G * 128])
    bsr = din("bsr", [1, DEPTH * SGW])
    posb = din("posb", [128, NT])
    invf = din("invf", [128, 1])
    carry = din("carry", [128, 1])
    yT = nc.dram_tensor("yT", [D, NT], F32, kind="ExternalOutput").ap()

    wbf = nc.dram_tensor("wbf", [DEPTH * WTOT], BF16, kind="Internal").ap()
    hT = dscr("hT", [D, NT], F32)
    qT_d = dscr("qT_d", [RW, NT], BF16)
    kT_d = dscr("kT_d", [RW, NT], BF16)
    ktm_d = dscr("ktm_d", [NT, RW], BF16)
    vtm_d = dscr("vtm_d", [NT, RW], BF16)
    gact_d = dscr("gact_d", [NT, RW], BF16)
    svln_d = dscr("svln_d", [NT, SGW], BF16)
    uT_d = dscr("uT_d", [SGW, NT], BF16)
    ypart_d = dscr("ypart_d", [NT, RW], F32)
    ymixT_d = dscr("ymixT_d", [D, NT], BF16)

    with ExitStack() as st:
        P = Prog(nc, st, same_engine_sync=same_engine_sync)
        arena_t = st.enter_context(nc.sbuf_tensor("arena", [128, ARENA], U8))
        A = Arena(arena_t[:], ARENA)
        banks = [st.enter_context(nc.psum_tensor(f"bank{i}", [128, 512], F32)) for i in range(8)]
        bankb = [Buf(f"bank{i}") for i in range(8)]

        def tilebufs(name):
            return [Buf(f"{name}{t}") for t in range(NTILE)]
        B_hT = tilebufs("hT")
        B_q = tilebufs("q"); B_k = tilebufs("k"); B_ktm = tilebufs("ktm"); B_v = tilebufs("v")
        B_g = tilebufs("g"); B_sv = tilebufs("sv"); B_u = tilebufs("u"); B_yp = tilebufs("yp")
        B_ymr = tilebufs("ymr"); B_yms = tilebufs("yms")
        B_w = {(l, n): Buf(f"w{l}{n}") for l in range(DEPTH) for n, *_ in WSPEC}

        CH_EL = 128 * 8192
        cast_q = []
        for l in range(DEPTH):
            for n, k_, nn_, g_ in WSPEC:
                off = l * WTOT + WOFF[n][0]
                tot = k_ * nn_
                ds = P.new_dsem(f"wc{l}{n}", persistent=True)
                o = 0
                while o < tot:
                    sz = min(CH_EL, tot - o)
                    src = wcat[off + o: off + o + sz].rearrange("(p f) -> p f", p=128)
                    dst = wbf[off + o: off + o + sz].rearrange("(p f) -> p f", p=128)
                    cast_q.append((l, n, dst, src, ds))
                    o += sz

        def issue_casts(n=None, layer=None, name=None):
            while cast_q:
                l_, n_, dst, src, ds = cast_q[0]
                if layer is not None and (l_ > layer or (name is not None and (l_, n_) != (layer, name))):
                    break
                if layer is None and n is not None and n <= 0:
                    break
                cast_q.pop(0)
                P.dma("pool", dst, src, ds, writes=[B_w[(l_, n_)]])
                if n is not None:
                    n -= 1

        issue_casts(layer=0, name="w_in")

        def wview(l, name, g):
            off, kc, gw, ng = WOFF[name]
            base = l * WTOT + off + g * 128 * kc * gw
            return wbf[base: base + 128 * kc * gw].rearrange("(p f) -> p f", p=128)

        gains_sb = A.alloc([(DEPTH * 3 + 1) * KC], F32); Bc = Buf("const")
        ld0 = P.new_dsem("ld0", persistent=True)
        ld1 = P.new_dsem("ld1", persistent=True)
        ld2 = P.new_dsem("ld2", persistent=True)
        P.dma("sp", gains_sb, gains, ld0, writes=[Bc])
        dec_sb = A.alloc([DEPTH * 8], F32)
        P.dma("sp", dec_sb, dec, ld0, writes=[Bc])
        invf_sb = A.alloc([1], F32)
        P.dma("sp", invf_sb, invf, ld0, writes=[Bc])
        carry_sb = A.alloc([1], F32)
        P.dma("sp", carry_sb, carry, ld0, writes=[Bc])
        diff = A.alloc([128], F32)
        fidx1 = A.alloc([128], F32)
        pidx = A.alloc([1], F32)
        c128mp = A.alloc([1], F32)
        c127mp = A.alloc([1], F32)
        ident = A.alloc([128], BF16)
        ones = A.alloc([128], BF16)
        tmpc = A.alloc([128], F32)
        P.op("pool", lambda e: e.iota(diff, [[1, 128]], base=0, channel_multiplier=-1, allow_small_or_imprecise_dtypes=True), writes=[Bc])
        P.op("pool", lambda e: e.iota(fidx1, [[1, 128]], base=1, channel_multiplier=0, allow_small_or_imprecise_dtypes=True), writes=[Bc])
        P.op("pool", lambda e: e.iota(pidx, [[0, 1]], base=0, channel_multiplier=1, allow_small_or_imprecise_dtypes=True), writes=[Bc])
        P.op("pool", lambda e: e.memset(ones, 1.0), writes=[Bc])
        P.op("dve", lambda e: e.tensor_scalar(out=c128mp, in0=pidx, scalar1=-1.0, scalar2=128.0, op0=ALU.mult, op1=ALU.add), reads=[Bc], writes=[Bc])
        P.op("dve", lambda e: e.tensor_scalar(out=c127mp, in0=pidx, scalar1=-1.0, scalar2=127.0, op0=ALU.mult, op1=ALU.add), reads=[Bc], writes=[Bc])
        P.op("dve", lambda e: e.tensor_scalar(out=tmpc, in0=diff, scalar1=0.0, scalar2=None, op0=ALU.is_equal), reads=[Bc], writes=[Bc])
        P.op("dve", lambda e: e.tensor_copy(out=ident, in_=tmpc), reads=[Bc], writes=[Bc])
        lg = A.alloc([8], F32)
        e1 = A.alloc([8], F32)
        maskT = A.alloc([H, 128], F32)
        qdecf = A.alloc([8, 128], BF16)
        qdecb = A.alloc([H], F32)
        kdecf = A.alloc([H], F32)
        kdecb = A.alloc([H], F32)
        cdec = A.alloc([8], F32)
        lng = A.alloc([SGW], F32)
        lnb = A.alloc([SGW], F32)
        wsT_b = A.alloc([G, 128], BF16)
        bs_b = A.alloc([SGW], BF16, parts=1)
        Bl = Buf("layerconst")
        A_base = A.off

        def layer_setup(l):
            P.dma("sp", lng, sgln[:, (2 * l) * SGW:(2 * l + 1) * SGW], ld2, writes=[Bl])
            P.dma("sp", lnb, sgln[:, (2 * l + 1) * SGW:(2 * l + 2) * SGW], ld2, writes=[Bl])
            P.dma("pool", wsT_b, wsT[:, l * G * 128:(l + 1) * G * 128].rearrange("p (g i) -> p g i", g=G), ld1, writes=[Bl])
            P.dma("pool", bs_b, bsr[:, l * SGW:(l + 1) * SGW], ld1, writes=[Bl])
            P.op("act", lambda e: e.activation(out=e1, in_=dec_sb[:, l * 8:(l + 1) * 8], func=AF.Exp, scale=-math.log(2.0)), reads=[Bc, Bl], writes=[Bl])
            P.op("dve", lambda e: e.tensor_scalar(out=e1, in0=e1, scalar1=-(2.0 ** -5), scalar2=1.0, op0=ALU.mult, op1=ALU.add), reads=[Bl], writes=[Bl])
            P.op("act", lambda e: e.activation(out=lg, in_=e1, func=AF.Ln), reads=[Bl], writes=[Bl])
            P.op("act", lambda e: e.activation(out=cdec, in_=lg, func=AF.Exp, scale=128.0), reads=[Bl], writes=[Bl])
            for h in range(H):
                P.op("dve", lambda e, h=h: e.tensor_scalar(out=tmpc, in0=diff, scalar1=0.0, scalar2=lg[:, h:h + 1], op0=ALU.max, op1=ALU.mult), reads=[Bc, Bl], writes=[Bl])
                P.op("dve", lambda e, h=h: e.tensor_scalar(out=maskT[:, h, :], in0=diff, scalar1=-1.0, scalar2=0.0, op0=ALU.mult, op1=ALU.max), reads=[Bc, Bl], writes=[Bl])
                P.op("dve", lambda e, h=h: e.scalar_tensor_tensor(out=tmpc, in0=maskT[:, h, :], scalar=lg[:, 4 + h:5 + h], in1=tmpc, op0=ALU.mult, op1=ALU.add), reads=[Bl], writes=[Bl])
                P.op("act", lambda e, h=h: e.activation(out=maskT[:, h, :], in_=tmpc, func=AF.Exp), reads=[Bl], writes=[Bl])
                for dc in range(2):
                    P.op("act", lambda e, h=h, dc=dc: e.activation(out=qdecf[:, 2 * h + dc, :], in_=fidx1, func=AF.Exp, scale=lg[:, h:h + 1]), reads=[Bc, Bl], writes=[Bl])
                P.op("act", lambda e, h=h: e.activation(out=qdecb[:, h:h + 1], in_=c128mp, func=AF.Exp, scale=lg[:, 4 + h:5 + h]), reads=[Bc, Bl], writes=[Bl])
                P.op("act", lambda e, h=h: e.activation(out=kdecf[:, h:h + 1], in_=c127mp, func=AF.Exp, scale=lg[:, h:h + 1]), reads=[Bc, Bl], writes=[Bl])
                P.op("act", lambda e, h=h: e.activation(out=kdecb[:, h:h + 1], in_=pidx, func=AF.Exp, scale=lg[:, 4 + h:5 + h]), reads=[Bc, Bl], writes=[Bl])

        _subs = {}

        def subs(parent, n):
            key = (id(parent), n)
            if key not in _subs:
                _subs[key] = [Buf(f"{parent.name}.{i}") for i in range(n)]
                _subs[key].append(parent)
            return _subs[key][:n]

        def gain_ap(idx, kc):
            return gains_sb[:, idx * KC + kc: idx * KC + kc + 1]

        def phase_end():
            P.barrier()
            P.release_phase_dsems()
            A.off = A_base
            for b in bankb:
                b.w = None; b.r = {}

        def rstd_from_psum(ps_ap, out_ap, n, reads, writes):
            P.op("dve", lambda e: e.tensor_scalar(out=out_ap, in0=ps_ap, scalar1=1.0 / n, scalar2=EPS, op0=ALU.mult, op1=ALU.add), reads=reads, writes=writes)
            P.op("act", lambda e: e.activation(out=out_ap, in_=out_ap, func=AF.Sqrt), reads=writes, writes=writes)
            P.op("dve", lambda e: e.reciprocal(out=out_ap, in_=out_ap), reads=writes, writes=writes)

        for l in range(DEPTH):
            src_h = xT if l == 0 else hT
            layer_setup(l)

            if True:
                hin = Ring(P, A, 2, [4, T], F32, "hin")
                sqr = Ring(P, A, 4, [T], BF16, "sq", dsem=False)
                hgr = Ring(P, A, 2, [KC, T], BF16, "hg", dsem=False)
                rsb = Ring(P, A, 2, [T], F32, "rsb", dsem=False)
                rst = Ring(P, A, 2, [NCH], F32, "rst", dsem=False)
                tab_ap = A.alloc([4, T], F32); tab_b = Buf("tabs")
                posr = Ring(P, A, 2, [T], F32, "pos")
                wr = Ring(P, A, 3, [KC * 512], BF16, "w")
                tmps = Ring(P, A, 2, [4, T], F32, "ropet", dsem=False)
                trig = A.alloc([2, T], F32); Btrig = Buf("trig")
                trigi = A.alloc([T], I32)
                outr = Ring(P, A, 4, [4 * T], BF16, "out")
                svg = Ring(P, A, 4, [T], F32, "svg", dsem=False)
                svx = Ring(P, A, 4, [T], F32, "svx", dsem=False)
                ut = Ring(P, A, 2, [T], F32, "ut", dsem=False)
                lnst = Ring(P, A, 2, [6, 16], F32, "lnst", dsem=False)
                pacc = [2, 3, 4, 5]
                pacc_i = [0]
                ptr = [6, 7]
                ptr_i = [0]

                def next_acc():
                    b = pacc[pacc_i[0] % 4]
                    pacc_i[0] += 1
                    return b

                norm_state = {}

                def p1_begin(t):
                    pap, pb, pds = posr.next()
                    P.dma("sp", pap, posb[:, t * T:(t + 1) * T], pds, writes=[pb])
                    norm_state[t] = dict(hin={}, pos=(pap, pb), hg=hgr.next(), rsb=rsb.next(), rst=rst.next(), sq={})

                def p1_load(t, part):
                    t0 = t * T
                    ap, b, ds = hin.next()
                    kc0 = part * 4
                    P.dma("sp", ap, src_h[kc0 * 128:(kc0 + 4) * 128, t0:t0 + T].rearrange("(k p) s -> p k s", p=128),
                          ds, reads=[B_hT[t]] if l > 0 else [], writes=[b])
                    norm_state[t]["hin"][part] = (ap, b)

                def p1_norm_elem(t, part):
                    s = norm_state[t]
                    hg_ap, hg_b, _ = s["hg"]
                    hap, hb = s["hin"][part]
                    for j in range(4):
                        kc = part * 4 + j
                        sq_ap, sq_b, _ = sqr.next()
                        s["sq"][kc] = (sq_ap, sq_b)
                        P.op("act", lambda e, hap=hap, j=j, sq_ap=sq_ap: e.activation(out=sq_ap, in_=hap[:, j, :], func=AF.Square), reads=[hb], writes=[sq_b])
                        ga = gain_ap(l * 3 + 0, kc)
                        P.op("act", lambda e, hap=hap, j=j, kc=kc, ga=ga: e.activation(out=hg_ap[:, kc, :], in_=hap[:, j, :], func=AF.Copy, scale=ga), reads=[hb, Bc], writes=[hg_b])

                def p1_norm_pe(t, part):
                    s = norm_state[t]
                    for j in range(4):
                        kc = part * 4 + j
                        sq_ap, sq_b = s["sq"][kc]
                        P.op("pe", lambda e, sq_ap=sq_ap, kc=kc: e.matmul(banks[0][:, :], lhsT=ones, rhs=sq_ap, start=(kc == 0), stop=(kc == KC - 1)), reads=[sq_b, Bc], writes=[bankb[0]], inc=False)
                        for c in range(NCH):
                            P.op("pe", lambda e, sq_ap=sq_ap, kc=kc, c=c: e.matmul(banks[1][:, c:c + 1], lhsT=sq_ap[:, c * 128:(c + 1) * 128], rhs=ones[:, 0:1], start=(kc == 0 and c == 0), stop=(kc == KC - 1 and c == NCH - 1), skip_group_check=True), reads=[sq_b, Bc], writes=[bankb[1]], inc=(c == NCH - 1))

                def p1_norm_fin(t):
                    s = norm_state[t]
                    rsb_ap, rsb_b, _ = s["rsb"]
                    rst_ap, rst_b, _ = s["rst"]
                    pap, pb = s["pos"]
                    rstd_from_psum(banks[0][:, :], rsb_ap, D, [bankb[0]], [rsb_b])
                    rstd_from_psum(banks[1][:, 0:NCH], rst_ap, D, [bankb[1]], [rst_b])
                    u = trig[:, 0, :]
                    f = trig[:, 1, :]
                    for j, sh in ((0, 0.25), (1, 0.0)):
                        P.op("dve", lambda e: e.tensor_scalar(out=u, in0=pap, scalar1=invf_sb[:, 0:1], scalar2=1.0 / (2 * math.pi), op0=ALU.mult, op1=ALU.mult), reads=[pb, Bc], writes=[Btrig])
                        if sh:
                            P.op("dve", lambda e, sh=sh: e.tensor_scalar(out=u, in0=u, scalar1=sh, scalar2=None, op0=ALU.add), reads=[Btrig], writes=[Btrig])
                        P.op("dve", lambda e: e.tensor_copy(out=trigi, in_=u), reads=[Btrig], writes=[Btrig])
                        P.op("dve", lambda e: e.tensor_copy(out=f, in_=trigi), reads=[Btrig], writes=[Btrig])
                        P.op("dve", lambda e: e.tensor_sub(out=u, in0=u, in1=f), reads=[Btrig], writes=[Btrig])
                        P.op("dve", lambda e: e.tensor_scalar(out=f, in0=u, scalar1=0.5, scalar2=None, op0=ALU.is_gt), reads=[Btrig], writes=[Btrig])
                        P.op("dve", lambda e: e.tensor_sub(out=u, in0=u, in1=f), reads=[Btrig], writes=[Btrig])
                        P.op("act", lambda e: e.activation(out=u, in_=u, func=AF.Sin, scale=2 * math.pi), reads=[Btrig], writes=[Btrig])
                        P.op("dve", lambda e, j=j: e.tensor_mul(out=tab_ap[:, j, :], in0=u, in1=rsb_ap), reads=[Btrig, rsb_b], writes=[tab_b])
                        P.op("pool", lambda e, j=j: e.tensor_scalar(out=tab_ap[:, 2 + j, :], in0=tab_ap[:, j, :], scalar1=DH ** -0.5, scalar2=None, op0=ALU.mult), reads=[tab_b], writes=[tab_b])

                wq = []
                GORDER = [10, 0, 4, 1, 11, 5, 2, 6, 8, 3, 7, 9]

                def p1_wload(g):
                    ap, b, ds = wr.next()
                    P.dma("sp", ap, wview(l, "w_in", g), ds, reads=[B_w[(l, "w_in")]], writes=[b])
                    wq.append((ap.rearrange("p (k g) -> p k g", k=KC), b))

                def fm_view(o_ap):
                    return o_ap.rearrange("p (b s) -> p b s", b=4)

                def tm_view(o_ap):
                    return o_ap.rearrange("p (c f) -> p c f", c=NCH)

                def p1_proj(t):
                    s = norm_state[t]
                    t0 = t * T
                    hg_ap, hg_b, _ = s["hg"]
                    rsb_ap, rsb_b, _ = s["rsb"]
                    rst_ap, rst_b, _ = s["rst"]
                    nx = t + 1 < NTILE
                    for gi in range(12):
                        g = GORDER[gi]
                        if nx and 4 <= gi < 8:
                            p1_norm_elem(t + 1, gi - 4)
                        if nx and gi == 1:
                            p1_begin(t + 1)
                        if gi == 0 and t == 0:
                            p1_wload(GORDER[0]); p1_wload(GORDER[1])
                        nxt = t * 12 + gi + 2
                        if nxt < NTILE * 12:
                            p1_wload(GORDER[nxt % 12])
                        if nx and 2 <= gi < 6:
                            p1_load(t + 1, gi - 2)
                        if nx and gi == 10:
                            p1_norm_fin(t + 1)
                        w_ap, w_b = wq.pop(0)
                        o_ap, o_b, o_ds = outr.next()
                        col0 = (g % 2) * 512
                        if g < 4 or 8 <= g < 10:
                            of = fm_view(o_ap)
                            accs = []
                            for blk in range(4):
                                bk = next_acc()
                                for kc in range(KC):
                                    P.op("pe", lambda e, bk=bk, blk=blk, kc=kc, w_ap=w_ap: e.matmul(banks[bk][:, :], lhsT=w_ap[:, kc, blk * 128:(blk + 1) * 128], rhs=hg_ap[:, kc, :], start=(kc == 0), stop=(kc == KC - 1)),
                                         reads=[w_b, hg_b], writes=[bankb[bk]], inc=(kc == KC - 1))
                                accs.append(bk)
                                if g >= 8:
                                    ut_ap, ut_b, _ = ut.next()
                                    P.op("dve", lambda e, bk=bk, ut_ap=ut_ap: e.tensor_mul(out=ut_ap, in0=banks[bk][:, :], in1=rsb_ap), reads=[bankb[bk], rsb_b], writes=[ut_b])
                                    P.op("act", lambda e, ut_ap=ut_ap, blk=blk, of=of: e.activation(out=of[:, blk, :], in_=ut_ap, func=AF.Gelu_apprx_tanh), reads=[ut_b], writes=[o_b])
                            if g < 4:
                                ci, si = (2, 3) if g >= 2 else (0, 1)
                                for pr in range(2):
                                    b0, b1 = accs[2 * pr], accs[2 * pr + 1]
                                    tm_ap, tm_b, _ = tmps.next()
                                    tb = subs(tm_b, 4)
                                    P.op("dve", lambda e, b0=b0, tm_ap=tm_ap, ci=ci: e.tensor_mul(out=tm_ap[:, 0, :], in0=banks[b0][:, :], in1=tab_ap[:, ci, :]), reads=[bankb[b0], tab_b], writes=[tb[0]])
                                    P.op("dve", lambda e, b1=b1, tm_ap=tm_ap, si=si: e.tensor_mul(out=tm_ap[:, 1, :], in0=banks[b1][:, :], in1=tab_ap[:, si, :]), reads=[bankb[b1], tab_b], writes=[tb[1]])
                                    P.op("dve", lambda e, b0=b0, tm_ap=tm_ap, si=si: e.tensor_mul(out=tm_ap[:, 2, :], in0=banks[b0][:, :], in1=tab_ap[:, si, :]), reads=[bankb[b0], tab_b], writes=[tb[2]])
                                    P.op("dve", lambda e, b1=b1, tm_ap=tm_ap, ci=ci: e.tensor_mul(out=tm_ap[:, 3, :], in0=banks[b1][:, :], in1=tab_ap[:, ci, :]), reads=[bankb[b1], tab_b], writes=[tb[3]])
                                    P.op("pool", lambda e, tm_ap=tm_ap, pr=pr, of=of: e.tensor_sub(out=of[:, 2 * pr, :], in0=tm_ap[:, 0, :], in1=tm_ap[:, 1, :]), reads=[tb[0], tb[1]], writes=[o_b])
                                    P.op("pool", lambda e, tm_ap=tm_ap, pr=pr, of=of: e.tensor_add(out=of[:, 2 * pr + 1, :], in0=tm_ap[:, 2, :], in1=tm_ap[:, 3, :]), reads=[tb[2], tb[3]], writes=[o_b])
                            dd, db = (qT_d, B_q) if g < 2 else ((kT_d, B_k) if g < 4 else (uT_d, B_u))
                            P.dma("pool", dd[col0:col0 + 512, t0:t0 + T].rearrange("(b p) s -> p b s", p=128), of, o_ds, reads=[o_b], writes=[db[t]])
                            if 2 <= g < 4:
                                o2_ap, o2_b, o2_ds = outr.next()
                                o2 = tm_view(o2_ap)
                                bk = ptr[ptr_i[0] % 2]; ptr_i[0] += 1
                                pv = banks[bk][:, :].bitcast(BF16)
                                for half in range(2):
                                    for cc in range(2):
                                        c = half * 2 + cc
                                        for blk in range(4):
                                            sl = (cc * 4 + blk) * 128
                                            P.op("pe", lambda e, sl=sl, blk=blk, c=c, of=of, pv=pv: e.transpose(out=pv[:, sl:sl + 128], in_=of[:, blk, c * 128:(c + 1) * 128], identity=ident),
                                                 reads=[o_b, Bc], writes=[bankb[bk]], inc=(blk == 3 and cc == 1))
                                    P.op("act", lambda e, half=half, o2=o2, pv=pv: e.activation(out=o2[:, half * 2:half * 2 + 2, :], in_=pv.rearrange("p (c f) -> p c f", c=2), func=AF.Copy), reads=[bankb[bk]], writes=[o2_b])
                                P.dma("pool", ktm_d[t0:t0 + T, col0:col0 + 512].rearrange("(c p) f -> p c f", p=128), o2, o2_ds, reads=[o2_b], writes=[B_ktm[t]])
                        else:
                            ot = tm_view(o_ap)
                            for c in range(NCH):
                                bk = next_acc()
                                for kc in range(KC):
                                    P.op("pe", lambda e, bk=bk, kc=kc, c=c, w_ap=w_ap: e.matmul(banks[bk][:, :], lhsT=hg_ap[:, kc, c * 128:(c + 1) * 128], rhs=w_ap[:, kc, :], start=(kc == 0), stop=(kc == KC - 1)),
                                         reads=[w_b, hg_b], writes=[bankb[bk]], inc=(kc == KC - 1))
                                if g < 6:
                                    P.op("act", lambda e, bk=bk, c=c, ot=ot: e.activation(out=ot[:, c, :], in_=banks[bk][:, :], func=AF.Copy, scale=rst_ap[:, c:c + 1]), reads=[bankb[bk], rst_b], writes=[o_b])
                                elif g < 8:
                                    P.op("act", lambda e, bk=bk, c=c, ot=ot: e.activation(out=ot[:, c, :], in_=banks[bk][:, :], func=AF.Silu, scale=rst_ap[:, c:c + 1]), reads=[bankb[bk], rst_b], writes=[o_b])
                                else:
                                    if c == 0:
                                        st_ap, st_b, _ = lnst.next()
                                        sv_slots = []
                                    sg_ap, sg_b, _ = svg.next()
                                    sx_ap, sx_b, _ = svx.next()
                                    sv_slots.append((sg_ap, sg_b, sx_ap, sx_b))
                                    for g4 in range(4):
                                        P.op("act", lambda e, bk=bk, c=c, g4=g4, sg_ap=sg_ap, st_ap=st_ap: e.activation(out=sg_ap[:, g4 * 128:(g4 + 1) * 128], in_=banks[bk][:, g4 * 128:(g4 + 1) * 128], func=AF.Gelu_apprx_tanh, scale=rst_ap[:, c:c + 1], accum_out=st_ap[:, 0, c * 4 + g4:c * 4 + g4 + 1]), reads=[bankb[bk], rst_b], writes=[sg_b, st_b])
                                    for g4 in range(4):
                                        P.op("act", lambda e, c=c, g4=g4, sg_ap=sg_ap, sx_ap=sx_ap, st_ap=st_ap: e.activation(out=sx_ap[:, g4 * 128:(g4 + 1) * 128], in_=sg_ap[:, g4 * 128:(g4 + 1) * 128], func=AF.Square, accum_out=st_ap[:, 1, c * 4 + g4:c * 4 + g4 + 1]), reads=[sg_b], writes=[sx_b, st_b])
                            if g >= 10:
                                P.op("dve", lambda e, st_ap=st_ap: e.tensor_scalar(out=st_ap[:, 2, :], in0=st_ap[:, 0, :], scalar1=1.0 / 128, scalar2=None, op0=ALU.mult), reads=[st_b], writes=[st_b])
                                P.op("dve", lambda e, st_ap=st_ap: e.tensor_mul(out=st_ap[:, 3, :], in0=st_ap[:, 2, :], in1=st_ap[:, 2, :]), reads=[st_b], writes=[st_b])
                                P.op("dve", lambda e, st_ap=st_ap: e.scalar_tensor_tensor(out=st_ap[:, 3, :], in0=st_ap[:, 1, :], scalar=1.0 / 128, in1=st_ap[:, 3, :], op0=ALU.mult, op1=ALU.subtract), reads=[st_b], writes=[st_b])
                                P.op("dve", lambda e, st_ap=st_ap: e.tensor_scalar(out=st_ap[:, 4, :], in0=st_ap[:, 3, :], scalar1=EPS, scalar2=None, op0=ALU.add), reads=[st_b], writes=[st_b])
                                P.op("act", lambda e, st_ap=st_ap: e.activation(out=st_ap[:, 4, :], in_=st_ap[:, 4, :], func=AF.Sqrt), reads=[st_b], writes=[st_b])
                                P.op("dve", lambda e, st_ap=st_ap: e.reciprocal(out=st_ap[:, 4, :], in_=st_ap[:, 4, :]), reads=[st_b], writes=[st_b])
                                P.op("dve", lambda e, st_ap=st_ap: e.scalar_tensor_tensor(out=st_ap[:, 5, :], in0=st_ap[:, 2, :], scalar=-1.0, in1=st_ap[:, 4, :], op0=ALU.mult, op1=ALU.mult), reads=[st_b], writes=[st_b])
                                for c in range(NCH):
                                    sg_ap, sg_b, sx_ap, sx_b = sv_slots[c]
                                    for g4 in range(4):
                                        P.op("act", lambda e, c=c, g4=g4, sg_ap=sg_ap, sx_ap=sx_ap, st_ap=st_ap: e.activation(out=sx_ap[:, g4 * 128:(g4 + 1) * 128], in_=sg_ap[:, g4 * 128:(g4 + 1) * 128], func=AF.Identity, scale=st_ap[:, 4, c * 4 + g4:c * 4 + g4 + 1], bias=st_ap[:, 5, c * 4 + g4:c * 4 + g4 + 1]), reads=[sg_b, st_b], writes=[sx_b])
                                    P.op("pool", lambda e, sx_ap=sx_ap, col0=col0: e.tensor_mul(out=sx_ap, in0=sx_ap, in1=lng[:, col0:col0 + 512]), reads=[sx_b, Bl], writes=[sx_b])
                                    P.op("pool", lambda e, sx_ap=sx_ap, col0=col0, c=c, ot=ot: e.tensor_add(out=ot[:, c, :], in0=sx_ap, in1=lnb[:, col0:col0 + 512]), reads=[sx_b, Bl], writes=[o_b])
                            dd, db = (vtm_d, B_v) if g < 6 else ((gact_d, B_g) if g < 8 else (svln_d, B_sv))
                            P.dma("pool", dd[t0:t0 + T, col0:col0 + 512].rearrange("(c p) f -> p c f", p=128), ot, o_ds, reads=[o_b], writes=[db[t]])
                        if nx and 4 <= gi < 8:
                            p1_norm_pe(t + 1, gi - 4)

                p1_begin(0)
                p1_load(0, 0); p1_load(0, 1)
                p1_norm_elem(0, 0); p1_norm_pe(0, 0)
                p1_load(0, 2)
                p1_norm_elem(0, 1); p1_norm_pe(0, 1)
                p1_load(0, 3)
                p1_norm_elem(0, 2); p1_norm_pe(0, 2)
                p1_norm_elem(0, 3); p1_norm_pe(0, 3)
                p1_norm_fin(0)
                for t in range(NTILE):
                    if l == 0:
                        issue_casts(n=1)
                    p1_proj(t)
                phase_end()
            if stop_after == (l, 1):
                break

            if True:
                qr = Ring(P, A, 2, [8, T], BF16, "q2")
                kr = Ring(P, A, 2, [8, T], BF16, "k2")
                ktr = Ring(P, A, 2, [NCH, RW], BF16, "kt2")
                vr = Ring(P, A, 2, [NCH, RW], BF16, "v2")
                svr = Ring(P, A, 2, [NCH, SGW], BF16, "sv2")
                ur = Ring(P, A, 2, [8, T], BF16, "u2")
                Sf = A.alloc([8, DH], F32)
                Sfb2 = [A.alloc([8, DH], BF16), A.alloc([8, DH], BF16)]
                B_S = [Buf(f"S{h}") for h in range(H)]
                B_Sb2 = [[Buf(f"Sb{i}{h}") for h in range(H)] for i in range(2)]
                par = [0]
                ptr_ = Ring(P, A, 2, [H, 128], BF16, "PT", dsem=False)
                qfr = Ring(P, A, 2, [8, 128], BF16, "qf", dsem=False)
                kfr = Ring(P, A, 2, [RW], BF16, "kf", dsem=False)
                yp_sb = A.alloc([NCH, RW], F32); Byp = Buf("yp_sb"); dyp = P.new_dsem("dyp")
                ysg_sb = A.alloc([8, T], BF16); Bysg = Buf("ysg_sb"); dysg = P.new_dsem("dysg")
                BK_SC, BK_Y, BK_S, BK_Z = 0, (1, 2), (3, 4), (5, 6)
                maskv = maskT.rearrange("p h i -> p (h i)")

                def p2_load(t):
                    t0 = t * T
                    q_ap, q_b, q_ds = qr.next()
                    P.dma("sp", q_ap, qT_d[:, t0:t0 + T].rearrange("(b p) s -> p b s", p=128), q_ds, reads=[B_q[t]], writes=[q_b])
                    k_ap, k_b, k_ds = kr.next()
                    P.dma("sp", k_ap, kT_d[:, t0:t0 + T].rearrange("(b p) s -> p b s", p=128), k_ds, reads=[B_k[t]], writes=[k_b])
                    kt_ap, kt_b, kt_ds = ktr.next()
                    P.dma("sp", kt_ap, ktm_d[t0:t0 + T, :].rearrange("(c p) f -> p c f", p=128), kt_ds, reads=[B_ktm[t]], writes=[kt_b])
                    v_ap, v_b, v_ds = vr.next()
                    P.dma("sp", v_ap, vtm_d[t0:t0 + T, :].rearrange("(c p) f -> p c f", p=128), v_ds, reads=[B_v[t]], writes=[v_b])
                    sv_ap, sv_b, sv_ds = svr.next()
                    P.dma("sp", sv_ap, svln_d[t0:t0 + T, :].rearrange("(c p) f -> p c f", p=128), sv_ds, reads=[B_sv[t]], writes=[sv_b])
                    u_ap, u_b, u_ds = ur.next()
                    P.dma("sp", u_ap, uT_d[:, t0:t0 + T].rearrange("(b p) s -> p b s", p=128), u_ds, reads=[B_u[t]], writes=[u_b])
                    return (q_ap, q_b, k_ap, k_b, kt_ap, kt_b, v_ap, v_b, sv_ap, sv_b, u_ap, u_b)

                def state_update(Sx, Sxb, BS, BSb, kdec_ap, cd0, kt_ap, kt_b, v_ap, v_b, c, kfr):
                    kf_ap, kf_b0, _ = kfr.next()
                    kfb = subs(kf_b0, H)
                    for h in range(H):
                        P.op("act", lambda e, h=h: e.activation(out=kf_ap[:, h * DH:(h + 1) * DH], in_=kt_ap[:, c, h * DH:(h + 1) * DH], func=AF.Copy, scale=kdec_ap[:, h:h + 1]), reads=[kt_b, Bl], writes=[kfb[h]])
                    for h in range(H):
                        bk = BK_S[h % 2]
                        for dc in range(2):
                            P.op("pe", lambda e, h=h, dc=dc, bk=bk: e.matmul(banks[bk][:, dc * DH:(dc + 1) * DH], lhsT=kf_ap[:, h * DH + dc * 128: h * DH + (dc + 1) * 128], rhs=v_ap[:, c, h * DH:(h + 1) * DH], start=True, stop=True),
                                 reads=[kfb[h], v_b], writes=[bankb[bk]], inc=(dc == 1))
                        Sh = Sx[:, 2 * h:2 * h + 2, :].rearrange("p a b -> p (a b)")
                        Shb = Sxb[:, 2 * h:2 * h + 2, :].rearrange("p a b -> p (a b)")
                        P.op("dve", lambda e, h=h, bk=bk, Sh=Sh: e.scalar_tensor_tensor(out=Sh, in0=Sh, scalar=cdec[:, cd0 + h:cd0 + h + 1], in1=banks[bk][:, :], op0=ALU.mult, op1=ALU.add), reads=[bankb[bk], Bl], writes=[BS[h]])
                        P.op("act", lambda e, Sh=Sh, Shb=Shb: e.activation(out=Shb, in_=Sh, func=AF.Copy), reads=[BS[h]], writes=[BSb[h]])

                def state_reset(Sx, Sxb, BS, BSb, zero):
                    Sa = Sx.rearrange("p a b -> p (a b)")
                    Sab = Sxb.rearrange("p a b -> p (a b)")
                    if zero:
                        P.op("pool", lambda e: e.memset(Sa, 0.0), writes=BS)
                    else:
                        P.op("dve", lambda e: e.tensor_scalar(out=Sa, in0=Sa, scalar1=carry_sb[:, 0:1], scalar2=None, op0=ALU.mult), reads=[Bc], writes=BS)
                    P.op("pool", lambda e: e.tensor_copy(out=Sab, in_=Sa), reads=BS, writes=BSb)

                def p2_chunk(t, c, bufs):
                    (q_ap, q_b, k_ap, k_b, kt_ap, kt_b, v_ap, v_b, sv_ap, sv_b, u_ap, u_b) = bufs
                    gc = t * NCH + c
                    cs = slice(c * 128, (c + 1) * 128)
                    if gc == 0:
                        state_reset(Sf, Sfb2[par[0]], B_S, B_Sb2[par[0]], True)
                    elif gc * 128 == SEG:
                        par[0] ^= 1
                        state_reset(Sf, Sfb2[par[0]], B_S, B_Sb2[par[0]], False)
                    Sfb, B_Sb = Sfb2[par[0]], B_Sb2[par[0]]
                    par[0] ^= 1
                    Sfb_n, B_Sb_n = Sfb2[par[0]], B_Sb2[par[0]]
                    for h in range(H):
                        for dc in range(2):
                            P.op("pe", lambda e, h=h, dc=dc: e.matmul(banks[BK_SC][:, h * 128:(h + 1) * 128], lhsT=k_ap[:, 2 * h + dc, cs], rhs=q_ap[:, 2 * h + dc, cs], start=(dc == 0), stop=(dc == 1)),
                                 reads=[k_b, q_b], writes=[bankb[BK_SC]], inc=(h == H - 1 and dc == 1))
                    pt_ap, pt_b, _ = ptr_.next()
                    P.op("dve", lambda e: e.tensor_tensor(out=pt_ap.rearrange("p h i -> p (h i)"), in0=banks[BK_SC][:, :], in1=maskv, op=ALU.mult), reads=[bankb[BK_SC], Bl], writes=[pt_b])
                    qf_ap, qf_b, _ = qfr.next()
                    P.op("pool", lambda e: e.tensor_tensor(out=qf_ap, in0=q_ap[:, :, cs], in1=qdecf, op=ALU.mult), reads=[q_b, Bl], writes=[qf_b])
                    state_update(Sf, Sfb_n, B_S, B_Sb_n, kdecf, 0, kt_ap, kt_b, v_ap, v_b, c, kfr)
                    for g in range(G):
                        bk = BK_Z[g // 4]
                        o = banks[bk][:, (g % 4) * 128:(g % 4 + 1) * 128]
                        P.op("pe", lambda e, g=g, o=o: e.matmul(o, lhsT=sv_ap[:, c, g * 128:(g + 1) * 128], rhs=wsT_b[:, g, :], start=True, stop=False), reads=[sv_b, Bl], writes=[bankb[bk]], inc=False)
                        P.op("pe", lambda e, g=g, o=o: e.matmul(o, lhsT=ones[0:1, :], rhs=bs_b[0:1, g * 128:(g + 1) * 128], start=False, stop=True), reads=[Bc, Bl], writes=[bankb[bk]], inc=(g % 4 == 3))
                    for zb in range(2):
                        bk = BK_Z[zb]
                        P.op("dve", lambda e, zb=zb, bk=bk: e.tensor_tensor(out=ysg_sb[:, zb * 4:(zb + 1) * 4, cs], in0=banks[bk][:, :].rearrange("p (g i) -> p g i", g=4), in1=u_ap[:, zb * 4:(zb + 1) * 4, cs], op=ALU.mult), reads=[bankb[bk], u_b], writes=[Bysg])
                    for h in range(H):
                        bk = BK_Y[h // 2]
                        o = banks[bk][:, (h % 2) * DH:(h % 2 + 1) * DH]
                        P.op("pe", lambda e, h=h, o=o: e.matmul(o, lhsT=pt_ap[:, h, :], rhs=v_ap[:, c, h * DH:(h + 1) * DH], start=True, stop=False), reads=[pt_b, v_b], writes=[bankb[bk]], inc=False)
                        P.op("pe", lambda e, h=h, o=o: e.matmul(o, lhsT=qf_ap[:, 2 * h, :], rhs=Sfb[:, 2 * h, :], start=False, stop=False), reads=[qf_b, B_Sb[h]], writes=[bankb[bk]], inc=False)
                        P.op("pe", lambda e, h=h, o=o: e.matmul(o, lhsT=qf_ap[:, 2 * h + 1, :], rhs=Sfb[:, 2 * h + 1, :], start=False, stop=True), reads=[qf_b, B_Sb[h]], writes=[bankb[bk]], inc=True)
                    for half in range(2):
                        bk = BK_Y[half]
                        P.op("act", lambda e, half=half, bk=bk: e.activation(out=yp_sb[:, c, half * 512:(half + 1) * 512], in_=banks[bk][:, :], func=AF.Copy), reads=[bankb[bk]], writes=[Byp])

                nxt_bufs = p2_load(0)
                for t in range(NTILE):
                    cur = nxt_bufs
                    if t + 1 < NTILE:
                        nxt_bufs = p2_load(t + 1)
                    if l == 0:
                        issue_casts(n=1)
                    for c in range(NCH):
                        p2_chunk(t, c, cur)
                    t0 = t * T
                    P.dma("pool", ypart_d[t0:t0 + T, :].rearrange("(c p) f -> p c f", p=128), yp_sb, dyp, reads=[Byp], writes=[B_yp[t]])
                    P.dma("pool", ymixT_d[RW:2 * RW, t0:t0 + T].rearrange("(b p) s -> p b s", p=128), ysg_sb, dysg, reads=[Bysg], writes=[B_yms[t]])
                phase_end()
            if stop_after == (l, 2):
                break

            if True:
                qr = Ring(P, A, 2, [8, T], BF16, "q3")
                ktr = Ring(P, A, 2, [NCH, RW], BF16, "kt3")
                vr = Ring(P, A, 2, [NCH, RW], BF16, "v3")
                ypr = Ring(P, A, 2, [NCH, RW], F32, "yp3")
                gr = Ring(P, A, 2, [NCH, RW], BF16, "g3")
                Sb_ = A.alloc([8, DH], F32)
                Sbb2 = [A.alloc([8, DH], BF16), A.alloc([8, DH], BF16)]
                B_S = [Buf(f"S{h}") for h in range(H)]
                B_Sb2 = [[Buf(f"Sb{i}{h}") for h in range(H)] for i in range(2)]
                par = [0]
                kfr = Ring(P, A, 2, [RW], BF16, "kb", dsem=False)
                yr_ = Ring(P, A, 3, [RW], F32, "y3", dsem=False)
                ynr = Ring(P, A, 2, [RW], BF16, "yn3", dsem=False)
                junk = A.alloc([DH], F32); Bjunk = Buf("junk")
                ssr = Ring(P, A, 2, [H], F32, "ss3", dsem=False)
                yret_sb = A.alloc([8, T], BF16); Byret = Buf("yret_sb"); dyret = P.new_dsem("dyret")
                BK_Y, BK_S, BK_T = (1, 2), (3, 4), (5, 6)
                tr_i = [0]

                def p3_load(t):
                    t0 = t * T
                    q_ap, q_b, q_ds = qr.next()
                    P.dma("sp", q_ap, qT_d[:, t0:t0 + T].rearrange("(b p) s -> p b s", p=128), q_ds, reads=[B_q[t]], writes=[q_b])
                    kt_ap, kt_b, kt_ds = ktr.next()
                    P.dma("sp", kt_ap, ktm_d[t0:t0 + T, :].rearrange("(c p) f -> p c f", p=128), kt_ds, reads=[B_ktm[t]], writes=[kt_b])
                    v_ap, v_b, v_ds = vr.next()
                    P.dma("sp", v_ap, vtm_d[t0:t0 + T, :].rearrange("(c p) f -> p c f", p=128), v_ds, reads=[B_v[t]], writes=[v_b])
                    yp_ap, yp_b, yp_ds = ypr.next()
                    P.dma("sp", yp_ap, ypart_d[t0:t0 + T, :].rearrange("(c p) f -> p c f", p=128), yp_ds, reads=[B_yp[t]], writes=[yp_b])
                    g_ap, g_b, g_ds = gr.next()
                    P.dma("sp", g_ap, gact_d[t0:t0 + T, :].rearrange("(c p) f -> p c f", p=128), g_ds, reads=[B_g[t]], writes=[g_b])
                    return (q_ap, q_b, kt_ap, kt_b, v_ap, v_b, yp_ap, yp_b, g_ap, g_b)

                def p3_chunk(t, c, bufs):
                    (q_ap, q_b, kt_ap, kt_b, v_ap, v_b, yp_ap, yp_b, g_ap, g_b) = bufs
                    gc = t * NCH + c
                    cs = slice(c * 128, (c + 1) * 128)
                    if gc == NTILE * NCH - 1:
                        state_reset(Sb_, Sbb2[par[0]], B_S, B_Sb2[par[0]], True)
                    elif (gc + 1) * 128 == SEG:
                        par[0] ^= 1
                        state_reset(Sb_, Sbb2[par[0]], B_S, B_Sb2[par[0]], False)
                    Sbb, B_Sb = Sbb2[par[0]], B_Sb2[par[0]]
                    par[0] ^= 1
                    state_update(Sb_, Sbb2[par[0]], B_S, B_Sb2[par[0]], kdecb, 4, kt_ap, kt_b, v_ap, v_b, c, kfr)
                    y_ap, y_b0, _ = yr_.next()
                    y_b = subs(y_b0, H)
                    for h in range(H):
                        bk = BK_Y[h // 2]
                        o = banks[bk][:, (h % 2) * DH:(h % 2 + 1) * DH]
                        for dc in range(2):
                            P.op("pe", lambda e, h=h, dc=dc, o=o: e.matmul(o, lhsT=q_ap[:, 2 * h + dc, cs], rhs=Sbb[:, 2 * h + dc, :], start=(dc == 0), stop=(dc == 1)), reads=[q_b, B_Sb[h]], writes=[bankb[bk]], inc=(dc == 1))
                        P.op("dve", lambda e, h=h, o=o: e.scalar_tensor_tensor(out=y_ap[:, h * DH:(h + 1) * DH], in0=o, scalar=qdecb[:, h:h + 1], in1=yp_ap[:, c, h * DH:(h + 1) * DH], op0=ALU.mult, op1=ALU.add), reads=[bankb[bk], yp_b, Bl], writes=[y_b[h]])
                    return (y_ap, y_b, g_ap, g_b, c, cs)

                def p3_tail(st_):
                    (y_ap, y_b, g_ap, g_b, c, cs) = st_
                    ss_ap, ss_b, _ = ssr.next()
                    for h in range(H):
                        P.op("act", lambda e, h=h: e.activation(out=junk, in_=y_ap[:, h * DH:(h + 1) * DH], func=AF.Square, accum_out=ss_ap[:, h:h + 1]), reads=[y_b[h]], writes=[ss_b])
                    rstd_from_psum(ss_ap, ss_ap, DH, [ss_b], [ss_b])
                    yn_ap, yn_b0, _ = ynr.next()
                    yn_b = subs(yn_b0, H)
                    for h in range(H):
                        P.op("dve", lambda e, h=h: e.scalar_tensor_tensor(out=yn_ap[:, h * DH:(h + 1) * DH], in0=y_ap[:, h * DH:(h + 1) * DH], scalar=ss_ap[:, h:h + 1], in1=g_ap[:, c, h * DH:(h + 1) * DH], op0=ALU.mult, op1=ALU.mult), reads=[y_b[h], ss_b, g_b], writes=[yn_b[h]])
                    bk = BK_T[tr_i[0] % 2]; tr_i[0] += 1
                    pv = banks[bk][:, :].bitcast(BF16)
                    for blk in range(8):
                        P.op("pe", lambda e, blk=blk, pv=pv: e.transpose(out=pv[:, blk * 128:(blk + 1) * 128], in_=yn_ap[:, blk * 128:(blk + 1) * 128], identity=ident), reads=[yn_b[blk // 2], Bc], writes=[bankb[bk]], inc=(blk == 7))
                    P.op("act", lambda e, pv=pv: e.activation(out=yret_sb[:, :, cs], in_=pv.rearrange("p (b i) -> p b i", b=8), func=AF.Copy), reads=[bankb[bk]], writes=[Byret])

                nxt_bufs = p3_load(NTILE - 1)
                pend_tail = []

                def flush_tail(keep):
                    while len(pend_tail) > keep:
                        st_, t_, c_ = pend_tail.pop(0)
                        p3_tail(st_)
                        if c_ == 0:
                            t0 = t_ * T
                            P.dma("pool", ymixT_d[0:RW, t0:t0 + T].rearrange("(b p) s -> p b s", p=128), yret_sb, dyret, reads=[Byret], writes=[B_ymr[t_]])

                for t in range(NTILE - 1, -1, -1):
                    cur = nxt_bufs
                    if l == 0:
                        issue_casts(n=1)
                    for c in range(NCH - 1, -1, -1):
                        st_ = p3_chunk(t, c, cur)
                        flush_tail(0)
                        pend_tail.append((st_, t, c))
                        if c == NCH - 1 and t - 1 >= 0:
                            nxt_bufs = p3_load(t - 1)
                flush_tail(0)
                phase_end()
            if stop_after == (l, 3):
                break

            if True:
                last_layer = (l == DEPTH - 1)
                h_sb = A.alloc([KC, T], F32); Bh = [Buf(f"h{i}") for i in range(KC)]; dhq = [P.new_dsem(f"dh4{q}") for q in range(4)]; dhsq = [P.new_dsem(f"dhs4{q}") for q in range(4)]
                ym_sb = A.alloc([KC, T], BF16); Bym = Buf("ym"); dym = P.new_dsem("dym4")
                hg_sb = A.alloc([KC, T], BF16); Bhg = Buf("hg4")
                aT = A.alloc([FC, T], BF16); BaT = [Buf(f"aT{i}") for i in range(FC)]
                wr = Ring(P, A, 6, [KC * 256], BF16, "w4")
                dp = P.new_dsem("dp4")
                p_b = A.alloc([2, T], BF16); Bpb = Buf("pb")
                rsb_ap = A.alloc([T], F32); rsb_b = Buf("rsb4")
                sqr = Ring(P, A, 2, [T], BF16, "sq4", dsem=False)
                tgr = Ring(P, A, 4, [T], F32, "tf4", dsem=False)
                sfr = tgr
                mr = tgr
                sgr = Ring(P, A, 4, [T], BF16, "tb4", dsem=False)
                tur = sgr
                outr = Ring(P, A, 2, [2, T], F32, "o4") if last_layer else None
                pacc = [1, 2, 3, 4, 5, 6, 7]
                pacc_i = [0]

                def next_acc():
                    b = pacc[pacc_i[0] % len(pacc)]
                    pacc_i[0] += 1
                    return b

                wq = []
                wsched = []
                for t in range(NTILE):
                    wsched += [("w_out", g, None) for g in range(8)]
                    if p4_stage == 1:
                        continue
                    for g in range(22):
                        wsched += [("w_ffn_gate", g, None), ("w_ffn_up", g, None)]
                    for g in range(16):
                        wsched += [("w_ffn_down", g, 0), ("w_ffn_down", g, 1)]
                    if p4_stage == 2:
                        continue
                    for g in range(8):
                        wsched += [("w_ple_gate", g, None), ("w_ple_proj", g, None)]
                wi = [0]

                def wprefetch():
                    if wi[0] < len(wsched):
                        name, g, half = wsched[wi[0]]
                        wi[0] += 1
                        ap, b, ds = wr.next()
                        off, kc_, gw, ng = WOFF[name]
                        src = wview(l, name, g)
                        if half is not None:
                            kc_ = kc_ // 2
                            src = src[:, half * kc_ * gw:(half + 1) * kc_ * gw]
                        P.dma("sp", ap[:, 0:kc_ * gw], src, ds, reads=[B_w[(l, name)]], writes=[b])
                        wq.append((ap[:, 0:kc_ * gw].rearrange("p (k g) -> p k g", k=kc_), b))

                def wnext():
                    return wq.pop(0)

                def norm_block(nb, gidx, pend):
                    sq_ap, sq_b, _ = sqr.next()
                    P.op("act", lambda e, nb=nb, sq_ap=sq_ap: e.activation(out=sq_ap, in_=h_sb[:, nb, :], func=AF.Square), reads=[Bh[nb]], writes=[sq_b])
                    P.op("act", lambda e, nb=nb: e.activation(out=hg_sb[:, nb, :], in_=h_sb[:, nb, :], func=AF.Copy, scale=gain_ap(gidx, nb)), reads=[Bh[nb], Bc], writes=[Bhg])
                    pend.append((nb, sq_ap, sq_b))

                def norm_pe(pend, keep):
                    while len(pend) > keep:
                        nb, sq_ap, sq_b = pend.pop(0)
                        P.op("pe", lambda e, nb=nb, sq_ap=sq_ap: e.matmul(banks[0][:, :], lhsT=ones, rhs=sq_ap, start=(nb == 0), stop=(nb == KC - 1)), reads=[sq_b, Bc], writes=[bankb[0]], inc=True)

                def dbg_store(t0):
                    for q4 in range(4):
                        P.dma("pool", hT[q4 * 512:(q4 + 1) * 512, t0:t0 + T].rearrange("(k p) s -> p k s", p=128), h_sb[:, q4 * 4:(q4 + 1) * 4, :], dhsq[q4],
                              reads=Bh[q4 * 4:(q4 + 1) * 4])
                        if t0 // T + 1 < NTILE:
                            h_load(t0 // T + 1, q4)

                def h_load(t_, q4):
                    P.dma("pool", h_sb[:, q4 * 4:(q4 + 1) * 4, :], src_h[q4 * 512:(q4 + 1) * 512, t_ * T:(t_ + 1) * T].rearrange("(k p) s -> p k s", p=128), dhq[q4],
                          reads=[B_hT[t_]] if l > 0 else [], writes=Bh[q4 * 4:(q4 + 1) * 4])

                def ym_load(t_):
                    for q2 in range(2):
                        P.dma("sp", ym_sb[:, q2 * 8:(q2 + 1) * 8, :], ymixT_d[q2 * 1024:(q2 + 1) * 1024, t_ * T:(t_ + 1) * T].rearrange("(k p) s -> p k s", p=128), dym,
                              reads=[B_ymr[t_], B_yms[t_]], writes=[Bym])

                def p4_tile(t):
                    t0 = t * T
                    if t == 0:
                        ym_load(0)
                        for q4 in range(4):
                            h_load(0, q4)
                    P.dma("pool", p_b, pT[l * PLE:(l + 1) * PLE, t0:t0 + T].rearrange("(k p) s -> p k s", p=128), dp, writes=[Bpb])
                    pend = []
                    for g in range(8):
                        w_ap, w_b = wnext()
                        for blk in range(2):
                            nb = g * 2 + blk
                            bk = next_acc()
                            for kc in range(KC):
                                P.op("pe", lambda e, bk=bk, blk=blk, kc=kc, w_ap=w_ap: e.matmul(banks[bk][:, :], lhsT=w_ap[:, kc, blk * 128:(blk + 1) * 128], rhs=ym_sb[:, kc, :], start=(kc == 0), stop=(kc == KC - 1)),
                                     reads=[w_b, Bym], writes=[bankb[bk]], inc=(kc == KC - 1))
                            P.op("dve", lambda e, bk=bk, nb=nb: e.tensor_add(out=h_sb[:, nb, :], in0=banks[bk][:, :], in1=h_sb[:, nb, :]), reads=[bankb[bk]], writes=[Bh[nb]])
                            norm_pe(pend, 0)
                            norm_block(nb, l * 3 + 1, pend)
                        wprefetch()
                    norm_pe(pend, 0)
                    rstd_from_psum(banks[0][:, :], rsb_ap, D, [bankb[0]], [rsb_b])
                    if t + 1 < NTILE:
                        ym_load(t + 1)
                    issue_casts(n=4)
                    if p4_stage == 1:
                        return dbg_store(t0)
                    for g in range(22):
                        wg_ap, wg_b = wnext()
                        wu_ap, wu_b = wnext()
                        for blk in range(2):
                            fb = g * 2 + blk
                            bg = next_acc()
                            for kc in range(KC):
                                P.op("pe", lambda e, bg=bg, blk=blk, kc=kc, wg_ap=wg_ap: e.matmul(banks[bg][:, :], lhsT=wg_ap[:, kc, blk * 128:(blk + 1) * 128], rhs=hg_sb[:, kc, :], start=(kc == 0), stop=(kc == KC - 1)),
                                     reads=[wg_b, Bhg], writes=[bankb[bg]], inc=(kc == KC - 1))
                            bu = next_acc()
                            for kc in range(KC):
                                P.op("pe", lambda e, bu=bu, blk=blk, kc=kc, wu_ap=wu_ap: e.matmul(banks[bu][:, :], lhsT=wu_ap[:, kc, blk * 128:(blk + 1) * 128], rhs=hg_sb[:, kc, :], start=(kc == 0), stop=(kc == KC - 1)),
                                     reads=[wu_b, Bhg], writes=[bankb[bu]], inc=(kc == KC - 1))
                            tg_ap, tg_b, _ = tgr.next()
                            sg_ap, sg_b, _ = sgr.next()
                            tu_ap, tu_b, _ = tur.next()
                            P.op("dve", lambda e, bg=bg, tg_ap=tg_ap: e.tensor_mul(out=tg_ap, in0=banks[bg][:, :], in1=rsb_ap), reads=[bankb[bg], rsb_b], writes=[tg_b])
                            P.op("act", lambda e, tg_ap=tg_ap, sg_ap=sg_ap: e.activation(out=sg_ap, in_=tg_ap, func=AF.Silu), reads=[tg_b], writes=[sg_b])
                            P.op("dve", lambda e, bu=bu, tu_ap=tu_ap: e.tensor_mul(out=tu_ap, in0=banks[bu][:, :], in1=rsb_ap), reads=[bankb[bu], rsb_b], writes=[tu_b])
                            P.op("pool", lambda e, fb=fb, sg_ap=sg_ap, tu_ap=tu_ap: e.tensor_mul(out=aT[:, fb, :], in0=sg_ap, in1=tu_ap), reads=[sg_b, tu_b], writes=[BaT[fb]])
                        wprefetch(); wprefetch()
                    pend = []
                    for nb in range(KC):
                        bk = next_acc()
                        for half in range(2):
                            w_ap, w_b = wnext()
                            for f2 in range(FC // 2):
                                fb = half * (FC // 2) + f2
                                P.op("pe", lambda e, bk=bk, fb=fb, f2=f2, w_ap=w_ap: e.matmul(banks[bk][:, :], lhsT=w_ap[:, f2, :], rhs=aT[:, fb, :], start=(fb == 0), stop=(fb == FC - 1)),
                                     reads=[w_b, BaT[fb]], writes=[bankb[bk]], inc=(fb == FC - 1 or f2 == FC // 2 - 1))
                            wprefetch()
                        P.op("dve", lambda e, bk=bk, nb=nb: e.tensor_add(out=h_sb[:, nb, :], in0=banks[bk][:, :], in1=h_sb[:, nb, :]), reads=[bankb[bk]], writes=[Bh[nb]])
                        norm_pe(pend, 0)
                        norm_block(nb, l * 3 + 2, pend)
                    norm_pe(pend, 0)
                    rstd_from_psum(banks[0][:, :], rsb_ap, D, [bankb[0]], [rsb_b])
                    if p4_stage == 2:
                        return dbg_store(t0)
                    pend = []
                    for g in range(8):
                        wg_ap, wg_b = wnext()
                        wp_ap, wp_b = wnext()
                        for blk in range(2):
                            nb = g * 2 + blk
                            bg = next_acc()
                            for kc in range(KC):
                                P.op("pe", lambda e, bg=bg, blk=blk, kc=kc, wg_ap=wg_ap: e.matmul(banks[bg][:, :], lhsT=wg_ap[:, kc, blk * 128:(blk + 1) * 128], rhs=hg_sb[:, kc, :], start=(kc == 0), stop=(kc == KC - 1)),
                                     reads=[wg_b, Bhg], writes=[bankb[bg]], inc=(kc == KC - 1))
                            bp = next_acc()
                            for kc in range(2):
                                P.op("pe", lambda e, bp=bp, blk=blk, kc=kc, wp_ap=wp_ap: e.matmul(banks[bp][:, :], lhsT=wp_ap[:, kc, blk * 128:(blk + 1) * 128], rhs=p_b[:, kc, :], start=(kc == 0), stop=(kc == 1)),
                                     reads=[wp_b, Bpb], writes=[bankb[bp]], inc=(kc == 1))
                            tg_ap, tg_b, _ = tgr.next()
                            sf_ap, sf_b, _ = sfr.next()
                            m_ap, m_b, _ = mr.next()
                            P.op("dve", lambda e, bg=bg, tg_ap=tg_ap: e.tensor_mul(out=tg_ap, in0=banks[bg][:, :], in1=rsb_ap), reads=[bankb[bg], rsb_b], writes=[tg_b])
                            P.op("act", lambda e, tg_ap=tg_ap, sf_ap=sf_ap: e.activation(out=sf_ap, in_=tg_ap, func=AF.Sigmoid), reads=[tg_b], writes=[sf_b])
                            P.op("dve", lambda e, bp=bp, m_ap=m_ap, sf_ap=sf_ap: e.tensor_mul(out=m_ap, in0=banks[bp][:, :], in1=sf_ap), reads=[bankb[bp], sf_b], writes=[m_b])
                            P.op("pool", lambda e, nb=nb, m_ap=m_ap: e.tensor_add(out=h_sb[:, nb, :], in0=h_sb[:, nb, :], in1=m_ap), reads=[m_b], writes=[Bh[nb]])
                            if not last_layer and nb % 4 == 3:
                                q4 = nb // 4
                                P.dma("pool", hT[q4 * 512:(q4 + 1) * 512, t0:t0 + T].rearrange("(k p) s -> p k s", p=128), h_sb[:, q4 * 4:(q4 + 1) * 4, :], dhsq[q4],
                                      reads=Bh[q4 * 4:(q4 + 1) * 4], writes=[B_hT[t]])
                                if t + 1 < NTILE:
                                    h_load(t + 1, q4)
                            if last_layer and p4_stage is None:
                                sq_ap, sq_b, _ = sqr.next()
                                P.op("act", lambda e, nb=nb, sq_ap=sq_ap: e.activation(out=sq_ap, in_=h_sb[:, nb, :], func=AF.Square), reads=[Bh[nb]], writes=[sq_b])
                                norm_pe(pend, 0)
                                pend.append((nb, sq_ap, sq_b))
                        wprefetch(); wprefetch()
                    if last_layer and p4_stage is None:
                        norm_pe(pend, 0)
                        rstd_from_psum(banks[0][:, :], rsb_ap, D, [bankb[0]], [rsb_b])
                        for q8 in range(8):
                            o_ap, o_b, o_ds = outr.next()
                            for j in range(2):
                                nb = q8 * 2 + j
                                P.op("dve", lambda e, nb=nb, j=j, o_ap=o_ap: e.scalar_tensor_tensor(out=o_ap[:, j, :], in0=h_sb[:, nb, :], scalar=gain_ap(DEPTH * 3, nb), in1=rsb_ap, op0=ALU.mult, op1=ALU.mult), reads=[Bh[nb], rsb_b, Bc], writes=[o_b])
                            P.dma("pool", yT[q8 * 256:(q8 + 1) * 256, t0:t0 + T].rearrange("(k p) s -> p k s", p=128), o_ap, o_ds, reads=[o_b])
                            if q8 % 2 == 1 and t + 1 < NTILE:
                                h_load(t + 1, q8 // 2)


                issue_casts(layer=l)
                for _ in range(6):
                    wprefetch()
                for t in range(NTILE):
                    p4_tile(t)
                issue_casts(layer=DEPTH - 1)
                phase_end()
            if stop_after == (l, 4):
                break

        P.barrier()
        blk = st.enter_context(nc.Block())
        P.emit(blk)
    return nc


def _rearr(W, gw):
    K, N = W.shape
    return np.ascontiguousarray(W.reshape(K // 128, 128, N // gw, gw).transpose(2, 1, 0, 3)).reshape(-1)


def prep_shared(inp, DEPTH):
    f32 = np.float32
    wcat = np.empty((DEPTH * WTOT,), f32)
    for l in range(DEPTH):
        for n, k_, nn_, g_ in WSPEC:
            off = l * WTOT + WOFF[n][0]
            wcat[off: off + k_ * nn_] = _rearr(np.asarray(inp[n][l], f32), g_)
    gl = []
    for l in range(DEPTH):
        for nm in ("norm_mix_g", "norm_ffn_g", "norm_ple_g"):
            gl.append(np.asarray(inp[nm][l], f32).reshape(KC, 128).T)
    gl.append(np.asarray(inp["norm_final_g"], f32).reshape(KC, 128).T)
    gains = np.ascontiguousarray(np.concatenate(gl, axis=1))
    dec = np.ascontiguousarray(np.broadcast_to(np.asarray(inp["ret_decay"], f32)[:DEPTH].reshape(1, DEPTH * 8), (128, DEPTH * 8)))
    sg = []
    for l in range(DEPTH):
        sg.append(np.asarray(inp["sg_ln_g"][l], f32)); sg.append(np.asarray(inp["sg_ln_b"][l], f32))
    sgln = np.ascontiguousarray(np.broadcast_to(np.concatenate(sg)[None, :], (128, DEPTH * 2 * SGW)))
    wsT = np.ascontiguousarray(np.asarray(inp["sg_w"], f32)[:DEPTH].transpose(3, 0, 1, 2)).reshape(128, DEPTH * G * 128)
    bsr = np.ascontiguousarray(np.asarray(inp["sg_b"], f32)[:DEPTH].reshape(1, DEPTH * SGW))
    half = 128
    invf = (np.float32(10000.0) ** (-np.arange(half, dtype=f32) / np.float32(half))).astype(f32).reshape(128, 1)
    return dict(wcat=wcat, gains=gains, dec=dec, sgln=sgln, wsT=wsT, bsr=bsr, invf=invf)


def prep_core(x_rows, p_rows, pos, carry_flag, DEPTH):
    f32 = np.float32
    NT = x_rows.shape[0]
    xT = np.ascontiguousarray(x_rows.T)
    pT = np.ascontiguousarray(p_rows.transpose(0, 2, 1)).reshape(DEPTH * PLE, NT)
    posb = np.ascontiguousarray(np.broadcast_to(pos.astype(f32)[None, :], (128, NT)))
    carry = np.full((128, 1), carry_flag, f32)
    return dict(xT=xT, pT=pT, posb=posb, carry=carry)


_NC_CACHE = {}


def kernel(x_prompt, x_sample, p_prompt, p_sample, norm_mix_g, w_in, ret_decay, sg_ln_g, sg_ln_b,
           sg_w, sg_b, w_out, norm_ffn_g, w_ffn_gate, w_ffn_up, w_ffn_down, norm_ple_g, w_ple_gate,
           w_ple_proj, norm_final_g):
    DEPTH = 2
    NT = 8192
    SEG = NT // 2
    f32 = np.float32
    x_prompt = np.asarray(x_prompt, f32); x_sample = np.asarray(x_sample, f32)
    p_prompt = np.asarray(p_prompt, f32); p_sample = np.asarray(p_sample, f32)
    inp = dict(norm_mix_g=norm_mix_g, w_in=w_in, ret_decay=ret_decay, sg_ln_g=sg_ln_g, sg_ln_b=sg_ln_b, sg_w=sg_w,
               sg_b=sg_b, w_out=w_out, norm_ffn_g=norm_ffn_g, w_ffn_gate=w_ffn_gate, w_ffn_up=w_ffn_up,
               w_ffn_down=w_ffn_down, norm_ple_g=norm_ple_g, w_ple_gate=w_ple_gate, w_ple_proj=w_ple_proj,
               norm_final_g=norm_final_g)
    shared = prep_shared(inp, DEPTH)
    in_maps = []
    plan = {0: ("p", 0), 1: ("s2", 0, 1), 2: ("s1", 2), 3: ("s1", 3), 4: ("p", 1), 5: ("s2", 4, 5), 6: ("s1", 6), 7: ("s1", 7)}
    pos2 = np.concatenate([np.arange(SEG), np.arange(SEG)])
    for c in range(8):
        pl = plan[c]
        if pl[0] == "p":
            xr = x_prompt[pl[1]]; pr = p_prompt[:, pl[1]]; pos = np.arange(NT); cf = 1.0
        elif pl[0] == "s2":
            xr = np.concatenate([x_sample[pl[1]], x_sample[pl[2]]], axis=0)
            pr = np.concatenate([p_sample[:, pl[1]], p_sample[:, pl[2]]], axis=1); pos = pos2; cf = 0.0
        else:
            xr = np.concatenate([x_sample[pl[1]], np.zeros((SEG, D), f32)], axis=0)
            pr = np.concatenate([p_sample[:, pl[1]], np.zeros((DEPTH, SEG, PLE), f32)], axis=1); pos = pos2; cf = 0.0
        m = dict(shared)
        m.update(prep_core(xr, pr, pos, cf, DEPTH))
        in_maps.append(m)
    if "nc" not in _NC_CACHE:
        _NC_CACHE["nc"] = build(NT, DEPTH=DEPTH)
    res = run_bass_kernel_spmd(_NC_CACHE["nc"], in_maps, core_ids=list(range(8)))
    outs = [np.asarray(r["yT"]) for r in res.results]
    y_prompt = np.empty((2, NT, D), f32)
    y_sample = np.empty((8, SEG, D), f32)
    for c in range(8):
        pl = plan[c]
        o = outs[c].T
        if pl[0] == "p":
            y_prompt[pl[1]] = o
        elif pl[0] == "s2":
            y_sample[pl[1]] = o[:SEG]; y_sample[pl[2]] = o[SEG:]
        else:
            y_sample[pl[1]] = o[:SEG]
    return (y_prompt, y_sample)
```
